# Optimizing a Trainium2 kernel written in Bass

```python
import math
import jax, jax.numpy as jnp
from jax import lax
import numpy as np

D_MODEL = 1024
BATCH = 4
SEQ = 4096
DEPTH = 4

N_MIXERS = 2
N_A_LAYERS = (DEPTH + 1) // 2
N_B_LAYERS = DEPTH // 2
GMLP_WIDTH = D_MODEL
GMLP_CHUNK = 128
GMLP_GROUPS = 8
GMLP_GROUP_DIM = GMLP_WIDTH // GMLP_GROUPS
N_HEADS = 16
HEAD_DIM = D_MODEL // N_HEADS
MOBA_BLOCK = 256
MOBA_TOPK = 3
Q_CHUNK = 128
REL_BUCKETS = 32
REL_MAX_DIST = 128
FFN_DIM = 2816
CONV_WIDTH = 3
LN_EPS = 1e-5
DN_ALPHA = (2 * DEPTH) ** 0.25
DN_BETA = (8 * DEPTH) ** -0.25

kernel_name = "hybrid_gmlp_moba_convffn_deepnorm"


def layer_norm(x, g, b):
    xf = x.astype(jnp.float32)
    mu = jnp.mean(xf, axis=-1, keepdims=True)
    var = jnp.mean(jnp.square(xf - mu), axis=-1, keepdims=True)
    y = (xf - mu) * lax.rsqrt(var + LN_EPS)
    return (y * g.astype(jnp.float32) + b.astype(jnp.float32)).astype(x.dtype)


def rel_bucket(dist):
    n = jnp.maximum(dist, 0)
    max_exact = REL_BUCKETS // 2
    nf = jnp.maximum(n, 1).astype(jnp.float32)
    large = max_exact + (jnp.log(nf / max_exact) / math.log(REL_MAX_DIST / max_exact)
                         * (REL_BUCKETS - max_exact)).astype(jnp.int32)
    large = jnp.minimum(large, REL_BUCKETS - 1)
    return jnp.where(n < max_exact, n, large)


def chunked_gmlp(x, w_in, ln_g, ln_b, w_s, b_s, w_out):
    B, S, _ = x.shape
    z = jax.nn.gelu(x @ w_in)
    u, v = jnp.split(z, 2, axis=-1)
    v = layer_norm(v, ln_g, ln_b)
    nc = S // GMLP_CHUNK
    v = v.reshape(B, nc, GMLP_CHUNK, GMLP_GROUPS, GMLP_GROUP_DIM)
    causal = jnp.tril(jnp.ones((GMLP_CHUNK, GMLP_CHUNK), dtype=bool))
    w = jnp.where(causal[None], w_s, jnp.zeros_like(w_s))
    sv = jnp.einsum('gts,bnsgc->bntgc', w, v) + b_s.T[None, None, :, :, None]
    return (u * sv.reshape(B, S, GMLP_WIDTH)) @ w_out


def moba_attention(x, w_qkv, w_o, rel_bias):
    B, S, _ = x.shape
    H, Dh, L = N_HEADS, HEAD_DIM, MOBA_BLOCK
    nb = -(-S // L)
    sp = nb * L
    nq = S // Q_CHUNK
    topk = min(MOBA_TOPK, nb)
    qkv = (x @ w_qkv).reshape(B, S, 3, H, Dh)
    q = jnp.transpose(qkv[:, :, 0], (0, 2, 1, 3)) * (Dh ** -0.5)
    pad = ((0, 0), (0, 0), (0, sp - S), (0, 0))
    k = jnp.pad(jnp.transpose(qkv[:, :, 1], (0, 2, 1, 3)), pad)
    v = jnp.pad(jnp.transpose(qkv[:, :, 2], (0, 2, 1, 3)), pad)
    kb = k.reshape(B, H, nb, L, Dh)
    vb = v.reshape(B, H, nb, L, Dh)
    kbar = jnp.mean(kb.astype(jnp.float32), axis=3).astype(k.dtype)
    bias_hb = rel_bias.T
    qc = jnp.moveaxis(q.reshape(B, H, nq, Q_CHUNK, Dh), 2, 0)
    bi = jnp.arange(B)[:, None, None, None]
    hi = jnp.arange(H)[None, :, None, None]
    hi5 = hi[..., None]
    blk = jnp.arange(nb)
    off = jnp.arange(L)

    def chunk_fn(args):
        c, q_c = args
        t = c * Q_CHUNK + jnp.arange(Q_CHUNK)
        cur = (c * Q_CHUNK) // L
        gate = jnp.einsum('bhqd,bhnd->bhqn', q_c, kbar).astype(jnp.float32)
        gate = jnp.where(blk < cur, gate, -jnp.inf)
        _, idx = lax.top_k(gate, topk)
        sel_ok = idx < cur
        k_sel = kb[bi, hi, idx]
        v_sel = vb[bi, hi, idx]
        k_own = lax.dynamic_index_in_dim(kb, cur, axis=2, keepdims=False)
        v_own = lax.dynamic_index_in_dim(vb, cur, axis=2, keepdims=False)
        s_sel = jnp.einsum('bhqd,bhqjld->bhqjl', q_c, k_sel).astype(jnp.float32)
        s_own = jnp.einsum('bhqd,bhld->bhql', q_c, k_own).astype(jnp.float32)
        d_sel = t[None, None, :, None, None] - (idx[..., None] * L + off)
        d_own = t[:, None] - (cur * L + off)[None, :]
        s_sel = s_sel + bias_hb[hi5, rel_bucket(d_sel)].astype(jnp.float32)
        s_own = s_own + bias_hb[:, rel_bucket(d_own)][None].astype(jnp.float32)
        s_sel = jnp.where(sel_ok[..., None], s_sel, -jnp.inf)
        s_own = jnp.where(d_own >= 0, s_own, -jnp.inf)
        logits = jnp.concatenate([s_sel.reshape(B, H, Q_CHUNK, topk * L), s_own], axis=-1)
        p = jax.nn.softmax(logits, axis=-1)
        p_sel = p[..., :topk * L].reshape(B, H, Q_CHUNK, topk, L).astype(v.dtype)
        p_own = p[..., topk * L:].astype(v.dtype)
        return (jnp.einsum('bhqjl,bhqjld->bhqd', p_sel, v_sel)
                + jnp.einsum('bhql,bhld->bhqd', p_own, v_own))

    o = lax.map(chunk_fn, (jnp.arange(nq, dtype=jnp.int32), qc))
    o = jnp.transpose(o, (1, 0, 3, 2, 4)).reshape(B, S, H * Dh)
    return o @ w_o


def conv_ffn(x, w_up, conv_w, conv_b, w_down):
    S = x.shape[1]
    h = x @ w_up
    hp = jnp.pad(h, ((0, 0), (CONV_WIDTH - 1, 0), (0, 0)))
    hc = conv_b
    for j in range(CONV_WIDTH):
        hc = hc + conv_w[j] * hp[:, j:j + S]
    g, val = jnp.split(hc, 2, axis=-1)
    return (jax.nn.gelu(g) * val) @ w_down


def setup_inputs(seed: int = 0) -> dict:
    key = jax.random.key(seed)
    ks = jax.random.split(key, 20)
    f32 = jnp.float32
    D = D_MODEL

    def nrm(k, shape, scale):
        return jax.random.normal(k, shape, f32) * scale

    return {
        "x": nrm(ks[0], (BATCH, SEQ, D), 1.0),
        "ln_mix_g": 1.0 + nrm(ks[1], (DEPTH, D), 0.01),
        "ln_mix_b": nrm(ks[2], (DEPTH, D), 0.01),
        "ln_ffn_g": 1.0 + nrm(ks[3], (DEPTH, D), 0.01),
        "ln_ffn_b": nrm(ks[4], (DEPTH, D), 0.01),
        "a_w_in": nrm(ks[5], (N_A_LAYERS, D, 2 * GMLP_WIDTH), D ** -0.5),
        "a_ln_g": 1.0 + nrm(ks[6], (N_A_LAYERS, GMLP_WIDTH), 0.01),
        "a_ln_b": nrm(ks[7], (N_A_LAYERS, GMLP_WIDTH), 0.01),
        "a_w_s": nrm(ks[8], (N_A_LAYERS, GMLP_GROUPS, GMLP_CHUNK, GMLP_CHUNK), GMLP_CHUNK ** -0.5),
        "a_b_s": 1.0 + nrm(ks[9], (N_A_LAYERS, GMLP_GROUPS, GMLP_CHUNK), 0.01),
        "a_w_out": nrm(ks[10], (N_A_LAYERS, GMLP_WIDTH, D), GMLP_WIDTH ** -0.5 * DN_BETA),
        "b_w_qkv": nrm(ks[11], (N_B_LAYERS, D, 3 * D), D ** -0.5),
        "b_w_o": nrm(ks[12], (N_B_LAYERS, D, D), D ** -0.5 * DN_BETA),
        "rel_bias": nrm(ks[13], (REL_BUCKETS, N_HEADS), 0.5),
        "f_w_up": nrm(ks[14], (DEPTH, D, 2 * FFN_DIM), D ** -0.5),
        "f_conv_w": nrm(ks[15], (DEPTH, CONV_WIDTH, 2 * FFN_DIM), CONV_WIDTH ** -0.5),
        "f_conv_b": nrm(ks[16], (DEPTH, 2 * FFN_DIM), 0.01),
        "f_w_down": nrm(ks[17], (DEPTH, FFN_DIM, D), FFN_DIM ** -0.5 * DN_BETA),
    }


def reference(x, ln_mix_g, ln_mix_b, ln_ffn_g, ln_ffn_b, a_w_in, a_ln_g, a_ln_b, a_w_s,
              a_b_s, a_w_out, b_w_qkv, b_w_o, rel_bias, f_w_up, f_conv_w, f_conv_b,
              f_w_down):
    for i in range(DEPTH):
        j = i // N_MIXERS
        if i % N_MIXERS == 0:
            y = chunked_gmlp(x, a_w_in[j], a_ln_g[j], a_ln_b[j], a_w_s[j], a_b_s[j], a_w_out[j])
        else:
            y = moba_attention(x, b_w_qkv[j], b_w_o[j], rel_bias)
        x = layer_norm(DN_ALPHA * x + y, ln_mix_g[i], ln_mix_b[i])
        y = conv_ffn(x, f_w_up[i], f_conv_w[i], f_conv_b[i], f_w_down[i])
        x = layer_norm(DN_ALPHA * x + y, ln_ffn_g[i], ln_ffn_b[i])
    return x
```

```python
import contextlib
import math
import numpy as np
import ml_dtypes
import concourse.bass as bass
import concourse.mybir as mybir
from concourse.bass_utils import run_bass_kernel_spmd

F32 = mybir.dt.float32
BF16 = mybir.dt.bfloat16
AF = mybir.ActivationFunctionType
ALU = mybir.AluOpType
AX = mybir.AxisListType
NPBF = ml_dtypes.bfloat16

D = 1024
DEPTH = 4
NT = 2048
NBLK = 8
H = 16
DH = 64
FF = 2816
ALPHA = (2 * DEPTH) ** 0.25
EPS = 1e-5
NEG = -30000.0

ENGS = ("pe", "act", "dve", "pool", "sp")
SAME_ENGINE_SYNC = True
N_DMA_SLOTS = 20
STORE_Q = "pool"


class Buf:
    __slots__ = ("name", "w", "r")

    def __init__(self, name=""):
        self.name = name
        self.w = None
        self.r = []


class Op:
    __slots__ = ("eng", "emit", "deps", "dma", "slot", "slot_val", "prev_val",
                 "signal", "count", "epoch")

    def __init__(self, eng, emit, dma):
        self.eng = eng
        self.emit = emit
        self.deps = set()
        self.dma = dma
        self.slot = None
        self.slot_val = 0
        self.prev_val = 0
        self.signal = False
        self.count = 0
        self.epoch = 0


class Prog:
    def __init__(self, nc):
        self.nc = nc
        self.ops = {e: [] for e in ENGS}
        self.slot_rr = {e: 0 for e in ENGS}
        self.slot_cum = {}
        self.pending_dma = []
        self.epoch = 0

    def op(self, eng, emit, reads=(), writes=(), dma=False):
        o = Op(eng, emit, dma)
        o.epoch = self.epoch
        for b in reads:
            if b.w is not None and b.w is not o:
                o.deps.add(b.w)
            b.r.append(o)
        for b in writes:
            if b.w is not None and b.w is not o:
                o.deps.add(b.w)
            for r in b.r:
                if r is not o:
                    o.deps.add(r)
            b.w = o
            b.r = []
        if dma:
            k = self.slot_rr[eng]
            self.slot_rr[eng] = (k + 1) % N_DMA_SLOTS
            key = (eng, k)
            prev = self.slot_cum.get(key, 0)
            o.slot = key
            o.prev_val = prev
            o.slot_val = prev + 16
            self.slot_cum[key] = o.slot_val
            self.pending_dma.append(o)
        self.ops[eng].append(o)
        return o

    def barrier(self):
        lasts = []
        for e in ENGS:
            for o in reversed(self.ops[e]):
                if not o.dma and o.emit is not None:
                    lasts.append(o)
                    break
        pend = list(self.pending_dma)
        self.pending_dma = []
        for e in ENGS:
            o = Op(e, None, False)
            o.epoch = self.epoch
            o.deps.update(lasts)
            o.deps.update(pend)
            self.ops[e].append(o)
        self.nbar = getattr(self, "nbar", 0) + 1
        self.epoch = self.nbar // 3

    def finish(self, out_ops):
        o = Op("sp", None, False)
        o.epoch = self.epoch
        o.deps.update(out_ops)
        self.ops["sp"].append(o)

    def emit_all(self):
        nc = self.nc
        for e in ENGS:
            for o in self.ops[e]:
                for d in o.deps:
                    if d.dma:
                        continue
                    if d.eng == o.eng and (d.eng == "pe" or not SAME_ENGINE_SYNC):
                        continue
                    d.signal = True
        for e in ENGS:
            c = {}
            for o in self.ops[e]:
                if o.signal:
                    c[o.epoch] = c.get(o.epoch, 0) + 1
                o.count = c.get(o.epoch, 0)
        with contextlib.ExitStack() as es:
            esem = {}
            for e in ENGS:
                for ep in range(self.epoch + 1):
                    if any(o.signal and o.epoch == ep for o in self.ops[e]):
                        esem[(e, ep)] = es.enter_context(nc.semaphore("s_%s_%d" % (e, ep)))
            dsem = {}
            for key in self.slot_cum:
                dsem[key] = es.enter_context(nc.semaphore("d_%s_%d" % key))
            block = es.enter_context(nc.Block())

            def run(e, eng):
                waited = {}
                for o in self.ops[e]:
                    waits = {}
                    for d in o.deps:
                        if d.dma:
                            s, v = dsem[d.slot], d.slot_val
                            k = ("d",) + d.slot
                        else:
                            if d.eng == e and (e == "pe" or not SAME_ENGINE_SYNC):
                                continue
                            s, v = esem[(d.eng, d.epoch)], d.count
                            k = ("e", d.eng, d.epoch)
                        if waits.get(k, (None, 0))[1] < v:
                            waits[k] = (s, v)
                    if o.dma and o.prev_val > 0:
                        k = ("d",) + o.slot
                        if waits.get(k, (None, 0))[1] < o.prev_val:
                            waits[k] = (dsem[o.slot], o.prev_val)
                    for k, (s, v) in waits.items():
                        if waited.get(k, 0) >= v:
                            continue
                        waited[k] = v
                        eng.wait_ge(s, v)
                    if o.emit is None:
                        continue
                    ins = o.emit(eng)
                    if o.dma:
                        ins.then_inc(dsem[o.slot], 16)
                    elif o.signal:
                        ins.then_inc(esem[(e, o.epoch)], 1)

            @block.tensor
            def _(eng):
                run("pe", eng)

            @block.scalar
            def _(eng):
                run("act", eng)

            @block.vector
            def _(eng):
                run("dve", eng)

            @block.gpsimd
            def _(eng):
                run("pool", eng)

            @block.sync
            def _(eng):
                run("sp", eng)


class Cx:
    def __init__(self, nc):
        self.nc = nc
        self.P = Prog(nc)
        self.es = None
        self.uid = 0
        self.dq = 0
        self.outs = []

    def sb(self, shape, dt, name=None):
        self.uid += 1
        t = self.es.enter_context(self.nc.sbuf_tensor("%s_%d" % (name or "t", self.uid), list(shape), dt))
        return t, Buf(name or "t")

    def ps(self, shape, dt, name=None):
        self.uid += 1
        t = self.es.enter_context(self.nc.psum_tensor("%s_%d" % (name or "p", self.uid), list(shape), dt))
        return t, Buf(name or "p")

    def dram(self, name, shape, dt, kind):
        return self.nc.dram_tensor(name, list(shape), dt, kind=kind)

    def mm(self, out, lhsT, rhs, start, stop, reads, writes):
        self.P.op("pe", lambda e: e.matmul(out, lhsT=lhsT, rhs=rhs, start=start, stop=stop),
                  reads=reads, writes=writes)

    def tr(self, out, in_, ident, reads, writes):
        self.P.op("pe", lambda e: e.transpose(out=out, in_=in_, identity=ident), reads=reads, writes=writes)

    def act(self, out, in_, func, reads, writes, bias=None, scale=None):
        kw = {}
        if bias is not None:
            kw["bias"] = bias
        if scale is not None:
            kw["scale"] = scale
        self.P.op("act", lambda e: e.activation(out=out, in_=in_, func=func, **kw), reads=reads, writes=writes)

    def ts(self, eng, out, in0, s1, s2, op0, op1, reads, writes):
        if op1 is None:
            self.P.op(eng, lambda e: e.tensor_scalar(out=out, in0=in0, scalar1=s1, scalar2=None, op0=op0),
                      reads=reads, writes=writes)
        else:
            self.P.op(eng, lambda e: e.tensor_scalar(out=out, in0=in0, scalar1=s1, scalar2=s2, op0=op0, op1=op1),
                      reads=reads, writes=writes)

    def stt(self, out, in0, scalar, in1, op0, op1, reads, writes):
        self.P.op("dve", lambda e: e.scalar_tensor_tensor(out=out, in0=in0, scalar=scalar, in1=in1,
                                                          op0=op0, op1=op1), reads=reads, writes=writes)

    def tt(self, eng, out, in0, in1, op, reads, writes):
        self.P.op(eng, lambda e: e.tensor_tensor(out=out, in0=in0, in1=in1, op=op), reads=reads, writes=writes)

    def copy(self, eng, out, in_, reads, writes):
        if eng == "act":
            self.P.op("act", lambda e: e.copy(out=out, in_=in_), reads=reads, writes=writes)
        else:
            self.P.op(eng, lambda e: e.tensor_copy(out=out, in_=in_), reads=reads, writes=writes)

    def memset(self, eng, ap, val, writes):
        self.P.op(eng, lambda e: e.memset(ap, val), writes=writes)

    def dma(self, q, out, in_, reads=(), writes=(), nonc=False):
        if nonc:
            def em(e):
                with self.nc.allow_non_contiguous_dma(reason="small strided param load"):
                    return e.dma_start(out=out, in_=in_)
        else:
            def em(e):
                return e.dma_start(out=out, in_=in_)
        return self.P.op(q, em, reads=reads, writes=writes, dma=True)


def bcast_rows(ap2d_row, n=128):
    a = ap2d_row.partition_broadcast(n)
    if len(a.shape) == 3:
        a = a[:, 0, :]
    return a


class Consts:
    pass


def load_consts(cx, din):
    c = Consts()
    c.ident, c.b_ident = cx.sb([128, 128], BF16, "ident")
    cx.dma("sp", c.ident[:], din["ident"], writes=[c.b_ident])
    c.eps, c.b_eps = cx.sb([128, 1], F32, "eps")
    cx.memset("dve", c.eps[:], EPS, [c.b_eps])
    return c


def rstd_op(cx, consts, rstd, b_rstd, var_ap, b_var):
    cx.act(rstd, var_ap, AF.Sqrt, [b_var, consts.b_eps], [b_rstd], bias=consts.eps[:, 0:1])
    cx.P.op("dve", lambda e: e.reciprocal(out=rstd, in_=rstd), reads=[b_rstd], writes=[b_rstd])


class Banks:
    def __init__(self, cx):
        self.pb = []
        self.t = []
        self.b = []
        for i in range(4):
            t, _ = cx.ps([128, 1024], F32, "pb%d" % i)
            self.pb.append(t)
            for hf in range(2):
                self.t.append(t[:, hf * 512:(hf + 1) * 512])
                self.b.append(Buf("bank%d" % (2 * i + hf)))
        self.tT = self.pb[3].bitcast(BF16)[:, 1024:2048]
        self.bT = self.b[7]

    def pair(self, i):
        return self.pb[i][:, :].rearrange("p (b c) -> p b c", b=2), [self.b[2 * i], self.b[2 * i + 1]]


class Epi:
    def __init__(self, cx, consts, banks, lng_row, lnb_row):
        self.cx = cx
        self.consts = consts
        self.banks = banks
        self.lng, self.b_lng = cx.sb([128, D], F32, "lng")
        self.lnb, self.b_lnb = cx.sb([128, D], F32, "lnb")
        cx.dma("sp", self.lng[:], bcast_rows(lng_row), writes=[self.b_lng])
        cx.dma("sp", self.lnb[:], bcast_rows(lnb_row), writes=[self.b_lnb])
        self.xr = [cx.sb([128, D], F32, "xr") for _ in range(2)]
        self.s = [cx.sb([128, D], F32, "s")] * 2
        self.xn = [cx.sb([128, D], F32, "xn")] * 2
        self.xnb = [cx.sb([128, D], BF16, "xnb") for _ in range(2)]
        self.xts = [cx.sb([128, 8, 128], BF16, "xts") for _ in range(2)]
        self.st = [cx.sb([128, 2, 6], F32, "st") for _ in range(2)]
        self.mv = [cx.sb([128, 2], F32, "mv") for _ in range(2)]
        self.rstd = [cx.sb([128, 1], F32, "rstd") for _ in range(2)]
        self.tl, self.b_tl = cx.sb([128, 8, 16], BF16, "tl")
        self.k = 0
        self.pending = None

    def prefetch(self, tile, xres_in):
        xr, b_xr = self.xr[self.k]
        self.cx.dma("sp", xr[:], xres_in[tile * 128:(tile + 1) * 128, :], writes=[b_xr])
        self.pre = tile

    def run(self, tile, y0, y1, by, xres_in, xres_out, xT_out, tail_out, out_bufs=None, next_tile=None):
        cx = self.cx
        k = self.k
        self.k ^= 1
        self.flush()
        xr, b_xr = self.xr[k]
        s, b_s = self.s[k]
        xn, b_xn = self.xn[k]
        xnb, b_xnb = self.xnb[k]
        xts, b_xts = self.xts[k]
        st, b_st = self.st[k]
        mv, b_mv = self.mv[k]
        rstd, b_rstd = self.rstd[k]
        rows = slice(tile * 128, (tile + 1) * 128)
        if getattr(self, "pre", None) != tile:
            cx.dma("sp", xr[:], xres_in[rows, :], writes=[b_xr])
        self.pre = None
        if next_tile is not None:
            self.prefetch(next_tile, xres_in)
        cx.stt(s[:, 0:512], xr[:, 0:512], ALPHA, y0, ALU.mult, ALU.add, [b_xr, by[0]], [b_s])
        cx.stt(s[:, 512:1024], xr[:, 512:1024], ALPHA, y1, ALU.mult, ALU.add, [b_xr, by[1]], [b_s])
        cx.P.op("dve", lambda e: e.bn_stats(out=st[:, 0, :], in_=s[:, 0:512]), reads=[b_s], writes=[b_st])
        cx.P.op("dve", lambda e: e.bn_stats(out=st[:, 1, :], in_=s[:, 512:1024]), reads=[b_s], writes=[b_st])
        cx.P.op("dve", lambda e: e.bn_aggr(out=mv[:], in_=st[:].rearrange("p a b -> p (a b)")),
                reads=[b_st], writes=[b_mv])
        rstd_op(cx, self.consts, rstd[:], b_rstd, mv[:, 1:2], b_mv)
        cx.stt(s[:], s[:], mv[:, 0:1], self.lng[:], ALU.subtract, ALU.mult, [b_s, b_mv, self.b_lng], [b_s])
        cx.stt(xn[:], s[:], rstd[:, 0:1], self.lnb[:], ALU.mult, ALU.add, [b_s, b_rstd, self.b_lnb], [b_xn])
        o = cx.dma(STORE_Q, xres_out[rows, :], xn[:], reads=[b_xn], writes=out_bufs or ())
        cx.outs.append(o)
        if xT_out is None:
            return
        cx.copy("act", xnb[:], xn[:], [b_xn], [b_xnb])
        self.pending = (tile, xnb, b_xnb, xts, b_xts, xT_out, tail_out, out_bufs)

    def flush(self):
        if self.pending is None:
            return
        cx = self.cx
        tile, xnb, b_xnb, xts, b_xts, xT_out, tail_out, out_bufs = self.pending
        self.pending = None
        bk = self.banks
        for c in range(8):
            cx.tr(bk.tT[:, c * 128:(c + 1) * 128], xnb[:, c * 128:(c + 1) * 128], self.consts.ident[:],
                  [b_xnb, self.consts.b_ident], [bk.bT])
        cx.copy("act", xts[:].rearrange("p c t -> p (c t)"), bk.tT[:, :], [bk.bT], [b_xts])
        o = cx.dma(STORE_Q, xT_out.rearrange("(c p) t -> p c t", p=128)[:, :, tile * 128:(tile + 1) * 128], xts[:],
                   reads=[b_xts], writes=out_bufs or ())
        cx.outs.append(o)
        if tile % 2 == 1:
            blk = tile // 2
            cx.copy("pool", self.tl[:, :, 2 * blk:2 * blk + 2], xts[:, :, 126:128], [b_xts], [self.b_tl])
        if tile == 15 and tail_out is not None:
            o = cx.dma("sp", tail_out.rearrange("(c p) t -> p c t", p=128), self.tl[:], reads=[self.b_tl],
                       writes=out_bufs or (), nonc=True)
            cx.outs.append(o)


class Stager:
    def __init__(self, cx, n=4, size=1024):
        self.bufs = [cx.sb([128, size], F32, "stg32") for _ in range(n)]
        self.k = 0
        self.size = size

    def load(self, cx, dst, src, b_dst, shape2=None, eng="pool"):
        t, b = self.bufs[self.k % len(self.bufs)]
        self.k += 1
        if shape2 is None:
            n = dst.shape[1]
            view = t[:, 0:n]
        else:
            a, bb = shape2
            view = t[:, 0:a * bb].rearrange("p (a b) -> p a b", a=a)
        cx.dma("sp", view, src, writes=[b])
        cx.copy(eng, dst, view, [b], [b_dst])


class WBufs:
    def __init__(self, ncols, piece):
        self.piece = piece
        self.bufs = [Buf("w") for _ in range((ncols + piece - 1) // piece)]

    def get(self, c0, c1):
        return self.bufs[c0 // self.piece:(c1 - 1) // self.piece + 1]


CAST_ROT = ("act", "dve", "act", "dve", "pool")


def load_w_bf16(cx, stager, dst, src, k_chunks, col0, ncols, maxc=1024, order=None):
    wb = WBufs(ncols, maxc)
    pieces = list(range(0, ncols, maxc))
    if order is not None:
        pieces = [pieces[i] for i in order]
    n = 0
    for c0 in pieces:
        c1 = min(ncols, c0 + maxc)
        for k in range(k_chunks):
            stager.load(cx, dst[:, k, c0:c1], src[k * 128:(k + 1) * 128, col0 + c0:col0 + c1], wb.get(c0, c1)[0],
                        eng=CAST_ROT[n % len(CAST_ROT)])
            n += 1
    return wb


def stage_prologue(cx, consts, banks, x_in, xT_out, tail_out, out_bufs=None):
    with contextlib.ExitStack() as es:
        cx.es = es
        xr = [cx.sb([128, D], F32, "pxr") for _ in range(2)]
        xb = [cx.sb([128, D], BF16, "pxb") for _ in range(2)]
        xts = [cx.sb([128, 8, 128], BF16, "pxts") for _ in range(2)]
        tl, b_tl = cx.sb([128, 8, 16], BF16, "ptl")
        for tile in range(16):
            k = tile % 2
            cx.dma("sp", xr[k][0][:], x_in[tile * 128:(tile + 1) * 128, :], writes=[xr[k][1]])
            cx.copy("dve", xb[k][0][:], xr[k][0][:], [xr[k][1]], [xb[k][1]])
            for c in range(8):
                cx.tr(banks.tT[:, c * 128:(c + 1) * 128], xb[k][0][:, c * 128:(c + 1) * 128], consts.ident[:],
                      [xb[k][1], consts.b_ident], [banks.bT])
            cx.copy("act", xts[k][0][:].rearrange("p c t -> p (c t)"), banks.tT[:, :], [banks.bT], [xts[k][1]])
            o = cx.dma("sp", xT_out.rearrange("(c p) t -> p c t", p=128)[:, :, tile * 128:(tile + 1) * 128],
                       xts[k][0][:], reads=[xts[k][1]], writes=out_bufs or ())
            cx.outs.append(o)
            if tile % 2 == 1:
                blk = tile // 2
                cx.copy("pool", tl[:, :, 2 * blk:2 * blk + 2], xts[k][0][:, :, 126:128], [xts[k][1]], [b_tl])
        o = cx.dma("sp", tail_out.rearrange("(c p) t -> p c t", p=128), tl[:], reads=[b_tl],
                   writes=out_bufs or (), nonc=True)
        cx.outs.append(o)
        cx.P.barrier()
    cx.es = None


def stage_gmlp(cx, consts, banks, din, j, li, xres_in, xT_in, xres_out, xT_out, tail_out,
               in_bufs=(), out_bufs=None):
    w_in = din["a_w_in"][j]
    w_out = din["a_w_out"][j]
    with contextlib.ExitStack() as es:
        cx.es = es
        xT, b_xT = cx.sb([128, 8, NT], BF16, "xT")
        for c in range(8):
            cx.dma("sp", xT[:, c, :], xT_in[c * 128:(c + 1) * 128, :], reads=in_bufs, writes=[b_xT])
        stager = Stager(cx)
        win, _ = cx.sb([128, 8, 2048], BF16, "win")
        wb_win = load_w_bf16(cx, stager, win, w_in, 8, 0, 2048)
        wout, _ = cx.sb([128, 8, 1024], BF16, "wout")
        wb_wout = load_w_bf16(cx, stager, wout, w_out, 8, 0, 1024)
        wsn, b_wsn = cx.sb([128, 8, 128], BF16, "wsn")
        cx.dma("pool", wsn[:], din["a_w_s"][j].rearrange("g t s -> t g s"), writes=[b_wsn])
        tril, b_tril = cx.sb([128, 8, 128], BF16, "tril")
        cx.dma("sp", tril[:], din["tril"], writes=[b_tril])
        cx.tt("pool", wsn[:], wsn[:], tril[:], ALU.mult, [b_wsn, b_tril], [b_wsn])
        wmT, b_wmT = cx.sb([128, 8, 128], BF16, "wmT")
        for g in range(8):
            cx.tr(banks.tT[:, g * 128:(g + 1) * 128], wsn[:, g, :], consts.ident[:], [b_wsn, consts.b_ident],
                  [banks.bT])
        cx.copy("act", wmT[:].rearrange("p g t -> p (g t)"), banks.tT[:, :], [banks.bT], [b_wmT])
        lbb, b_lbb = cx.sb([128, 1024], BF16, "lbb")
        cx.dma("pool", lbb[:], bcast_rows(din["a_ln_b"][j:j + 1, :]), writes=[b_lbb])
        bsb, b_bsb = cx.sb([128, 8, 128], F32, "bsb")
        cx.dma("sp", bsb[:].rearrange("p g t -> p (g t)"),
               bcast_rows(din["a_b_s"][j:j + 1].rearrange("o g t -> o (g t)")), writes=[b_bsb])
        Bt, b_Bt = cx.sb([128, 8, 128], F32, "Bt")
        for c in range(8):
            bank = banks.t[c // 4]
            cx.mm(bank[:, (c % 4) * 128:(c % 4 + 1) * 128], lbb[:, c * 128:(c + 1) * 128], wmT[:, c, :],
                  True, True, [b_lbb, b_wmT], [banks.b[c // 4]])
        for hf in range(2):
            cx.tt("dve", Bt[:, hf * 4:(hf + 1) * 4, :].rearrange("p g t -> p (g t)"), banks.t[hf][:, :],
                  bsb[:, hf * 4:(hf + 1) * 4, :].rearrange("p g t -> p (g t)"), ALU.add,
                  [banks.b[hf], b_bsb], [b_Bt])
        gcol, b_gcol = cx.sb([128, 8], F32, "gcol")
        cx.dma("sp", gcol[:], din["a_ln_g"][j].rearrange("(c p) -> p c", p=128), writes=[b_gcol], nonc=True)
        epi = Epi(cx, consts, banks, din["ln_mix_g"][li:li + 1, :], din["ln_mix_b"][li:li + 1, :])
        uT, b_uT = cx.sb([128, 8, 512], F32, "uT")
        vg = [cx.sb([128, D], F32, "vg") for _ in range(2)]
        vn = [cx.sb([128, D], BF16, "vn") for _ in range(2)]
        t1 = [cx.sb([128, 8, 128], F32, "t1") for _ in range(2)]
        zT = [cx.sb([128, 8, 128], BF16, "zT") for _ in range(2)]
        st = [cx.sb([128, 2, 6], F32, "gst") for _ in range(2)]
        mv = [cx.sb([128, 2], F32, "gmv") for _ in range(2)]
        rstd = [cx.sb([128, 1], F32, "grstd") for _ in range(2)]
        nmr = [cx.sb([128, 1], F32, "gnmr") for _ in range(2)]
        def u_phase(tg):
            for c in range(8):
                bank, bb = banks.t[6], banks.b[6]
                for k in range(8):
                    cx.mm(bank[:, :], win[:, k, c * 128:(c + 1) * 128], xT[:, k, tg * 512:(tg + 1) * 512],
                          k == 0, k == 7, wb_win.get(c * 128, (c + 1) * 128) + [b_xT], [bb])
                cx.act(uT[:, c, :], bank[:, :], AF.Gelu_apprx_tanh, [bb], [b_uT])

        def part_a(tile):
            k2 = tile % 2
            tcols = slice(tile * 128, (tile + 1) * 128)
            for hf in range(2):
                for k in range(8):
                    cx.mm(banks.t[2 + hf][:, :], xT[:, k, tcols], win[:, k, 1024 + hf * 512:1024 + (hf + 1) * 512],
                          k == 0, k == 7, [b_xT] + wb_win.get(1024 + hf * 512, 1024 + (hf + 1) * 512), [banks.b[2 + hf]])
            vgt, b_vg = vg[k2]
            vnt, b_vn = vn[k2]
            for hf in range(2):
                cx.act(vgt[:, hf * 512:(hf + 1) * 512], banks.t[2 + hf][:, :], AF.Gelu_apprx_tanh,
                       [banks.b[2 + hf]], [b_vg])
            stt_, b_st = st[k2]
            mvt, b_mv = mv[k2]
            rs, b_rs = rstd[k2]
            cx.P.op("dve", lambda e, a=stt_, b=vgt: e.bn_stats(out=a[:, 0, :], in_=b[:, 0:512]),
                    reads=[b_vg], writes=[b_st])
            cx.P.op("dve", lambda e, a=stt_, b=vgt: e.bn_stats(out=a[:, 1, :], in_=b[:, 512:1024]),
                    reads=[b_vg], writes=[b_st])
            cx.P.op("dve", lambda e, a=mvt, b=stt_: e.bn_aggr(out=a[:], in_=b[:].rearrange("p a b -> p (a b)")),
                    reads=[b_st], writes=[b_mv])
            rstd_op(cx, consts, rs[:], b_rs, mvt[:, 1:2], b_mv)
            nm, b_nm = nmr[k2]
            cx.stt(nm[:], mvt[:, 0:1], -1.0, rs[:, 0:1], ALU.mult, ALU.mult, [b_mv, b_rs], [b_nm])
            cx.act(vnt[:], vgt[:], AF.Identity, [b_vg, b_rs, b_nm], [b_vn], bias=nm[:, 0:1], scale=rs[:, 0:1])

        def part_b(tile):
            k2 = tile % 2
            tt_ = tile % 4
            vnt, b_vn = vn[k2]
            for c in range(8):
                cx.mm(banks.t[4 + c // 4][:, (c % 4) * 128:(c % 4 + 1) * 128], vnt[:, c * 128:(c + 1) * 128],
                      wmT[:, c, :], True, True, [b_vn, b_wmT], [banks.b[4 + c // 4]])
            t1t, b_t1 = t1[k2]
            zt, b_z = zT[k2]
            for c in range(8):
                cx.stt(t1t[:, c, :], banks.t[4 + c // 4][:, (c % 4) * 128:(c % 4 + 1) * 128], gcol[:, c:c + 1],
                       Bt[:, c, :], ALU.mult, ALU.add, [banks.b[4 + c // 4], b_gcol, b_Bt], [b_t1])
            cx.tt("dve", zt[:], t1t[:], uT[:, :, tt_ * 128:(tt_ + 1) * 128], ALU.mult, [b_t1, b_uT], [b_z])

        def part_c(tile):
            k2 = tile % 2
            zt, b_z = zT[k2]
            yb = 0
            for hf in range(2):
                for c in range(8):
                    cx.mm(banks.t[yb + hf][:, :], zt[:, c, :], wout[:, c, hf * 512:(hf + 1) * 512],
                          c == 0, c == 7, [b_z] + wb_wout.get(hf * 512, (hf + 1) * 512), [banks.b[yb + hf]])
            epi.run(tile, banks.t[yb][:, :], banks.t[yb + 1][:, :], [banks.b[yb], banks.b[yb + 1]],
                    xres_in, xres_out, xT_out, tail_out, out_bufs, next_tile=(tile + 1 if tile + 1 < 16 else None))

        u_phase(0)
        part_a(0)
        for tile in range(16):
            part_b(tile)
            if tile + 1 < 16:
                if (tile + 1) % 4 == 0:
                    u_phase((tile + 1) // 4)
                part_a(tile + 1)
            part_c(tile)
        epi.flush()
        cx.P.barrier()
    cx.es = None


def stage_ffn(cx, consts, banks, din, li, xres_in, xT_in, halo_fn, xres_out, xT_out, tail_out,
              in_bufs=(), out_bufs=None):
    w_up = din["f_w_up"][li]
    w_down = din["f_w_down"][li]
    last = xT_out is None
    with contextlib.ExitStack() as es:
        cx.es = es
        big, b_wd = cx.sb([128, 22 * 1024], BF16, "wd")
        wd = big[:, :].rearrange("p (f n) -> p f n", f=22)
        xte, b_xte = cx.sb([128, 8, 4, 258], BF16, "xte")
        hbuf, b_h = cx.sb([128, 22, 1024], BF16, "hbuf")
        halo, b_halo = cx.sb([128, 8, 16], BF16, "halo")
        halo_fn(cx, halo, b_halo)
        cpar, b_cpar = cx.sb([44, 4, 128], F32, "cpar")
        for jt in range(3):
            cx.dma("sp", cpar[:, jt, :], din["f_conv_w"][li, jt].rearrange("(c p) -> c p", p=128), writes=[b_cpar])
        cx.dma("sp", cpar[:, 3, :], din["f_conv_b"][li].rearrange("(c p) -> c p", p=128), writes=[b_cpar])
        id32, b_id32 = cx.sb([44, 44], F32, "id32")
        cx.dma("sp", id32[:], din["ident32"][0:44, 0:44], writes=[b_id32])
        cwT, b_cw = cx.sb([128, 4, 44], F32, "cwT")
        b_cb = b_cw
        for jt in range(4):
            cx.tr(banks.t[0][:, jt * 44:(jt + 1) * 44], cpar[:, jt, :], id32[:], [b_cpar, b_id32], [banks.b[0]])
        cx.copy("dve", cwT[:].rearrange("p j c -> p (j c)"), banks.t[0][:, 0:176], [banks.b[0]], [b_cw])
        epi = Epi(cx, consts, banks, din["ln_ffn_g"][li:li + 1, :], din["ln_ffn_b"][li:li + 1, :])
        wup = [cx.sb([128, 8, 2, 256], BF16, "wup") for _ in range(2)]
        stager = Stager(cx)
        tmp = [[cx.sb([128, 2, 256], F32, "ct") for _ in range(2)] for _ in range(3)]
        wd_loaded = False
        nk = 0
        tail_fn = None
        for hf in range(2):
            for k in range(8):
                cx.dma("sp", xte[:, k, :, 2:258],
                       xT_in[k * 128:(k + 1) * 128, hf * 1024:(hf + 1) * 1024].rearrange("p (b t) -> p b t", b=4),
                       reads=in_bufs, writes=[b_xte])
            cx.copy("pool", xte[:, :, :, 0:2],
                    halo[:, :, hf * 8:(hf + 1) * 8].rearrange("p k (b t) -> p k b t", b=4), [b_halo], [b_xte])
            def load_wup(fg_, hf_):
                wt_, b_w_ = wup[(hf_ * 11 + fg_) % 2]
                for k in range(8):
                    stager.load(cx, wt_[:, k, :, :],
                                w_up[k * 128:(k + 1) * 128, :].rearrange("p (g f) -> p g f", g=2)[:, :, fg_ * 256:(fg_ + 1) * 256],
                                b_w_, shape2=(2, 256), eng=("act" if k % 2 == 0 else "pool"))
                if hf_ == 0:
                    for f in (2 * fg_, 2 * fg_ + 1):
                        stager.load(cx, wd[:, f, :], w_down[f * 128:(f + 1) * 128, :], b_wd,
                                    eng=("act" if f % 2 == 0 else "pool"))

            if hf == 0:
                load_wup(0, 0)
            for fg in range(11):
                wt, b_w = wup[(hf * 11 + fg) % 2]
                if fg + 1 < 11:
                    load_wup(fg + 1, hf)
                elif hf == 0:
                    load_wup(0, 1)
                for f2 in range(2):
                    fc = fg * 2 + f2
                    for bp in range(2):
                        kk = nk % 2
                        nk += 1
                        pg, bpg = banks.pair(2 * kk)
                        pv, bpv = banks.pair(2 * kk + 1)
                        for bl in range(2):
                            for k in range(8):
                                cx.mm(pg[:, bl, 0:258], wt[:, k, 0, f2 * 128:(f2 + 1) * 128], xte[:, k, 2 * bp + bl, :],
                                      k == 0, k == 7, [b_w, b_xte], [bpg[bl]])
                        for bl in range(2):
                            for k in range(8):
                                cx.mm(pv[:, bl, 0:258], wt[:, k, 1, f2 * 128:(f2 + 1) * 128], xte[:, k, 2 * bp + bl, :],
                                      k == 0, k == 7, [b_w, b_xte], [bpv[bl]])
                        (g0, bg0), (v0, bv0) = tmp[nk % 3]
                        cg = fc
                        cv = 22 + fc
                        cx.act(g0[:], pg[:, :, 0:256], AF.Identity, bpg + [b_cw, b_cb], [bg0],
                               bias=cwT[:, 3, cg:cg + 1], scale=cwT[:, 0, cg:cg + 1])
                        cx.act(v0[:], pv[:, :, 0:256], AF.Identity, bpv + [b_cw, b_cb], [bv0],
                               bias=cwT[:, 3, cv:cv + 1], scale=cwT[:, 0, cv:cv + 1])
                        if tail_fn is not None:
                            tail_fn()
                        cx.stt(g0[:], pg[:, :, 1:257], cwT[:, 1, cg:cg + 1], g0[:], ALU.mult, ALU.add, bpg + [b_cw, bg0], [bg0])
                        cx.stt(v0[:], pv[:, :, 1:257], cwT[:, 1, cv:cv + 1], v0[:], ALU.mult, ALU.add, bpv + [b_cw, bv0], [bv0])
                        cx.stt(g0[:], pg[:, :, 2:258], cwT[:, 2, cg:cg + 1], g0[:], ALU.mult, ALU.add, bpg + [b_cw, bg0], [bg0])
                        cx.stt(v0[:], pv[:, :, 2:258], cwT[:, 2, cv:cv + 1], v0[:], ALU.mult, ALU.add, bpv + [b_cw, bv0], [bv0])

                        def tail_fn(g0=g0, bg0=bg0, v0=v0, bv0=bv0, fc=fc, bp=bp):
                            cx.act(g0[:], g0[:], AF.Gelu_apprx_tanh, [bg0], [bg0])
                            cx.tt("pool", hbuf[:, fc, bp * 512:(bp + 1) * 512].rearrange("p (b t) -> p b t", b=2),
                                  g0[:], v0[:], ALU.mult, [bg0, bv0], [b_h])
            tail_fn()
            tail_fn = None
            for tl_ in range(8):
                tile = hf * 8 + tl_
                yb = 2 * (tile % 2)
                for h2 in range(2):
                    for fc in range(22):
                        cx.mm(banks.t[yb + h2][:, :], hbuf[:, fc, tl_ * 128:(tl_ + 1) * 128],
                              wd[:, fc, h2 * 512:(h2 + 1) * 512], fc == 0, fc == 21, [b_h, b_wd], [banks.b[yb + h2]])
                epi.run(tile, banks.t[yb][:, :], banks.t[yb + 1][:, :], [banks.b[yb], banks.b[yb + 1]],
                        xres_in, xres_out, xT_out, tail_out, out_bufs, next_tile=(tile + 1 if tl_ + 1 < 8 else None))
        epi.flush()
        cx.P.barrier()
    cx.es = None


def stage_attproj(cx, consts, banks, din, j, xT_in, qT_out, kT_out, v_out, kbar_out, in_bufs=(), out_bufs=None):
    wqkv = din["b_w_qkv"][j]
    with contextlib.ExitStack() as es:
        cx.es = es
        xT, b_xT = cx.sb([128, 8, NT], BF16, "xT")
        for c in range(8):
            cx.dma("sp", xT[:, c, :], xT_in[c * 128:(c + 1) * 128, :], reads=in_bufs, writes=[b_xT])
        stager = Stager(cx)
        w, _ = cx.sb([128, 8, 3072], BF16, "wqkv")
        wb_w = load_w_bf16(cx, stager, w, wqkv, 8, 0, 3072, maxc=512)
        stg = [cx.sb([128, 512], BF16, "stg") for _ in range(4)]
        kbss = [cx.sb([128, 8], F32, "kbs") for _ in range(2)]
        n = 0
        for fch in range(16):
            dst = qT_out if fch < 8 else kT_out
            r0 = (fch % 8) * 128
            for tg in range(4):
                kk = n % 2
                bank, bb = banks.t[kk], banks.b[kk]
                for k in range(8):
                    cx.mm(bank[:, :], w[:, k, fch * 128:(fch + 1) * 128], xT[:, k, tg * 512:(tg + 1) * 512],
                          k == 0, k == 7, wb_w.get(fch * 128, (fch + 1) * 128) + [b_xT], [bb])
                st_, b_st = stg[n % 4]
                if fch >= 8:
                    kbs, b_kbs = kbss[fch % 2]
                    cx.P.op("dve", lambda e, a=kbs, b=bank, t=tg: e.tensor_reduce(
                        out=a[:, 2 * t:2 * t + 2], in_=b[:, :].rearrange("p (b t) -> p b t", b=2),
                        axis=AX.X, op=ALU.add), reads=[bb], writes=[b_kbs])
                    if tg == 3:
                        cx.ts("dve", kbs[:], kbs[:], 1.0 / 256.0, None, ALU.mult, None, [b_kbs], [b_kbs])
                        o = cx.dma("sp", kbar_out[r0:r0 + 128, :], kbs[:], reads=[b_kbs], writes=out_bufs or ())
                        cx.outs.append(o)
                if n % 2 == 0 or fch >= 8:
                    cx.act(st_[:], bank[:, :], AF.Copy, [bb] + ([kbss[fch % 2][1]] if fch >= 8 else []), [b_st],
                           scale=(0.125 if fch < 8 else 1.0))
                else:
                    cx.ts("dve", st_[:], bank[:, :], (0.125 if fch < 8 else 1.0), None, ALU.mult, None, [bb], [b_st])
                o = cx.dma("sp", dst[r0:r0 + 128, tg * 512:(tg + 1) * 512], st_[:], reads=[b_st],
                           writes=out_bufs or ())
                cx.outs.append(o)
                n += 1
        for tile in range(16):
            for hf in range(2):
                kk = n % 2
                bank, bb = banks.t[kk], banks.b[kk]
                for k in range(8):
                    cx.mm(bank[:, :], xT[:, k, tile * 128:(tile + 1) * 128],
                          w[:, k, 2048 + hf * 512:2048 + (hf + 1) * 512], k == 0, k == 7,
                          [b_xT] + wb_w.get(2048 + hf * 512, 2048 + (hf + 1) * 512), [bb])
                st_, b_st = stg[n % 4]
                if n % 2 == 0:
                    cx.act(st_[:], bank[:, :], AF.Copy, [bb], [b_st])
                else:
                    cx.copy("dve", st_[:], bank[:, :], [bb], [b_st])
                o = cx.dma("sp", v_out[tile * 128:(tile + 1) * 128, hf * 512:(hf + 1) * 512], st_[:],
                           reads=[b_st], writes=out_bufs or ())
                cx.outs.append(o)
                n += 1
        cx.P.barrier()
    cx.es = None


def stage_attcore(cx, consts, banks, din, j, li, xres_in, qT_in, kT_all, v_all, kbar_all, bvd, xres_out, xT_out,
                  tail_out, in_bufs=(), out_bufs=None, qsel=(0, 1)):
    QW = 128 * len(qsel)
    q0 = qsel[0] * 128
    wo_d = din["b_w_o"][j]
    with contextlib.ExitStack() as es:
        cx.es = es
        rb, b_rb = cx.sb([33, 16], F32, "rb")
        cx.dma("sp", rb[0:32, :], din["rel_bias"], writes=[b_rb])
        cx.dma("sp", rb[32:33, :], din["ones16"], writes=[b_rb])
        oh, b_oh = cx.sb([33, 2048], F32, "oh")
        cx.dma("sp", oh[:], din["oh"], writes=[b_oh])
        bvs, b_bvs = cx.sb([16, 2048], BF16, "bvs")
        for hf in range(4):
            cx.mm(banks.t[hf][0:16, :], rb[:, :], oh[:, hf * 512:(hf + 1) * 512], True, True, [b_rb, b_oh],
                  [banks.b[hf]])
            cx.copy("dve", bvs[:, hf * 512:(hf + 1) * 512], banks.t[hf][0:16, :], [banks.b[hf]], [b_bvs])
        b_bvd = Buf("bvd")
        cx.dma("sp", bvd.ap(), bvs[:], reads=[b_bvs], writes=[b_bvd])
        chm, b_chm = cx.sb([128, 16], F32, "chm")
        cx.dma("sp", chm[:], bcast_rows(din["rel_bias"][31:32, :]), writes=[b_chm])
        gmask, b_gm = cx.sb([128, 16, 16], F32, "gmask")
        oof, b_oof = cx.sb([128, 16, 16], F32, "oof")
        farm, b_farm = cx.sb([128, 16, 16], F32, "farm")
        for i in range(8):
            for q in range(2):
                cx.dma("sp", gmask[:, 2 * i + q, :], din["gmask"][:, i, :], writes=[b_gm])
                cx.dma("sp", oof[:, 2 * i + q, :], din["oof"][:, i, :], writes=[b_oof])
        cx.memset("pool", farm[:], 0.0, [b_farm])
        for i in range(2, 8):
            cx.memset("pool", farm[:, 2 * i:2 * i + 2, 0:2 * i - 2], 1.0, [b_farm])
        jm, b_jm = cx.sb([128, 128], BF16, "jm")
        cx.dma("sp", jm[:], din["jm"], writes=[b_jm])
        stager = Stager(cx)
        wo, _ = cx.sb([128, 8, 1024], BF16, "wo")
        osb, b_osb = cx.sb([128, 16, D], BF16, "osb")
        kta = [cx.sb([80, 4096], BF16, "kta") for _ in range(2)]
        va = [cx.sb([128, 32, 65], BF16, "va") for _ in range(2)]
        qta = [cx.sb([80, NT], BF16, "qta") for _ in range(2)]
        tp = [cx.sb([128, 8, 256], BF16, "tp") for _ in range(2)]
        for k in range(2):
            cx.dma("sp", kta[k][0][64:80, :], din["koh"], writes=[kta[k][1]])
            cx.memset("pool", va[k][0][:, :, 64:65], 1.0, [va[k][1]])
        kb32s = [cx.sb([64, 16], F32, "kb32") for _ in range(2)]
        kbar = [cx.sb([64, 16], BF16, "kbar") for _ in range(2)]
        mpad, b_mp = cx.sb([128, 16, 80], BF16, "mpad")
        cx.memset("pool", mpad[:], 0.0, [b_mp])
        gm, b_g = cx.sb([128, 16, 16], F32, "gm")
        top8, b_t8 = cx.sb([128, 16, 8], F32, "top8")
        keep, b_kp = cx.sb([128, 16, 16], F32, "keep")
        ebuf = [cx.sb([128, 512], BF16, "ebuf") for _ in range(4)]
        rden = [cx.sb([128, 1], F32, "rden") for _ in range(4)]
        epi = Epi(cx, consts, banks, din["ln_mix_g"][li:li + 1, :], din["ln_mix_b"][li:li + 1, :])
        g7 = banks.t[7]

        def loads(h):
            hk = h % 2
            kt_, b_kt = kta[hk]
            va_, b_va = va[hk]
            qt_, b_qt = qta[hk]
            tp_, b_tp = tp[hk]
            for r in range(2):
                cx.dma("sp", kt_[0:64, :].rearrange("d (i r t) -> d i r t", i=8, r=2)[:, :, r, :],
                       kT_all[r, h * 64:(h + 1) * 64, :].rearrange("d (i t) -> d i t", i=8),
                       reads=in_bufs, writes=[b_kt])
            cx.dma("sp", qt_[0:64, :], qT_in[h * 64:(h + 1) * 64, :], reads=in_bufs, writes=[b_qt])
            kb32_, b_kb32_ = kb32s[hk]
            for r in range(2):
                cx.dma("sp", kb32_[:, :].rearrange("d (i r) -> d i r", r=2)[:, :, r], kbar_all[r, h * 64:(h + 1) * 64, :],
                       reads=in_bufs, writes=[b_kb32_], nonc=True)
            for r in range(2):
                for s_ in range(2):
                    cx.dma("sp", va_[:, :, 0:64].rearrange("p (i r s) d -> p i r s d", i=8, r=2)[:, :, r, s_, :],
                           v_all[r, :, h * 64:(h + 1) * 64].rearrange("(i s p) d -> p i s d", i=8, s=2)[:, :, s_, :],
                           reads=in_bufs, writes=[b_va], nonc=True)
            for rel in (-2, -1, 0, 1):
                for kt in range(2):
                    m0 = (rel + 2) * 512 + 128 * (1 - kt)
                    src = bass.AP(tensor=bvd, offset=h * 2048 + m0, ap=[[1, 128], [1, 256]])
                    cx.dma("sp", tp_[:, (rel + 2) * 2 + kt, :], src, reads=[b_bvd], writes=[b_tp])

        def gate1(h):
            hk = h % 2
            kt_, b_kt = kta[hk]
            qt_, b_qt = qta[hk]
            kbt, b_kb = kbar[hk]
            kb32_, b_kb32_ = kb32s[hk]
            cx.copy("dve", kbt[:], kb32_[:], [b_kb32_], [b_kb])

        def gate1b(h):
            hk = h % 2
            qt_, b_qt = qta[hk]
            kbt, b_kb = kbar[hk]
            for qc in range(16):
                cx.mm(g7[:, qc * 16:(qc + 1) * 16], qt_[0:64, qc * 128:(qc + 1) * 128], kbt[:, :], True, True,
                      [b_qt, b_kb], [banks.b[7]])
            cx.tt("dve", gm[:], g7[:, 0:256].rearrange("p (c n) -> p c n", c=16), gmask[:], ALU.add,
                  [banks.b[7], b_gm], [b_g])
            for qc in range(16):
                cx.P.op("dve", lambda e, c=qc: e.max(out=top8[:, c, :], in_=gm[:, c, :]), reads=[b_g], writes=[b_t8])
            cx.tt("dve", keep[:], gm[:], top8[:, :, 2:3].to_broadcast([128, 16, 16]), ALU.is_ge, [b_g, b_t8], [b_kp])
            cx.tt("dve", keep[:], keep[:], oof[:], ALU.max, [b_kp, b_oof], [b_kp])
            cx.ts("dve", keep[:], keep[:], -NEG, NEG, ALU.mult, ALU.add, [b_kp], [b_kp])
            cx.stt(mpad[:, :, 64:80], farm[:], chm[:, h:h + 1], keep[:], ALU.mult, ALU.add,
                   [b_farm, b_chm, b_kp], [b_mp])

        def gate2(h):
            hk = h % 2
            qt_, b_qt = qta[hk]
            for half in range(2):
                for c in range(8):
                    qc = half * 8 + c
                    cx.tr(banks.tT[0:80, c * 128:(c + 1) * 128], mpad[:, qc, :], consts.ident[:],
                          [b_mp, consts.b_ident], [banks.bT])
                cx.copy("dve", qt_[64:80, half * 1024:(half + 1) * 1024], banks.tT[64:80, :], [banks.bT], [b_qt])

        def main(h, hooks):
            hk = h % 2
            kt_, b_kt = kta[hk]
            va_, b_va = va[hk]
            qt_, b_qt = qta[hk]
            tp_, b_tp = tp[hk]
            slots = [(i, jb) for i in range(NBLK) for jb in range(2 * i + 2)]
            L = 2
            ebs = {}
            for t in range(len(slots) + L):
                if t < len(slots):
                    i, jb = slots[t]
                    if jb == 0 and i in hooks:
                        hooks[i]()
                    rel = jb - 2 * i
                    near = rel >= -2
                    sbk, bsb_ = banks.t[t % 3], banks.b[t % 3]
                    for kt in range(2):
                        gk = 2 * jb + kt
                        cx.mm(sbk[:, kt * QW:(kt + 1) * QW], kt_[0:80, gk * 128:(gk + 1) * 128],
                              qt_[0:80, i * 256 + q0:i * 256 + q0 + QW], True, not near, [b_kt, b_qt], [bsb_])
                        if near:
                            cx.mm(sbk[:, kt * QW:(kt + 1) * QW], jm[:, :],
                                  tp_[:, (rel + 2) * 2 + kt, q0:q0 + QW], False, True, [b_jm, b_tp], [bsb_])
                    eb, b_eb = ebuf[t % 4]
                    cx.act(eb[:, 0:2 * QW], sbk[:, 0:2 * QW], AF.Exp, [bsb_], [b_eb])
                    ebs[t] = (eb, b_eb)
                if t - L >= 0:
                    i, jb = slots[t - L]
                    eb, b_eb = ebs.pop(t - L)
                    nj = 2 * i + 2
                    ob = [(banks.t[3 + 2 * (i % 2) + q], banks.b[3 + 2 * (i % 2) + q]) for q in range(2)]
                    for q in qsel:
                        for kt in range(2):
                            gk = 2 * jb + kt
                            qo = (q - qsel[0]) * 128
                            cx.mm(ob[q][0][:, 0:65], eb[:, kt * QW + qo:kt * QW + qo + 128],
                                  va_[:, gk, :], jb == 0 and kt == 0, jb == nj - 1 and kt == 1,
                                  [b_eb, b_va], [ob[q][1]])
                    if jb == nj - 1:
                        for q in qsel:
                            rd, b_rd = rden[2 * (i % 2) + q]
                            cx.P.op("dve", lambda e, a=rd, b=ob[q][0]: e.reciprocal(out=a[:], in_=b[:, 64:65]),
                                    reads=[ob[q][1]], writes=[b_rd])
                            cx.ts("dve", osb[:, 2 * i + q, h * 64:(h + 1) * 64], ob[q][0][:, 0:64], rd[:, 0:1], None,
                                  ALU.mult, None, [ob[q][1], b_rd], [b_osb])

        loads(0)
        gate1(0)
        gate1b(0)
        gate2(0)
        wb_wo = load_w_bf16(cx, stager, wo, wo_d, 8, 0, 1024)
        for h in range(H):
            hooks = {}
            if h + 1 < H:
                loads(h + 1)
                hooks[3] = (lambda hh=h + 1: (gate1(hh), gate1b(hh)))
                hooks[6] = (lambda hh=h + 1: gate2(hh))
            main(h, hooks)
        ot = [cx.sb([128, 8, 128], BF16, "ot") for _ in range(3)]
        tiles = [t for t in range(16) if t % 2 in qsel]

        def prep(n):
            tile = tiles[n]
            otl, b_ot = ot[n % 3]
            for c in range(8):
                cx.tr(banks.tT[:, c * 128:(c + 1) * 128], osb[:, tile, c * 128:(c + 1) * 128], consts.ident[:],
                      [b_osb, consts.b_ident], [banks.bT])
            cx.copy("act", otl[:].rearrange("p c t -> p (c t)"), banks.tT[:, :], [banks.bT], [b_ot])

        prep(0)
        for n, tile in enumerate(tiles):
            otl, b_ot = ot[n % 3]
            yb = 2 * (n % 2)
            for hf in range(2):
                for c in range(8):
                    cx.mm(banks.t[yb + hf][:, :], otl[:, c, :], wo[:, c, hf * 512:(hf + 1) * 512], c == 0, c == 7,
                          [b_ot] + wb_wo.get(hf * 512, (hf + 1) * 512), [banks.b[yb + hf]])
            if n + 1 < len(tiles):
                prep(n + 1)
            epi.run(tile, banks.t[yb][:, :], banks.t[yb + 1][:, :], [banks.b[yb], banks.b[yb + 1]],
                    xres_in, xres_out, xT_out, tail_out, out_bufs,
                    next_tile=(tiles[n + 1] if n + 1 < len(tiles) else None))
        epi.flush()
        cx.P.barrier()
    cx.es = None


PARAM_SHAPES = {
    "ln_mix_g": (DEPTH, D), "ln_mix_b": (DEPTH, D), "ln_ffn_g": (DEPTH, D), "ln_ffn_b": (DEPTH, D),
    "a_w_in": (2, D, 2048), "a_ln_g": (2, D), "a_ln_b": (2, D), "a_w_s": (2, 8, 128, 128),
    "a_b_s": (2, 8, 128), "a_w_out": (2, D, D), "b_w_qkv": (2, D, 3072), "b_w_o": (2, D, D),
    "rel_bias": (32, 16), "f_w_up": (DEPTH, D, 2 * FF), "f_conv_w": (DEPTH, 3, 2 * FF),
    "f_conv_b": (DEPTH, 2 * FF), "f_w_down": (DEPTH, FF, D),
}
CONST_SHAPES = {
    "ident": ((128, 128), BF16), "jm": ((128, 128), BF16), "tril": ((128, 8, 128), BF16),
    "gmask": ((128, 8, 16), F32), "oof": ((128, 8, 16), F32), "oh": ((33, 2048), F32),
    "koh": ((16, 4096), BF16), "ones16": ((1, 16), F32), "ident32": ((128, 128), F32),
}
STAGE_PARAMS = {
    "pro": ["ident"],
    "gmlp": ["ln_mix_g", "ln_mix_b", "a_w_in", "a_ln_g", "a_ln_b", "a_w_s", "a_b_s", "a_w_out", "ident", "tril"],
    "ffn": ["ln_ffn_g", "ln_ffn_b", "f_w_up", "f_conv_w", "f_conv_b", "f_w_down", "ident", "ident32"],
    "attproj": ["b_w_qkv", "ident"],
    "attcore": ["ln_mix_g", "ln_mix_b", "b_w_o", "rel_bias", "ident", "jm", "gmask", "oof", "oh", "koh", "ones16"],
}


def declare(nc, names, single_layer):
    din = {}
    for n in names:
        if n in PARAM_SHAPES:
            shp = list(PARAM_SHAPES[n])
            if single_layer and n != "rel_bias":
                shp[0] = 1
            din[n] = nc.dram_tensor(n, shp, F32, kind="ExternalInput").ap()
        else:
            shp, dt = CONST_SHAPES[n]
            din[n] = nc.dram_tensor(n, list(shp), dt, kind="ExternalInput").ap()
    return din


def rel_bucket_np(dist):
    n = np.maximum(dist, 0)
    nf = np.maximum(n, 1).astype(np.float32)
    large = 16 + (np.log(nf / np.float32(16)) / np.float32(math.log(8)) * np.float32(16)).astype(np.int32)
    large = np.minimum(large, 31)
    return np.where(n < 16, n, large)


def host_consts(hA, hX):
    c = {}
    c["ident"] = np.eye(128, dtype=np.float32).astype(NPBF)
    c["jm"] = np.eye(128, dtype=np.float32)[::-1].copy().astype(NPBF)
    tr = np.tril(np.ones((128, 128), np.float32))
    c["tril"] = np.ascontiguousarray(np.broadcast_to(tr[:, None, :], (128, 8, 128))).astype(NPBF)
    hs = (hA, 1 - hA)
    gslot = np.array([2 * (s_ // 2) + hs[s_ % 2] for s_ in range(16)])
    gm = np.zeros((8, 16), np.float32)
    oo = np.zeros((8, 16), np.float32)
    for i in range(8):
        G = 2 * i + hX
        gm[i, gslot >= G] = -1e30
        oo[i, gslot >= G] = 1.0
    c["gmask"] = np.ascontiguousarray(np.broadcast_to(gm[None], (128, 8, 16)))
    c["oof"] = np.ascontiguousarray(np.broadcast_to(oo[None], (128, 8, 16)))
    oh = np.zeros((33, 2048), np.float32)
    m = np.arange(512)
    for ri, rel in enumerate((-2, -1, 0, 1)):
        sig = rel % 2
        bd = (2 if rel < 0 else 0) + hX - hs[sig]
        dist = m - 255 + 256 * bd
        bk = rel_bucket_np(dist)
        ok = dist >= 0
        oh[bk[ok], ri * 512 + m[ok]] = 1.0
        oh[32, ri * 512 + m[~ok]] = NEG
    c["oh"] = oh
    koh = np.zeros((16, 4096), np.float32)
    for n in range(16):
        koh[n, n * 256:(n + 1) * 256] = 1.0
    c["koh"] = koh.astype(NPBF)
    c["ones16"] = np.ones((1, 16), np.float32)
    c["ident32"] = np.eye(128, dtype=np.float32)
    fl = np.zeros((128, 2), np.float32)
    fl[:, hX] = 1.0
    c["hflag"] = fl
    return c


_PROGS = {}


def build_unfused(kind):
    if kind in _PROGS:
        return _PROGS[kind]
    nc = bass.Bass("TRN2", target_bir_lowering=False)
    din = declare(nc, STAGE_PARAMS[kind], True)

    def ext_in(name, shape, dt):
        return nc.dram_tensor(name, list(shape), dt, kind="ExternalInput").ap()

    def ext_out(name, shape, dt):
        return nc.dram_tensor(name, list(shape), dt, kind="ExternalOutput").ap()

    cx = Cx(nc)
    with contextlib.ExitStack() as top:
        cx.es = top
        consts = load_consts(cx, din)
        banks = Banks(cx)
        if kind == "pro":
            x_in = ext_in("xres_in", [NT, D], F32)
            stage_prologue(cx, consts, banks, x_in, ext_out("xT_out", [D, NT], BF16),
                           ext_out("tail_out", [D, 16], BF16))
        elif kind == "gmlp":
            stage_gmlp(cx, consts, banks, din, 0, 0, ext_in("xres_in", [NT, D], F32),
                       ext_in("xT_in", [D, NT], BF16), ext_out("xres_out", [NT, D], F32),
                       ext_out("xT_out", [D, NT], BF16), ext_out("tail_out", [D, 16], BF16))
        elif kind == "ffn":
            halo_in = ext_in("halo_in", [D, 16], BF16)

            def halo_fn(cx_, halo, b_halo):
                cx_.dma("sp", halo[:], halo_in.rearrange("(k p) t -> p k t", p=128), writes=[b_halo], nonc=True)

            stage_ffn(cx, consts, banks, din, 0, ext_in("xres_in", [NT, D], F32), ext_in("xT_in", [D, NT], BF16),
                      halo_fn, ext_out("xres_out", [NT, D], F32), ext_out("xT_out", [D, NT], BF16),
                      ext_out("tail_out", [D, 16], BF16))
        elif kind == "attproj":
            stage_attproj(cx, consts, banks, din, 0, ext_in("xT_in", [D, NT], BF16),
                          ext_out("qT_out", [D, NT], BF16), ext_out("kT_out", [D, NT], BF16),
                          ext_out("v_out", [NT, D], BF16), ext_out("kbar_out", [D, 8], F32))
        elif kind == "attcore":
            bvd = nc.dram_tensor("bvd", [16, 2048], BF16, kind="Internal")
            stage_attcore(cx, consts, banks, din, 0, 0, ext_in("xres_in", [NT, D], F32),
                          ext_in("qT_in", [D, NT], BF16), ext_in("kT_all", [2, D, NT], BF16),
                          ext_in("v_all", [2, NT, D], BF16), ext_in("kbar_all", [2, D, 8], F32), bvd,
                          ext_out("xres_out", [NT, D], F32),
                          ext_out("xT_out", [D, NT], BF16), ext_out("tail_out", [D, 16], BF16))
        cx.P.finish(cx.outs)
        cx.P.emit_all()
    _PROGS[kind] = nc
    return nc


def run_stage(kind, in_maps, cores):
    nc = build_unfused(kind)
    res = run_bass_kernel_spmd(nc, in_maps, core_ids=list(range(len(cores))))
    return res.results


LAYER_PARAM_IDX = {
    "gmlp": lambda li: {"ln_mix_g": li, "ln_mix_b": li, "a_w_in": li // 2, "a_ln_g": li // 2, "a_ln_b": li // 2,
                        "a_w_s": li // 2, "a_b_s": li // 2, "a_w_out": li // 2},
    "ffn": lambda li: {"ln_ffn_g": li, "ln_ffn_b": li, "f_w_up": li, "f_conv_w": li, "f_conv_b": li,
                       "f_w_down": li},
    "attproj": lambda li: {"b_w_qkv": li // 2},
    "attcore": lambda li: {"ln_mix_g": li, "ln_mix_b": li, "b_w_o": li // 2},
}


def stage_inputs(kind, li, params, consts_c):
    m = {}
    idx = LAYER_PARAM_IDX.get(kind, lambda li: {})(li)
    for n in STAGE_PARAMS[kind]:
        if n in idx:
            m[n] = np.ascontiguousarray(params[n][idx[n]:idx[n] + 1])
        elif n == "rel_bias":
            m[n] = params[n]
        else:
            m[n] = consts_c[n]
    return m


def to_local(x):
    B = x.shape[0]
    xb = x.reshape(B, 16, 256, D)
    return [np.ascontiguousarray(xb[c // 2, (c % 2)::2].reshape(NT, D)) for c in range(2 * B)]


def from_local(outs, B):
    y = np.zeros((B, 16, 256, D), np.float32)
    for c in range(2 * B):
        y[c // 2, (c % 2)::2] = outs[c].reshape(8, 256, D)
    return y.reshape(B, 4096, D)


def make_halo(tails, c):
    half = c % 2
    pt = tails[c ^ 1]
    halo = np.zeros((D, 16), NPBF)
    if half == 0:
        halo[:, 2:16] = pt[:, 0:14]
    else:
        halo[:, :] = pt
    return halo


def forward_unfused(x, params, ncores=8, nlayers=DEPTH, debug=None):
    cores = list(range(ncores))
    cc = [host_consts(c % 2, c % 2) for c in cores]
    xl = to_local(np.asarray(x, np.float32)[: ncores // 2])
    r = run_stage("pro", [dict(xres_in=xl[c], **stage_inputs("pro", 0, params, cc[c])) for c in cores], cores)
    xres = xl
    xT = [r[c]["xT_out"] for c in cores]
    tails = [r[c]["tail_out"] for c in cores]
    for li in range(nlayers):
        if li % 2 == 0:
            r = run_stage("gmlp", [dict(xres_in=xres[c], xT_in=xT[c], **stage_inputs("gmlp", li, params, cc[c]))
                                   for c in cores], cores)
        else:
            r = run_stage("attproj", [dict(xT_in=xT[c], **stage_inputs("attproj", li, params, cc[c]))
                                      for c in cores], cores)
            ims = []
            for c in cores:
                p0, p1 = c, c ^ 1
                ims.append(dict(xres_in=xres[c], qT_in=r[c]["qT_out"],
                                kT_all=np.stack([r[p0]["kT_out"], r[p1]["kT_out"]]),
                                v_all=np.stack([r[p0]["v_out"], r[p1]["v_out"]]),
                                kbar_all=np.stack([r[p0]["kbar_out"], r[p1]["kbar_out"]]),
                                **stage_inputs("attcore", li, params, cc[c])))
            r = run_stage("attcore", ims, cores)
        xres = [r[c]["xres_out"] for c in cores]
        xT = [r[c]["xT_out"] for c in cores]
        tails = [r[c]["tail_out"] for c in cores]
        if debug is not None:
            debug.append(("mix%d" % li, from_local(xres, ncores // 2)))
        r = run_stage("ffn", [dict(xres_in=xres[c], xT_in=xT[c], halo_in=make_halo(tails, c),
                                   **stage_inputs("ffn", li, params, cc[c])) for c in cores], cores)
        xres = [r[c]["xres_out"] for c in cores]
        xT = [r[c]["xT_out"] for c in cores]
        tails = [r[c]["tail_out"] for c in cores]
        if debug is not None:
            debug.append(("ffn%d" % li, from_local(xres, ncores // 2)))
    return from_local(xres, ncores // 2)


PASS_TABLES = ["gmask", "oof", "oh", "hflag"]
CONST_SHAPES["hflag"] = ((128, 2), F32)
COMMON_CONSTS = ["ident", "jm", "tril", "koh", "ones16", "ident32"]


def build_fused(nlayers=DEPTH):
    key = ("fused", nlayers)
    if key in _PROGS:
        return _PROGS[key]
    nc = bass.Bass("TRN2", target_bir_lowering=False)
    din = declare(nc, list(PARAM_SHAPES.keys()) + COMMON_CONSTS, False)
    dinx = {}
    for X in "AB":
        d = dict(din)
        for n in PASS_TABLES:
            shp, dt = CONST_SHAPES[n]
            d[n] = nc.dram_tensor("%s_%s" % (n, X), list(shp), dt, kind="ExternalInput").ap()
        dinx[X] = d

    def internal(name, shape, dt):
        return nc.dram_tensor(name, list(shape), dt, kind="Internal")

    x_in = {X: nc.dram_tensor("x_%s" % X, [NT, D], F32, kind="ExternalInput").ap() for X in "AB"}
    out = nc.dram_tensor("out", [NT, D], F32, kind="ExternalOutput").ap()
    xres = {X: [internal("xres_%s%d" % (X, k), [NT, D], F32).ap() for k in range(2)] for X in "AB"}
    xTb = {X: [internal("xT_%s%d" % (X, k), [D, NT], BF16).ap() for k in range(2)] for X in "AB"}
    tail = {X: internal("tail_%s" % X, [D, 16], BF16).ap() for X in "AB"}
    qT = {X: internal("qT_%s" % X, [D, NT], BF16).ap() for X in "AB"}
    kT_all = internal("kT_all", [2, D, NT], BF16).ap()
    v_all = internal("v_all", [2, NT, D], BF16).ap()
    kbar_all = internal("kbar_all", [2, D, 8], F32).ap()
    bvd = {X: internal("bvd_%s" % X, [16, 2048], BF16) for X in "AB"}
    sig = {"A": 0, "B": 1}
    other = {"A": "B", "B": "A"}
    cx = Cx(nc)
    with contextlib.ExitStack() as top:
        cx.es = top
        consts = load_consts(cx, din)
        banks = Banks(cx)
        cur = {}
        for X in "AB":
            stage_prologue(cx, consts, banks, x_in[X], xTb[X][0], tail[X])
            cur[X] = dict(xres=x_in[X], xT=xTb[X][0], k=0)

        def nxt(X, final=False):
            st = cur[X]
            k = st["k"]
            st["k"] ^= 1
            if final:
                return out, None
            return xres[X][k], xTb[X][k ^ 1]

        def halo_fn_for(X):
            def halo_fn(cx_, halo, b_halo):
                to, b_to = cx_.sb([128, 8, 16], BF16, "to")
                fl, b_fl = cx_.sb([128, 2], F32, "hfl")
                tmp, b_tmp = cx_.sb([128, 8, 16], F32, "htmp")
                cx_.dma("sp", to[:], tail[other[X]].rearrange("(k p) t -> p k t", p=128), writes=[b_to], nonc=True)
                cx_.dma("sp", fl[:], dinx[X]["hflag"], writes=[b_fl])
                cx_.ts("dve", tmp[:], to[:], fl[:, 1:2], None, ALU.mult, None, [b_to, b_fl], [b_tmp])
                cx_.copy("dve", halo[:, :, 0:2], tmp[:, :, 0:2], [b_tmp], [b_halo])
                cx_.stt(halo[:, :, 2:16], to[:, :, 0:14], fl[:, 0:1], tmp[:, :, 2:16], ALU.mult, ALU.add,
                        [b_to, b_fl, b_tmp], [b_halo])
            return halo_fn

        for li in range(nlayers):
            lastl = li == DEPTH - 1
            j = li // 2
            passes = "AB"
            if li % 2 == 0:
                for X in passes:
                    xo, xTo = nxt(X)
                    stage_gmlp(cx, consts, banks, dinx[X], j, li, cur[X]["xres"], cur[X]["xT"], xo, xTo, tail[X])
                    cur[X]["xres"], cur[X]["xT"] = xo, xTo
            else:
                for X in passes:
                    stage_attproj(cx, consts, banks, dinx[X], j, cur[X]["xT"], qT[X], kT_all[sig[X]],
                                  v_all[sig[X]], kbar_all[sig[X]])
                for X in "AB":
                    xo, xTo = nxt(X)
                    stage_attcore(cx, consts, banks, dinx[X], j, li, cur[X]["xres"], qT[X], kT_all, v_all, kbar_all,
                                  bvd[X], xo, xTo, tail[X], qsel=((1,) if (lastl and X == "B") else (0, 1)))
                    cur[X]["xres"], cur[X]["xT"] = xo, xTo
            for X in ("A" if lastl else "AB"):
                cx.outs = []
                xo, xTo = nxt(X, final=lastl)
                stage_ffn(cx, consts, banks, dinx[X], li, cur[X]["xres"], cur[X]["xT"], halo_fn_for(X), xo, xTo,
                          None)
                cur[X]["xres"], cur[X]["xT"] = xo, xTo
        if nlayers < DEPTH:
            cx.es = top
            t, bt = cx.sb([128, D], F32, "dbg")
            cx.outs = []
            for tile in range(16):
                cx.dma("sp", t[:], cur["A"]["xres"][tile * 128:(tile + 1) * 128, :], writes=[bt])
                cx.outs.append(cx.dma("sp", out[tile * 128:(tile + 1) * 128, :], t[:], reads=[bt]))
        cx.P.finish(cx.outs)
        cx.P.emit_all()
    _PROGS[key] = nc
    return nc


def forward_fused(x, params, ncores=8, nlayers=DEPTH):
    nc = build_fused(nlayers)
    xl = to_local(np.asarray(x, np.float32)[: ncores // 2])
    in_maps = []
    for c in range(ncores):
        hA = c % 2
        m = {k: params[k] for k in PARAM_SHAPES}
        ca = host_consts(hA, hA)
        cb = host_consts(hA, 1 - hA)
        for n in COMMON_CONSTS:
            m[n] = ca[n]
        for n in PASS_TABLES:
            m[n + "_A"] = ca[n]
            m[n + "_B"] = cb[n]
        m["x_A"] = xl[c]
        m["x_B"] = xl[c ^ 1]
        in_maps.append(m)
    res = run_bass_kernel_spmd(nc, in_maps, core_ids=list(range(ncores)))
    return from_local([res.results[c]["out"] for c in range(ncores)], ncores // 2)


def kernel(**inputs):
    params = {k: np.ascontiguousarray(np.asarray(v, np.float32)) for k, v in inputs.items() if k != "x"}
    x = np.asarray(inputs["x"], np.float32)
    return forward_fused(x, params).astype(np.float32)
```

```python
import contextlib
import math
import numpy as np
import ml_dtypes
import concourse.bass as bass
import concourse.mybir as mybir
from concourse.bass_utils import run_bass_kernel_spmd

F32 = mybir.dt.float32
BF16 = mybir.dt.bfloat16
AF = mybir.ActivationFunctionType
ALU = mybir.AluOpType
AX = mybir.AxisListType
NPBF = ml_dtypes.bfloat16

D = 1024
DEPTH = 4
NT = 2048
NBLK = 8
H = 16
DH = 64
FF = 2816
ALPHA = (2 * DEPTH) ** 0.25
EPS = 1e-5
NEG = -30000.0

ENGS = ("pe", "act", "dve", "pool", "sp")
SAME_ENGINE_SYNC = True
N_DMA_SLOTS = 20
STORE_Q = "pool"


class Buf:
    __slots__ = ("name", "w", "r")

    def __init__(self, name=""):
        self.name = name
        self.w = None
        self.r = []


class Op:
    __slots__ = ("eng", "emit", "deps", "dma", "slot", "slot_val", "prev_val",
                 "signal", "count", "epoch")

    def __init__(self, eng, emit, dma):
        self.eng = eng
        self.emit = emit
        self.deps = set()
        self.dma = dma
        self.slot = None
        self.slot_val = 0
        self.prev_val = 0
        self.signal = False
        self.count = 0
        self.epoch = 0


class Prog:
    def __init__(self, nc):
        self.nc = nc
        self.ops = {e: [] for e in ENGS}
        self.slot_rr = {e: 0 for e in ENGS}
        self.slot_cum = {}
        self.pending_dma = []
        self.epoch = 0

    def op(self, eng, emit, reads=(), writes=(), dma=False):
        o = Op(eng, emit, dma)
        o.epoch = self.epoch
        for b in reads:
            if b.w is not None and b.w is not o:
                o.deps.add(b.w)
            b.r.append(o)
        for b in writes:
            if b.w is not None and b.w is not o:
                o.deps.add(b.w)
            for r in b.r:
                if r is not o:
                    o.deps.add(r)
            b.w = o
            b.r = []
        if dma:
            k = self.slot_rr[eng]
            self.slot_rr[eng] = (k + 1) % N_DMA_SLOTS
            key = (eng, k)
            prev = self.slot_cum.get(key, 0)
            o.slot = key
            o.prev_val = prev
            o.slot_val = prev + 16
            self.slot_cum[key] = o.slot_val
            self.pending_dma.append(o)
        self.ops[eng].append(o)
        return o

    def barrier(self):
        lasts = []
        for e in ENGS:
            for o in reversed(self.ops[e]):
                if not o.dma and o.emit is not None:
                    lasts.append(o)
                    break
        pend = list(self.pending_dma)
        self.pending_dma = []
        for e in ENGS:
            o = Op(e, None, False)
            o.epoch = self.epoch
            o.deps.update(lasts)
            o.deps.update(pend)
            self.ops[e].append(o)
        self.nbar = getattr(self, "nbar", 0) + 1
        self.epoch = self.nbar // 3

    def finish(self, out_ops):
        o = Op("sp", None, False)
        o.epoch = self.epoch
        o.deps.update(out_ops)
        self.ops["sp"].append(o)

    def emit_all(self):
        nc = self.nc
        for e in ENGS:
            for o in self.ops[e]:
                for d in o.deps:
                    if d.dma:
                        continue
                    if d.eng == o.eng and (d.eng == "pe" or not SAME_ENGINE_SYNC):
                        continue
                    d.signal = True
        for e in ENGS:
            c = {}
            for o in self.ops[e]:
                if o.signal:
                    c[o.epoch] = c.get(o.epoch, 0) + 1
                o.count = c.get(o.epoch, 0)
        with contextlib.ExitStack() as es:
            esem = {}
            for e in ENGS:
                for ep in range(self.epoch + 1):
                    if any(o.signal and o.epoch == ep for o in self.ops[e]):
                        esem[(e, ep)] = es.enter_context(nc.semaphore("s_%s_%d" % (e, ep)))
            dsem = {}
            for key in self.slot_cum:
                dsem[key] = es.enter_context(nc.semaphore("d_%s_%d" % key))
            block = es.enter_context(nc.Block())

            def run(e, eng):
                waited = {}
                for o in self.ops[e]:
                    waits = {}
                    for d in o.deps:
                        if d.dma:
                            s, v = dsem[d.slot], d.slot_val
                            k = ("d",) + d.slot
                        else:
                            if d.eng == e and (e == "pe" or not SAME_ENGINE_SYNC):
                                continue
                            s, v = esem[(d.eng, d.epoch)], d.count
                            k = ("e", d.eng, d.epoch)
                        if waits.get(k, (None, 0))[1] < v:
                            waits[k] = (s, v)
                    if o.dma and o.prev_val > 0:
                        k = ("d",) + o.slot
                        if waits.get(k, (None, 0))[1] < o.prev_val:
                            waits[k] = (dsem[o.slot], o.prev_val)
                    for k, (s, v) in waits.items():
                        if waited.get(k, 0) >= v:
                            continue
                        waited[k] = v
                        eng.wait_ge(s, v)
                    if o.emit is None:
                        continue
                    ins = o.emit(eng)
                    if o.dma:
                        ins.then_inc(dsem[o.slot], 16)
                    elif o.signal:
                        ins.then_inc(esem[(e, o.epoch)], 1)

            @block.tensor
            def _(eng):
                run("pe", eng)

            @block.scalar
            def _(eng):
                run("act", eng)

            @block.vector
            def _(eng):
                run("dve", eng)

            @block.gpsimd
            def _(eng):
                run("pool", eng)

            @block.sync
            def _(eng):
                run("sp", eng)


class Cx:
    def __init__(self, nc):
        self.nc = nc
        self.P = Prog(nc)
        self.es = None
        self.uid = 0
        self.dq = 0
        self.outs = []

    def sb(self, shape, dt, name=None):
        self.uid += 1
        t = self.es.enter_context(self.nc.sbuf_tensor("%s_%d" % (name or "t", self.uid), list(shape), dt))
        return t, Buf(name or "t")

    def ps(self, shape, dt, name=None):
        self.uid += 1
        t = self.es.enter_context(self.nc.psum_tensor("%s_%d" % (name or "p", self.uid), list(shape), dt))
        return t, Buf(name or "p")

    def dram(self, name, shape, dt, kind):
        return self.nc.dram_tensor(name, list(shape), dt, kind=kind)

    def mm(self, out, lhsT, rhs, start, stop, reads, writes):
        self.P.op("pe", lambda e: e.matmul(out, lhsT=lhsT, rhs=rhs, start=start, stop=stop),
                  reads=reads, writes=writes)

    def tr(self, out, in_, ident, reads, writes):
        self.P.op("pe", lambda e: e.transpose(out=out, in_=in_, identity=ident), reads=reads, writes=writes)

    def act(self, out, in_, func, reads, writes, bias=None, scale=None):
        kw = {}
        if bias is not None:
            kw["bias"] = bias
        if scale is not None:
            kw["scale"] = scale
        self.P.op("act", lambda e: e.activation(out=out, in_=in_, func=func, **kw), reads=reads, writes=writes)

    def ts(self, eng, out, in0, s1, s2, op0, op1, reads, writes):
        if op1 is None:
            self.P.op(eng, lambda e: e.tensor_scalar(out=out, in0=in0, scalar1=s1, scalar2=None, op0=op0),
                      reads=reads, writes=writes)
        else:
            self.P.op(eng, lambda e: e.tensor_scalar(out=out, in0=in0, scalar1=s1, scalar2=s2, op0=op0, op1=op1),
                      reads=reads, writes=writes)

    def stt(self, out, in0, scalar, in1, op0, op1, reads, writes):
        self.P.op("dve", lambda e: e.scalar_tensor_tensor(out=out, in0=in0, scalar=scalar, in1=in1,
                                                          op0=op0, op1=op1), reads=reads, writes=writes)

    def tt(self, eng, out, in0, in1, op, reads, writes):
        self.P.op(eng, lambda e: e.tensor_tensor(out=out, in0=in0, in1=in1, op=op), reads=reads, writes=writes)

    def copy(self, eng, out, in_, reads, writes):
        if eng == "act":
            self.P.op("act", lambda e: e.copy(out=out, in_=in_), reads=reads, writes=writes)
        else:
            self.P.op(eng, lambda e: e.tensor_copy(out=out, in_=in_), reads=reads, writes=writes)

    def memset(self, eng, ap, val, writes):
        self.P.op(eng, lambda e: e.memset(ap, val), writes=writes)

    def dma(self, q, out, in_, reads=(), writes=(), nonc=False):
        if nonc:
            def em(e):
                with self.nc.allow_non_contiguous_dma(reason="small strided param load"):
                    return e.dma_start(out=out, in_=in_)
        else:
            def em(e):
                return e.dma_start(out=out, in_=in_)
        return self.P.op(q, em, reads=reads, writes=writes, dma=True)


def bcast_rows(ap2d_row, n=128):
    a = ap2d_row.partition_broadcast(n)
    if len(a.shape) == 3:
        a = a[:, 0, :]
    return a


class Consts:
    pass


def load_consts(cx, din):
    c = Consts()
    c.ident, c.b_ident = cx.sb([128, 128], BF16, "ident")
    cx.dma("sp", c.ident[:], din["ident"], writes=[c.b_ident])
    c.eps, c.b_eps = cx.sb([128, 1], F32, "eps")
    cx.memset("dve", c.eps[:], EPS, [c.b_eps])
    return c


def rstd_op(cx, consts, rstd, b_rstd, var_ap, b_var):
    cx.act(rstd, var_ap, AF.Sqrt, [b_var, consts.b_eps], [b_rstd], bias=consts.eps[:, 0:1])
    cx.P.op("dve", lambda e: e.reciprocal(out=rstd, in_=rstd), reads=[b_rstd], writes=[b_rstd])


class Banks:
    def __init__(self, cx):
        self.pb = []
        self.t = []
        self.b = []
        for i in range(4):
            t, _ = cx.ps([128, 1024], F32, "pb%d" % i)
            self.pb.append(t)
            for hf in range(2):
                self.t.append(t[:, hf * 512:(hf + 1) * 512])
                self.b.append(Buf("bank%d" % (2 * i + hf)))
        self.tT = self.pb[3].bitcast(BF16)[:, 1024:2048]
        self.bT = self.b[7]

    def pair(self, i):
        return self.pb[i][:, :].rearrange("p (b c) -> p b c", b=2), [self.b[2 * i], self.b[2 * i + 1]]


class Epi:
    def __init__(self, cx, consts, banks, lng_row, lnb_row):
        self.cx = cx
        self.consts = consts
        self.banks = banks
        self.lng, self.b_lng = cx.sb([128, D], F32, "lng")
        self.lnb, self.b_lnb = cx.sb([128, D], F32, "lnb")
        cx.dma("sp", self.lng[:], bcast_rows(lng_row), writes=[self.b_lng])
        cx.dma("sp", self.lnb[:], bcast_rows(lnb_row), writes=[self.b_lnb])
        self.xr = [cx.sb([128, D], F32, "xr") for _ in range(2)]
        self.s = [cx.sb([128, D], F32, "s")] * 2
        self.xn = [cx.sb([128, D], F32, "xn")] * 2
        self.xnb = [cx.sb([128, D], BF16, "xnb") for _ in range(2)]
        self.xts = [cx.sb([128, 8, 128], BF16, "xts") for _ in range(2)]
        self.st = [cx.sb([128, 2, 6], F32, "st") for _ in range(2)]
        self.mv = [cx.sb([128, 2], F32, "mv") for _ in range(2)]
        self.rstd = [cx.sb([128, 1], F32, "rstd") for _ in range(2)]
        self.tl, self.b_tl = cx.sb([128, 8, 16], BF16, "tl")
        self.k = 0
        self.pending = None

    def prefetch(self, tile, xres_in):
        xr, b_xr = self.xr[self.k]
        self.cx.dma("sp", xr[:], xres_in[tile * 128:(tile + 1) * 128, :], writes=[b_xr])
        self.pre = tile

    def run(self, tile, y0, y1, by, xres_in, xres_out, xT_out, tail_out, out_bufs=None, next_tile=None):
        cx = self.cx
        k = self.k
        self.k ^= 1
        self.flush()
        xr, b_xr = self.xr[k]
        s, b_s = self.s[k]
        xn, b_xn = self.xn[k]
        xnb, b_xnb = self.xnb[k]
        xts, b_xts = self.xts[k]
        st, b_st = self.st[k]
        mv, b_mv = self.mv[k]
        rstd, b_rstd = self.rstd[k]
        rows = slice(tile * 128, (tile + 1) * 128)
        if getattr(self, "pre", None) != tile:
            cx.dma("sp", xr[:], xres_in[rows, :], writes=[b_xr])
        self.pre = None
        if next_tile is not None:
            self.prefetch(next_tile, xres_in)
        cx.stt(s[:, 0:512], xr[:, 0:512], ALPHA, y0, ALU.mult, ALU.add, [b_xr, by[0]], [b_s])
        cx.stt(s[:, 512:1024], xr[:, 512:1024], ALPHA, y1, ALU.mult, ALU.add, [b_xr, by[1]], [b_s])
        cx.P.op("dve", lambda e: e.bn_stats(out=st[:, 0, :], in_=s[:, 0:512]), reads=[b_s], writes=[b_st])
        cx.P.op("dve", lambda e: e.bn_stats(out=st[:, 1, :], in_=s[:, 512:1024]), reads=[b_s], writes=[b_st])
        cx.P.op("dve", lambda e: e.bn_aggr(out=mv[:], in_=st[:].rearrange("p a b -> p (a b)")),
                reads=[b_st], writes=[b_mv])
        rstd_op(cx, self.consts, rstd[:], b_rstd, mv[:, 1:2], b_mv)
        cx.stt(s[:], s[:], mv[:, 0:1], self.lng[:], ALU.subtract, ALU.mult, [b_s, b_mv, self.b_lng], [b_s])
        cx.stt(xn[:], s[:], rstd[:, 0:1], self.lnb[:], ALU.mult, ALU.add, [b_s, b_rstd, self.b_lnb], [b_xn])
        o = cx.dma(STORE_Q, xres_out[rows, :], xn[:], reads=[b_xn], writes=out_bufs or ())
        cx.outs.append(o)
        if xT_out is None:
            return
        cx.copy("act", xnb[:], xn[:], [b_xn], [b_xnb])
        self.pending = (tile, xnb, b_xnb, xts, b_xts, xT_out, tail_out, out_bufs)

    def flush(self):
        if self.pending is None:
            return
        cx = self.cx
        tile, xnb, b_xnb, xts, b_xts, xT_out, tail_out, out_bufs = self.pending
        self.pending = None
        bk = self.banks
        for c in range(8):
            cx.tr(bk.tT[:, c * 128:(c + 1) * 128], xnb[:, c * 128:(c + 1) * 128], self.consts.ident[:],
                  [b_xnb, self.consts.b_ident], [bk.bT])
        cx.copy("act", xts[:].rearrange("p c t -> p (c t)"), bk.tT[:, :], [bk.bT], [b_xts])
        o = cx.dma(STORE_Q, xT_out.rearrange("(c p) t -> p c t", p=128)[:, :, tile * 128:(tile + 1) * 128], xts[:],
                   reads=[b_xts], writes=out_bufs or ())
        cx.outs.append(o)
        if tile % 2 == 1:
            blk = tile // 2
            cx.copy("pool", self.tl[:, :, 2 * blk:2 * blk + 2], xts[:, :, 126:128], [b_xts], [self.b_tl])
        if tile == 15 and tail_out is not None:
            o = cx.dma("sp", tail_out.rearrange("(c p) t -> p c t", p=128), self.tl[:], reads=[self.b_tl],
                       writes=out_bufs or (), nonc=True)
            cx.outs.append(o)


class Stager:
    def __init__(self, cx, n=4, size=1024):
        self.bufs = [cx.sb([128, size], F32, "stg32") for _ in range(n)]
        self.k = 0
        self.size = size

    def load(self, cx, dst, src, b_dst, shape2=None, eng="pool"):
        t, b = self.bufs[self.k % len(self.bufs)]
        self.k += 1
        if shape2 is None:
            n = dst.shape[1]
            view = t[:, 0:n]
        else:
            a, bb = shape2
            view = t[:, 0:a * bb].rearrange("p (a b) -> p a b", a=a)
        cx.dma("sp", view, src, writes=[b])
        cx.copy(eng, dst, view, [b], [b_dst])

    def dma(self, cx, src, n=None, shape2=None):
        t, b = self.bufs[self.k % len(self.bufs)]
        self.k += 1
        if shape2 is None:
            view = t[:, 0:n]
        else:
            a, bb = shape2
            view = t[:, 0:a * bb].rearrange("p (a b) -> p a b", a=a)
        cx.dma("sp", view, src, writes=[b])
        return view, b


class WBufs:
    def __init__(self, ncols, piece):
        self.piece = piece
        self.bufs = [Buf("w") for _ in range((ncols + piece - 1) // piece)]

    def get(self, c0, c1):
        return self.bufs[c0 // self.piece:(c1 - 1) // self.piece + 1]


CAST_ROT = ("act", "dve", "act", "dve", "pool")
WUP_CAST = ("act", "pool", "dve", "act", "pool", "dve", "act", "pool")


def load_w_bf16(cx, stager, dst, src, k_chunks, col0, ncols, maxc=1024, order=None):
    wb = WBufs(ncols, maxc)
    pieces = list(range(0, ncols, maxc))
    if order is not None:
        pieces = [pieces[i] for i in order]
    n = 0
    for c0 in pieces:
        c1 = min(ncols, c0 + maxc)
        for k in range(k_chunks):
            stager.load(cx, dst[:, k, c0:c1], src[k * 128:(k + 1) * 128, col0 + c0:col0 + c1], wb.get(c0, c1)[0],
                        eng=CAST_ROT[n % len(CAST_ROT)])
            n += 1
    return wb


def stage_prologue(cx, consts, banks, x_in, xT_out, tail_out, out_bufs=None):
    with contextlib.ExitStack() as es:
        cx.es = es
        xr = [cx.sb([128, D], F32, "pxr") for _ in range(2)]
        xb = [cx.sb([128, D], BF16, "pxb") for _ in range(2)]
        xts = [cx.sb([128, 8, 128], BF16, "pxts") for _ in range(2)]
        tl, b_tl = cx.sb([128, 8, 16], BF16, "ptl")
        for tile in range(16):
            k = tile % 2
            cx.dma("sp", xr[k][0][:], x_in[tile * 128:(tile + 1) * 128, :], writes=[xr[k][1]])
            cx.copy("dve", xb[k][0][:], xr[k][0][:], [xr[k][1]], [xb[k][1]])
            for c in range(8):
                cx.tr(banks.tT[:, c * 128:(c + 1) * 128], xb[k][0][:, c * 128:(c + 1) * 128], consts.ident[:],
                      [xb[k][1], consts.b_ident], [banks.bT])
            cx.copy("act", xts[k][0][:].rearrange("p c t -> p (c t)"), banks.tT[:, :], [banks.bT], [xts[k][1]])
            o = cx.dma("sp", xT_out.rearrange("(c p) t -> p c t", p=128)[:, :, tile * 128:(tile + 1) * 128],
                       xts[k][0][:], reads=[xts[k][1]], writes=out_bufs or ())
            cx.outs.append(o)
            if tile % 2 == 1:
                blk = tile // 2
                cx.copy("pool", tl[:, :, 2 * blk:2 * blk + 2], xts[k][0][:, :, 126:128], [xts[k][1]], [b_tl])
        o = cx.dma("sp", tail_out.rearrange("(c p) t -> p c t", p=128), tl[:], reads=[b_tl],
                   writes=out_bufs or (), nonc=True)
        cx.outs.append(o)
        cx.P.barrier()
    cx.es = None


def stage_gmlp(cx, consts, banks, din, j, li, xres_in, xT_in, xres_out, xT_out, tail_out,
               in_bufs=(), out_bufs=None):
    w_in = din["a_w_in"][j]
    w_out = din["a_w_out"][j]
    with contextlib.ExitStack() as es:
        cx.es = es
        xT, b_xT = cx.sb([128, 8, NT], BF16, "xT")
        for c in range(8):
            cx.dma("sp", xT[:, c, :], xT_in[c * 128:(c + 1) * 128, :], reads=in_bufs, writes=[b_xT])
        stager = Stager(cx)
        win, _ = cx.sb([128, 8, 2048], BF16, "win")
        wb_win = load_w_bf16(cx, stager, win, w_in, 8, 0, 2048)
        wout, _ = cx.sb([128, 8, 1024], BF16, "wout")
        wb_wout = load_w_bf16(cx, stager, wout, w_out, 8, 0, 1024)
        def spatial_setup():
            wsn, b_wsn = cx.sb([128, 8, 128], BF16, "wsn")
            cx.dma("pool", wsn[:], din["a_w_s"][j].rearrange("g t s -> t g s"), writes=[b_wsn])
            tril, b_tril = cx.sb([128, 8, 128], BF16, "tril")
            cx.dma("sp", tril[:], din["tril"], writes=[b_tril])
            cx.tt("pool", wsn[:], wsn[:], tril[:], ALU.mult, [b_wsn, b_tril], [b_wsn])
            wmT, b_wmT = cx.sb([128, 8, 128], BF16, "wmT")
            for g in range(8):
                cx.tr(banks.tT[:, g * 128:(g + 1) * 128], wsn[:, g, :], consts.ident[:], [b_wsn, consts.b_ident],
                      [banks.bT])
            cx.copy("act", wmT[:].rearrange("p g t -> p (g t)"), banks.tT[:, :], [banks.bT], [b_wmT])
            lbb, b_lbb = cx.sb([128, 1024], BF16, "lbb")
            cx.dma("pool", lbb[:], bcast_rows(din["a_ln_b"][j:j + 1, :]), writes=[b_lbb])
            bsb, b_bsb = cx.sb([128, 8, 128], F32, "bsb")
            cx.dma("sp", bsb[:].rearrange("p g t -> p (g t)"),
                   bcast_rows(din["a_b_s"][j:j + 1].rearrange("o g t -> o (g t)")), writes=[b_bsb])
            Bt, b_Bt = cx.sb([128, 8, 128], F32, "Bt")
            for c in range(8):
                bank = banks.t[c // 4]
                cx.mm(bank[:, (c % 4) * 128:(c % 4 + 1) * 128], lbb[:, c * 128:(c + 1) * 128], wmT[:, c, :],
                      True, True, [b_lbb, b_wmT], [banks.b[c // 4]])
            for hf in range(2):
                cx.tt("dve", Bt[:, hf * 4:(hf + 1) * 4, :].rearrange("p g t -> p (g t)"), banks.t[hf][:, :],
                      bsb[:, hf * 4:(hf + 1) * 4, :].rearrange("p g t -> p (g t)"), ALU.add,
                      [banks.b[hf], b_bsb], [b_Bt])
            gcol, b_gcol = cx.sb([128, 8], F32, "gcol")
            cx.dma("sp", gcol[:], din["a_ln_g"][j].rearrange("(c p) -> p c", p=128), writes=[b_gcol], nonc=True)
            return wmT, b_wmT, Bt, b_Bt, gcol, b_gcol

        epi = Epi(cx, consts, banks, din["ln_mix_g"][li:li + 1, :], din["ln_mix_b"][li:li + 1, :])
        uT, b_uT = cx.sb([128, 8, 512], F32, "uT")
        vg = [cx.sb([128, D], F32, "vg") for _ in range(2)]
        vn = [cx.sb([128, D], BF16, "vn") for _ in range(2)]
        t1 = [cx.sb([128, 8, 128], F32, "t1") for _ in range(2)]
        zT = [cx.sb([128, 8, 128], BF16, "zT") for _ in range(2)]
        st = [cx.sb([128, 2, 6], F32, "gst") for _ in range(2)]
        mv = [cx.sb([128, 2], F32, "gmv") for _ in range(2)]
        rstd = [cx.sb([128, 1], F32, "grstd") for _ in range(2)]
        nmr = [cx.sb([128, 1], F32, "gnmr") for _ in range(2)]
        def u_phase(tg):
            for c in range(8):
                bank, bb = banks.t[6], banks.b[6]
                for k in range(8):
                    cx.mm(bank[:, :], win[:, k, c * 128:(c + 1) * 128], xT[:, k, tg * 512:(tg + 1) * 512],
                          k == 0, k == 7, wb_win.get(c * 128, (c + 1) * 128) + [b_xT], [bb])
                cx.act(uT[:, c, :], bank[:, :], AF.Gelu_apprx_tanh, [bb], [b_uT])

        def part_a(tile):
            k2 = tile % 2
            tcols = slice(tile * 128, (tile + 1) * 128)
            for hf in range(2):
                for k in range(8):
                    cx.mm(banks.t[2 + hf][:, :], xT[:, k, tcols], win[:, k, 1024 + hf * 512:1024 + (hf + 1) * 512],
                          k == 0, k == 7, [b_xT] + wb_win.get(1024 + hf * 512, 1024 + (hf + 1) * 512), [banks.b[2 + hf]])
            vgt, b_vg = vg[k2]
            vnt, b_vn = vn[k2]
            for hf in range(2):
                cx.act(vgt[:, hf * 512:(hf + 1) * 512], banks.t[2 + hf][:, :], AF.Gelu_apprx_tanh,
                       [banks.b[2 + hf]], [b_vg])
            stt_, b_st = st[k2]
            mvt, b_mv = mv[k2]
            rs, b_rs = rstd[k2]
            cx.P.op("dve", lambda e, a=stt_, b=vgt: e.bn_stats(out=a[:, 0, :], in_=b[:, 0:512]),
                    reads=[b_vg], writes=[b_st])
            cx.P.op("dve", lambda e, a=stt_, b=vgt: e.bn_stats(out=a[:, 1, :], in_=b[:, 512:1024]),
                    reads=[b_vg], writes=[b_st])
            cx.P.op("dve", lambda e, a=mvt, b=stt_: e.bn_aggr(out=a[:], in_=b[:].rearrange("p a b -> p (a b)")),
                    reads=[b_st], writes=[b_mv])
            rstd_op(cx, consts, rs[:], b_rs, mvt[:, 1:2], b_mv)
            nm, b_nm = nmr[k2]
            cx.stt(nm[:], mvt[:, 0:1], -1.0, rs[:, 0:1], ALU.mult, ALU.mult, [b_mv, b_rs], [b_nm])
            cx.act(vnt[:], vgt[:], AF.Identity, [b_vg, b_rs, b_nm], [b_vn], bias=nm[:, 0:1], scale=rs[:, 0:1])

        def part_b(tile):
            k2 = tile % 2
            tt_ = tile % 4
            vnt, b_vn = vn[k2]
            for c in range(8):
                cx.mm(banks.t[4 + c // 4][:, (c % 4) * 128:(c % 4 + 1) * 128], vnt[:, c * 128:(c + 1) * 128],
                      wmT[:, c, :], True, True, [b_vn, b_wmT], [banks.b[4 + c // 4]])
            t1t, b_t1 = t1[k2]
            zt, b_z = zT[k2]
            for c in range(8):
                cx.stt(t1t[:, c, :], banks.t[4 + c // 4][:, (c % 4) * 128:(c % 4 + 1) * 128], gcol[:, c:c + 1],
                       Bt[:, c, :], ALU.mult, ALU.add, [banks.b[4 + c // 4], b_gcol, b_Bt], [b_t1])
            cx.tt("dve", zt[:], t1t[:], uT[:, :, tt_ * 128:(tt_ + 1) * 128], ALU.mult, [b_t1, b_uT], [b_z])

        def part_c(tile):
            k2 = tile % 2
            zt, b_z = zT[k2]
            yb = 0
            for hf in range(2):
                for c in range(8):
                    cx.mm(banks.t[yb + hf][:, :], zt[:, c, :], wout[:, c, hf * 512:(hf + 1) * 512],
                          c == 0, c == 7, [b_z] + wb_wout.get(hf * 512, (hf + 1) * 512), [banks.b[yb + hf]])
            epi.run(tile, banks.t[yb][:, :], banks.t[yb + 1][:, :], [banks.b[yb], banks.b[yb + 1]],
                    xres_in, xres_out, xT_out, tail_out, out_bufs, next_tile=(tile + 1 if tile + 1 < 16 else None))

        u_phase(0)
        part_a(0)
        wmT, b_wmT, Bt, b_Bt, gcol, b_gcol = spatial_setup()
        for tile in range(16):
            part_b(tile)
            if tile + 1 < 16:
                if (tile + 1) % 4 == 0:
                    u_phase((tile + 1) // 4)
                part_a(tile + 1)
            part_c(tile)
        epi.flush()
        cx.P.barrier()
    cx.es = None


def stage_ffn(cx, consts, banks, din, li, xres_in, xT_in, halo_fn, xres_out, xT_out, tail_out,
              in_bufs=(), out_bufs=None):
    w_up = din["f_w_up"][li]
    w_down = din["f_w_down"][li]
    last = xT_out is None
    with contextlib.ExitStack() as es:
        cx.es = es
        big, b_wd = cx.sb([128, 22 * 1024], BF16, "wd")
        wd = big[:, :].rearrange("p (f n) -> p f n", f=22)
        xte, b_xte = cx.sb([128, 8, 4, 258], BF16, "xte")
        hbuf, b_h = cx.sb([128, 22, 1024], BF16, "hbuf")
        halo, b_halo = cx.sb([128, 8, 16], BF16, "halo")
        halo_fn(cx, halo, b_halo)
        cpar, b_cpar = cx.sb([44, 4, 128], F32, "cpar")
        for jt in range(3):
            cx.dma("sp", cpar[:, jt, :], din["f_conv_w"][li, jt].rearrange("(c p) -> c p", p=128), writes=[b_cpar])
        cx.dma("sp", cpar[:, 3, :], din["f_conv_b"][li].rearrange("(c p) -> c p", p=128), writes=[b_cpar])
        id32, b_id32 = cx.sb([44, 44], F32, "id32")
        cx.dma("sp", id32[:], din["ident32"][0:44, 0:44], writes=[b_id32])
        cwT, b_cw = cx.sb([128, 4, 44], F32, "cwT")
        b_cb = b_cw
        for jt in range(4):
            cx.tr(banks.t[0][:, jt * 44:(jt + 1) * 44], cpar[:, jt, :], id32[:], [b_cpar, b_id32], [banks.b[0]])
        cx.copy("dve", cwT[:].rearrange("p j c -> p (j c)"), banks.t[0][:, 0:176], [banks.b[0]], [b_cw])
        epi = Epi(cx, consts, banks, din["ln_ffn_g"][li:li + 1, :], din["ln_ffn_b"][li:li + 1, :])
        wup = [cx.sb([128, 8, 2, 256], BF16, "wup") for _ in range(2)]
        stager = Stager(cx)
        tmp = [[cx.sb([128, 2, 256], F32, "ct") for _ in range(2)] for _ in range(3)]
        wd_loaded = False
        nk = 0
        tail_fn = None
        for hf in range(2):
            for k in range(8):
                cx.dma("sp", xte[:, k, :, 2:258],
                       xT_in[k * 128:(k + 1) * 128, hf * 1024:(hf + 1) * 1024].rearrange("p (b t) -> p b t", b=4),
                       reads=in_bufs, writes=[b_xte])
            cx.copy("pool", xte[:, :, :, 0:2],
                    halo[:, :, hf * 8:(hf + 1) * 8].rearrange("p k (b t) -> p k b t", b=4), [b_halo], [b_xte])
            def wup_src(fg_, k):
                return w_up[k * 128:(k + 1) * 128, :].rearrange("p (g f) -> p g f", g=2)[:, :, fg_ * 256:(fg_ + 1) * 256]

            def load_wup(fg_, hf_):
                wt_, b_w_ = wup[(hf_ * 11 + fg_) % 2]
                for k in range(8):
                    stager.load(cx, wt_[:, k, :, :], wup_src(fg_, k), b_w_, shape2=(2, 256), eng=WUP_CAST[k])
                if hf_ == 0:
                    for f in (2 * fg_, 2 * fg_ + 1):
                        stager.load(cx, wd[:, f, :], w_down[f * 128:(f + 1) * 128, :], b_wd, eng="act")

            class Pref:
                def __init__(self, fg_, hf_, with_wd):
                    self.fg_, self.hf_ = fg_, hf_
                    self.wt_, self.b_w_ = wup[(hf_ * 11 + fg_) % 2]
                    self.st = {}
                    self.wd = [2 * fg_, 2 * fg_ + 1] if with_wd else []

                def dma(self, k):
                    self.st[k] = stager.dma(cx, wup_src(self.fg_, k), shape2=(2, 256))

                def cast(self, k):
                    view, b = self.st.pop(k)
                    cx.copy(WUP_CAST[k], self.wt_[:, k, :, :], view, [b], [self.b_w_])

                def wd_dma(self, i):
                    f = self.wd[i]
                    self.st[("wd", i)] = stager.dma(cx, w_down[f * 128:(f + 1) * 128, :], n=1024)

                def wd_cast(self, i):
                    f = self.wd[i]
                    view, b = self.st.pop(("wd", i))
                    cx.copy("act", wd[:, f, :], view, [b], [b_wd])

                def step(self, g):
                    if g == -1:
                        for k in range(4):
                            self.dma(k)
                    elif g == 0:
                        for k in (0, 1, 2):
                            self.cast(k)
                        for k in (4, 5, 6):
                            self.dma(k)
                    elif g == 1:
                        for k in (3, 4, 5):
                            self.cast(k)
                        self.dma(7)
                        if self.wd:
                            self.wd_dma(0)
                    elif g == 2:
                        for k in (6, 7):
                            self.cast(k)
                        if self.wd:
                            self.wd_cast(0)
                            self.wd_dma(1)
                    elif g == 3:
                        if self.wd:
                            self.wd_cast(1)

            if hf == 0:
                load_wup(0, 0)
            for fg in range(11):
                wt, b_w = wup[(hf * 11 + fg) % 2]
                pref = None
                if fg + 1 < 11:
                    pref = Pref(fg + 1, hf, hf == 0)
                elif hf == 0:
                    pref = Pref(0, 1, False)
                if pref is not None:
                    pref.step(-1)
                gi = 0
                for f2 in range(2):
                    fc = fg * 2 + f2
                    for bp in range(2):
                        kk = nk % 2
                        nk += 1
                        pg, bpg = banks.pair(2 * kk)
                        pv, bpv = banks.pair(2 * kk + 1)
                        for bl in range(2):
                            for k in range(8):
                                cx.mm(pg[:, bl, 0:258], wt[:, k, 0, f2 * 128:(f2 + 1) * 128], xte[:, k, 2 * bp + bl, :],
                                      k == 0, k == 7, [b_w, b_xte], [bpg[bl]])
                        for bl in range(2):
                            for k in range(8):
                                cx.mm(pv[:, bl, 0:258], wt[:, k, 1, f2 * 128:(f2 + 1) * 128], xte[:, k, 2 * bp + bl, :],
                                      k == 0, k == 7, [b_w, b_xte], [bpv[bl]])
                        (g0, bg0), (v0, bv0) = tmp[nk % 3]
                        cg = fc
                        cv = 22 + fc
                        cx.act(g0[:], pg[:, :, 0:256], AF.Identity, bpg + [b_cw, b_cb], [bg0],
                               bias=cwT[:, 3, cg:cg + 1], scale=cwT[:, 0, cg:cg + 1])
                        cx.act(v0[:], pv[:, :, 0:256], AF.Identity, bpv + [b_cw, b_cb], [bv0],
                               bias=cwT[:, 3, cv:cv + 1], scale=cwT[:, 0, cv:cv + 1])
                        if tail_fn is not None:
                            tail_fn()
                        cx.stt(g0[:], pg[:, :, 1:257], cwT[:, 1, cg:cg + 1], g0[:], ALU.mult, ALU.add, bpg + [b_cw, bg0], [bg0])
                        cx.stt(v0[:], pv[:, :, 1:257], cwT[:, 1, cv:cv + 1], v0[:], ALU.mult, ALU.add, bpv + [b_cw, bv0], [bv0])
                        cx.stt(g0[:], pg[:, :, 2:258], cwT[:, 2, cg:cg + 1], g0[:], ALU.mult, ALU.add, bpg + [b_cw, bg0], [bg0])
                        cx.stt(v0[:], pv[:, :, 2:258], cwT[:, 2, cv:cv + 1], v0[:], ALU.mult, ALU.add, bpv + [b_cw, bv0], [bv0])

                        def tail_fn(g0=g0, bg0=bg0, v0=v0, bv0=bv0, fc=fc, bp=bp):
                            cx.act(g0[:], g0[:], AF.Gelu_apprx_tanh, [bg0], [bg0])
                            cx.tt("pool", hbuf[:, fc, bp * 512:(bp + 1) * 512].rearrange("p (b t) -> p b t", b=2),
                                  g0[:], v0[:], ALU.mult, [bg0, bv0], [b_h])
                        if pref is not None:
                            pref.step(gi)
                        gi += 1
            tail_fn()
            tail_fn = None
            for tl_ in range(8):
                tile = hf * 8 + tl_
                yb = 2 * (tile % 2)
                for h2 in range(2):
                    for fc in range(22):
                        cx.mm(banks.t[yb + h2][:, :], hbuf[:, fc, tl_ * 128:(tl_ + 1) * 128],
                              wd[:, fc, h2 * 512:(h2 + 1) * 512], fc == 0, fc == 21, [b_h, b_wd], [banks.b[yb + h2]])
                epi.run(tile, banks.t[yb][:, :], banks.t[yb + 1][:, :], [banks.b[yb], banks.b[yb + 1]],
                        xres_in, xres_out, xT_out, tail_out, out_bufs, next_tile=(tile + 1 if tl_ + 1 < 8 else None))
        epi.flush()
        cx.P.barrier()
    cx.es = None


def stage_attproj(cx, consts, banks, din, j, xT_in, qT_out, kT_out, v_out, kbar_out, in_bufs=(), out_bufs=None):
    wqkv = din["b_w_qkv"][j]
    with contextlib.ExitStack() as es:
        cx.es = es
        xT, b_xT = cx.sb([128, 8, NT], BF16, "xT")
        for c in range(8):
            cx.dma("sp", xT[:, c, :], xT_in[c * 128:(c + 1) * 128, :], reads=in_bufs, writes=[b_xT])
        stager = Stager(cx)
        w, _ = cx.sb([128, 8, 3072], BF16, "wqkv")
        wb_w = load_w_bf16(cx, stager, w, wqkv, 8, 0, 3072, maxc=512)
        stg = [cx.sb([128, 512], BF16, "stg") for _ in range(4)]
        kbss = [cx.sb([128, 8], F32, "kbs") for _ in range(2)]
        n = 0
        for fch in range(16):
            dst = qT_out if fch < 8 else kT_out
            r0 = (fch % 8) * 128
            for tg in range(4):
                kk = n % 2
                bank, bb = banks.t[kk], banks.b[kk]
                for k in range(8):
                    cx.mm(bank[:, :], w[:, k, fch * 128:(fch + 1) * 128], xT[:, k, tg * 512:(tg + 1) * 512],
                          k == 0, k == 7, wb_w.get(fch * 128, (fch + 1) * 128) + [b_xT], [bb])
                st_, b_st = stg[n % 4]
                if fch >= 8:
                    kbs, b_kbs = kbss[fch % 2]
                    cx.P.op("dve", lambda e, a=kbs, b=bank, t=tg: e.tensor_reduce(
                        out=a[:, 2 * t:2 * t + 2], in_=b[:, :].rearrange("p (b t) -> p b t", b=2),
                        axis=AX.X, op=ALU.add), reads=[bb], writes=[b_kbs])
                    if tg == 3:
                        cx.ts("dve", kbs[:], kbs[:], 1.0 / 256.0, None, ALU.mult, None, [b_kbs], [b_kbs])
                        o = cx.dma("sp", kbar_out[r0:r0 + 128, :], kbs[:], reads=[b_kbs], writes=out_bufs or ())
                        cx.outs.append(o)
                if n % 2 == 0 or fch >= 8:
                    cx.act(st_[:], bank[:, :], AF.Copy, [bb] + ([kbss[fch % 2][1]] if fch >= 8 else []), [b_st],
                           scale=(0.125 if fch < 8 else 1.0))
                else:
                    cx.ts("dve", st_[:], bank[:, :], (0.125 if fch < 8 else 1.0), None, ALU.mult, None, [bb], [b_st])
                o = cx.dma("sp", dst[r0:r0 + 128, tg * 512:(tg + 1) * 512], st_[:], reads=[b_st],
                           writes=out_bufs or ())
                cx.outs.append(o)
                n += 1
        for tile in range(16):
            for hf in range(2):
                kk = n % 2
                bank, bb = banks.t[kk], banks.b[kk]
                for k in range(8):
                    cx.mm(bank[:, :], xT[:, k, tile * 128:(tile + 1) * 128],
                          w[:, k, 2048 + hf * 512:2048 + (hf + 1) * 512], k == 0, k == 7,
                          [b_xT] + wb_w.get(2048 + hf * 512, 2048 + (hf + 1) * 512), [bb])
                st_, b_st = stg[n % 4]
                if n % 2 == 0:
                    cx.act(st_[:], bank[:, :], AF.Copy, [bb], [b_st])
                else:
                    cx.copy("dve", st_[:], bank[:, :], [bb], [b_st])
                o = cx.dma("sp", v_out[tile * 128:(tile + 1) * 128, hf * 512:(hf + 1) * 512], st_[:],
                           reads=[b_st], writes=out_bufs or ())
                cx.outs.append(o)
                n += 1
        cx.P.barrier()
    cx.es = None


def stage_attcore(cx, consts, banks, din, j, li, xres_in, qT_in, kT_all, v_all, kbar_all, bvd, xres_out, xT_out,
                  tail_out, in_bufs=(), out_bufs=None, qsel=(0, 1)):
    QW = 128 * len(qsel)
    q0 = qsel[0] * 128
    wo_d = din["b_w_o"][j]
    with contextlib.ExitStack() as es:
        cx.es = es
        rb, b_rb = cx.sb([33, 16], F32, "rb")
        cx.dma("sp", rb[0:32, :], din["rel_bias"], writes=[b_rb])
        cx.dma("sp", rb[32:33, :], din["ones16"], writes=[b_rb])
        oh, b_oh = cx.sb([33, 2048], F32, "oh")
        cx.dma("sp", oh[:], din["oh"], writes=[b_oh])
        bvs, b_bvs = cx.sb([16, 2048], BF16, "bvs")
        for hf in range(4):
            cx.mm(banks.t[hf][0:16, :], rb[:, :], oh[:, hf * 512:(hf + 1) * 512], True, True, [b_rb, b_oh],
                  [banks.b[hf]])
            cx.copy("dve", bvs[:, hf * 512:(hf + 1) * 512], banks.t[hf][0:16, :], [banks.b[hf]], [b_bvs])
        b_bvd = Buf("bvd")
        cx.dma("sp", bvd.ap(), bvs[:], reads=[b_bvs], writes=[b_bvd])
        chm, b_chm = cx.sb([128, 16], F32, "chm")
        cx.dma("sp", chm[:], bcast_rows(din["rel_bias"][31:32, :]), writes=[b_chm])
        gmask, b_gm = cx.sb([128, 16, 16], F32, "gmask")
        oof, b_oof = cx.sb([128, 16, 16], F32, "oof")
        farm, b_farm = cx.sb([128, 16, 16], F32, "farm")
        for i in range(8):
            for q in range(2):
                cx.dma("sp", gmask[:, 2 * i + q, :], din["gmask"][:, i, :], writes=[b_gm])
                cx.dma("sp", oof[:, 2 * i + q, :], din["oof"][:, i, :], writes=[b_oof])
        cx.memset("pool", farm[:], 0.0, [b_farm])
        for i in range(2, 8):
            cx.memset("pool", farm[:, 2 * i:2 * i + 2, 0:2 * i - 2], 1.0, [b_farm])
        jm, b_jm = cx.sb([128, 128], BF16, "jm")
        cx.dma("sp", jm[:], din["jm"], writes=[b_jm])
        stager = Stager(cx)
        wo, _ = cx.sb([128, 8, 1024], BF16, "wo")
        osb, b_osb = cx.sb([128, 16, D], BF16, "osb")
        kta = [cx.sb([80, 4096], BF16, "kta") for _ in range(2)]
        va = [cx.sb([128, 32, 65], BF16, "va") for _ in range(2)]
        qta = [cx.sb([80, NT], BF16, "qta") for _ in range(2)]
        tp = [cx.sb([128, 8, 256], BF16, "tp") for _ in range(2)]
        for k in range(2):
            cx.dma("sp", kta[k][0][64:80, :], din["koh"], writes=[kta[k][1]])
            cx.memset("pool", va[k][0][:, :, 64:65], 1.0, [va[k][1]])
        kb32s = [cx.sb([64, 16], F32, "kb32") for _ in range(2)]
        kbar = [cx.sb([64, 16], BF16, "kbar") for _ in range(2)]
        mpad, b_mp = cx.sb([128, 16, 80], BF16, "mpad")
        cx.memset("pool", mpad[:], 0.0, [b_mp])
        gm, b_g = cx.sb([128, 16, 16], F32, "gm")
        top8, b_t8 = cx.sb([128, 16, 8], F32, "top8")
        keep, b_kp = cx.sb([128, 16, 16], F32, "keep")
        ebuf = [cx.sb([128, 512], BF16, "ebuf") for _ in range(4)]
        rden = [cx.sb([128, 1], F32, "rden") for _ in range(4)]
        epi = Epi(cx, consts, banks, din["ln_mix_g"][li:li + 1, :], din["ln_mix_b"][li:li + 1, :])
        g7 = banks.t[7]

        def loads(h):
            hk = h % 2
            kt_, b_kt = kta[hk]
            va_, b_va = va[hk]
            qt_, b_qt = qta[hk]
            tp_, b_tp = tp[hk]
            for r in range(2):
                cx.dma("sp", kt_[0:64, :].rearrange("d (i r t) -> d i r t", i=8, r=2)[:, :, r, :],
                       kT_all[r, h * 64:(h + 1) * 64, :].rearrange("d (i t) -> d i t", i=8),
                       reads=in_bufs, writes=[b_kt])
            cx.dma("sp", qt_[0:64, :], qT_in[h * 64:(h + 1) * 64, :], reads=in_bufs, writes=[b_qt])
            kb32_, b_kb32_ = kb32s[hk]
            for r in range(2):
                cx.dma("sp", kb32_[:, :].rearrange("d (i r) -> d i r", r=2)[:, :, r], kbar_all[r, h * 64:(h + 1) * 64, :],
                       reads=in_bufs, writes=[b_kb32_], nonc=True)
            for r in range(2):
                for s_ in range(2):
                    cx.dma("sp", va_[:, :, 0:64].rearrange("p (i r s) d -> p i r s d", i=8, r=2)[:, :, r, s_, :],
                           v_all[r, :, h * 64:(h + 1) * 64].rearrange("(i s p) d -> p i s d", i=8, s=2)[:, :, s_, :],
                           reads=in_bufs, writes=[b_va], nonc=True)
            for rel in (-2, -1, 0, 1):
                for kt in range(2):
                    m0 = (rel + 2) * 512 + 128 * (1 - kt)
                    src = bass.AP(tensor=bvd, offset=h * 2048 + m0, ap=[[1, 128], [1, 256]])
                    cx.dma("sp", tp_[:, (rel + 2) * 2 + kt, :], src, reads=[b_bvd], writes=[b_tp])

        def gate1(h):
            hk = h % 2
            kt_, b_kt = kta[hk]
            qt_, b_qt = qta[hk]
            kbt, b_kb = kbar[hk]
            kb32_, b_kb32_ = kb32s[hk]
            cx.copy("dve", kbt[:], kb32_[:], [b_kb32_], [b_kb])

        def gate1b(h):
            hk = h % 2
            qt_, b_qt = qta[hk]
            kbt, b_kb = kbar[hk]
            for qc in range(16):
                cx.mm(g7[:, qc * 16:(qc + 1) * 16], qt_[0:64, qc * 128:(qc + 1) * 128], kbt[:, :], True, True,
                      [b_qt, b_kb], [banks.b[7]])
            cx.tt("dve", gm[:], g7[:, 0:256].rearrange("p (c n) -> p c n", c=16), gmask[:], ALU.add,
                  [banks.b[7], b_gm], [b_g])
            for qc in range(16):
                cx.P.op("dve", lambda e, c=qc: e.max(out=top8[:, c, :], in_=gm[:, c, :]), reads=[b_g], writes=[b_t8])
            cx.tt("dve", keep[:], gm[:], top8[:, :, 2:3].to_broadcast([128, 16, 16]), ALU.is_ge, [b_g, b_t8], [b_kp])
            cx.tt("dve", keep[:], keep[:], oof[:], ALU.max, [b_kp, b_oof], [b_kp])
            cx.ts("dve", keep[:], keep[:], -NEG, NEG, ALU.mult, ALU.add, [b_kp], [b_kp])
            cx.stt(mpad[:, :, 64:80], farm[:], chm[:, h:h + 1], keep[:], ALU.mult, ALU.add,
                   [b_farm, b_chm, b_kp], [b_mp])

        def gate2(h):
            hk = h % 2
            qt_, b_qt = qta[hk]
            for half in range(2):
                for c in range(8):
                    qc = half * 8 + c
                    cx.tr(banks.tT[0:80, c * 128:(c + 1) * 128], mpad[:, qc, :], consts.ident[:],
                          [b_mp, consts.b_ident], [banks.bT])
                cx.copy("dve", qt_[64:80, half * 1024:(half + 1) * 1024], banks.tT[64:80, :], [banks.bT], [b_qt])

        def main(h, hooks):
            hk = h % 2
            kt_, b_kt = kta[hk]
            va_, b_va = va[hk]
            qt_, b_qt = qta[hk]
            tp_, b_tp = tp[hk]
            slots = [(i, jb) for i in range(NBLK) for jb in range(2 * i + 2)]
            L = 2
            ebs = {}
            for t in range(len(slots) + L):
                if t < len(slots):
                    i, jb = slots[t]
                    if jb == 0 and i in hooks:
                        hooks[i]()
                    rel = jb - 2 * i
                    near = rel >= -2
                    sbk, bsb_ = banks.t[t % 3], banks.b[t % 3]
                    for kt in range(2):
                        gk = 2 * jb + kt
                        cx.mm(sbk[:, kt * QW:(kt + 1) * QW], kt_[0:80, gk * 128:(gk + 1) * 128],
                              qt_[0:80, i * 256 + q0:i * 256 + q0 + QW], True, not near, [b_kt, b_qt], [bsb_])
                        if near:
                            cx.mm(sbk[:, kt * QW:(kt + 1) * QW], jm[:, :],
                                  tp_[:, (rel + 2) * 2 + kt, q0:q0 + QW], False, True, [b_jm, b_tp], [bsb_])
                    eb, b_eb = ebuf[t % 4]
                    cx.act(eb[:, 0:2 * QW], sbk[:, 0:2 * QW], AF.Exp, [bsb_], [b_eb])
                    ebs[t] = (eb, b_eb)
                if t - L >= 0:
                    i, jb = slots[t - L]
                    eb, b_eb = ebs.pop(t - L)
                    nj = 2 * i + 2
                    ob = [(banks.t[3 + 2 * (i % 2) + q], banks.b[3 + 2 * (i % 2) + q]) for q in range(2)]
                    for q in qsel:
                        for kt in range(2):
                            gk = 2 * jb + kt
                            qo = (q - qsel[0]) * 128
                            cx.mm(ob[q][0][:, 0:65], eb[:, kt * QW + qo:kt * QW + qo + 128],
                                  va_[:, gk, :], jb == 0 and kt == 0, jb == nj - 1 and kt == 1,
                                  [b_eb, b_va], [ob[q][1]])
                    if jb == nj - 1:
                        for q in qsel:
                            rd, b_rd = rden[2 * (i % 2) + q]
                            cx.P.op("dve", lambda e, a=rd, b=ob[q][0]: e.reciprocal(out=a[:], in_=b[:, 64:65]),
                                    reads=[ob[q][1]], writes=[b_rd])
                            cx.ts("dve", osb[:, 2 * i + q, h * 64:(h + 1) * 64], ob[q][0][:, 0:64], rd[:, 0:1], None,
                                  ALU.mult, None, [ob[q][1], b_rd], [b_osb])

        loads(0)
        gate1(0)
        gate1b(0)
        gate2(0)
        wb_wo = load_w_bf16(cx, stager, wo, wo_d, 8, 0, 1024)
        for h in range(H):
            hooks = {}
            if h + 1 < H:
                loads(h + 1)
                hooks[3] = (lambda hh=h + 1: (gate1(hh), gate1b(hh)))
                hooks[6] = (lambda hh=h + 1: gate2(hh))
            main(h, hooks)
        ot = [cx.sb([128, 8, 128], BF16, "ot") for _ in range(3)]
        tiles = [t for t in range(16) if t % 2 in qsel]

        def prep(n):
            tile = tiles[n]
            otl, b_ot = ot[n % 3]
            for c in range(8):
                cx.tr(banks.tT[:, c * 128:(c + 1) * 128], osb[:, tile, c * 128:(c + 1) * 128], consts.ident[:],
                      [b_osb, consts.b_ident], [banks.bT])
            cx.copy("act", otl[:].rearrange("p c t -> p (c t)"), banks.tT[:, :], [banks.bT], [b_ot])

        prep(0)
        for n, tile in enumerate(tiles):
            otl, b_ot = ot[n % 3]
            yb = 2 * (n % 2)
            for hf in range(2):
                for c in range(8):
                    cx.mm(banks.t[yb + hf][:, :], otl[:, c, :], wo[:, c, hf * 512:(hf + 1) * 512], c == 0, c == 7,
                          [b_ot] + wb_wo.get(hf * 512, (hf + 1) * 512), [banks.b[yb + hf]])
            if n + 1 < len(tiles):
                prep(n + 1)
            epi.run(tile, banks.t[yb][:, :], banks.t[yb + 1][:, :], [banks.b[yb], banks.b[yb + 1]],
                    xres_in, xres_out, xT_out, tail_out, out_bufs,
                    next_tile=(tiles[n + 1] if n + 1 < len(tiles) else None))
        epi.flush()
        cx.P.barrier()
    cx.es = None


PARAM_SHAPES = {
    "ln_mix_g": (DEPTH, D), "ln_mix_b": (DEPTH, D), "ln_ffn_g": (DEPTH, D), "ln_ffn_b": (DEPTH, D),
    "a_w_in": (2, D, 2048), "a_ln_g": (2, D), "a_ln_b": (2, D), "a_w_s": (2, 8, 128, 128),
    "a_b_s": (2, 8, 128), "a_w_out": (2, D, D), "b_w_qkv": (2, D, 3072), "b_w_o": (2, D, D),
    "rel_bias": (32, 16), "f_w_up": (DEPTH, D, 2 * FF), "f_conv_w": (DEPTH, 3, 2 * FF),
    "f_conv_b": (DEPTH, 2 * FF), "f_w_down": (DEPTH, FF, D),
}
CONST_SHAPES = {
    "ident": ((128, 128), BF16), "jm": ((128, 128), BF16), "tril": ((128, 8, 128), BF16),
    "gmask": ((128, 8, 16), F32), "oof": ((128, 8, 16), F32), "oh": ((33, 2048), F32),
    "koh": ((16, 4096), BF16), "ones16": ((1, 16), F32), "ident32": ((128, 128), F32),
}
STAGE_PARAMS = {
    "pro": ["ident"],
    "gmlp": ["ln_mix_g", "ln_mix_b", "a_w_in", "a_ln_g", "a_ln_b", "a_w_s", "a_b_s", "a_w_out", "ident", "tril"],
    "ffn": ["ln_ffn_g", "ln_ffn_b", "f_w_up", "f_conv_w", "f_conv_b", "f_w_down", "ident", "ident32"],
    "attproj": ["b_w_qkv", "ident"],
    "attcore": ["ln_mix_g", "ln_mix_b", "b_w_o", "rel_bias", "ident", "jm", "gmask", "oof", "oh", "koh", "ones16"],
}


def declare(nc, names, single_layer):
    din = {}
    for n in names:
        if n in PARAM_SHAPES:
            shp = list(PARAM_SHAPES[n])
            if single_layer and n != "rel_bias":
                shp[0] = 1
            din[n] = nc.dram_tensor(n, shp, F32, kind="ExternalInput").ap()
        else:
            shp, dt = CONST_SHAPES[n]
            din[n] = nc.dram_tensor(n, list(shp), dt, kind="ExternalInput").ap()
    return din


def rel_bucket_np(dist):
    n = np.maximum(dist, 0)
    nf = np.maximum(n, 1).astype(np.float32)
    large = 16 + (np.log(nf / np.float32(16)) / np.float32(math.log(8)) * np.float32(16)).astype(np.int32)
    large = np.minimum(large, 31)
    return np.where(n < 16, n, large)


def host_consts(hA, hX):
    c = {}
    c["ident"] = np.eye(128, dtype=np.float32).astype(NPBF)
    c["jm"] = np.eye(128, dtype=np.float32)[::-1].copy().astype(NPBF)
    tr = np.tril(np.ones((128, 128), np.float32))
    c["tril"] = np.ascontiguousarray(np.broadcast_to(tr[:, None, :], (128, 8, 128))).astype(NPBF)
    hs = (hA, 1 - hA)
    gslot = np.array([2 * (s_ // 2) + hs[s_ % 2] for s_ in range(16)])
    gm = np.zeros((8, 16), np.float32)
    oo = np.zeros((8, 16), np.float32)
    for i in range(8):
        G = 2 * i + hX
        gm[i, gslot >= G] = -1e30
        oo[i, gslot >= G] = 1.0
    c["gmask"] = np.ascontiguousarray(np.broadcast_to(gm[None], (128, 8, 16)))
    c["oof"] = np.ascontiguousarray(np.broadcast_to(oo[None], (128, 8, 16)))
    oh = np.zeros((33, 2048), np.float32)
    m = np.arange(512)
    for ri, rel in enumerate((-2, -1, 0, 1)):
        sig = rel % 2
        bd = (2 if rel < 0 else 0) + hX - hs[sig]
        dist = m - 255 + 256 * bd
        bk = rel_bucket_np(dist)
        ok = dist >= 0
        oh[bk[ok], ri * 512 + m[ok]] = 1.0
        oh[32, ri * 512 + m[~ok]] = NEG
    c["oh"] = oh
    koh = np.zeros((16, 4096), np.float32)
    for n in range(16):
        koh[n, n * 256:(n + 1) * 256] = 1.0
    c["koh"] = koh.astype(NPBF)
    c["ones16"] = np.ones((1, 16), np.float32)
    c["ident32"] = np.eye(128, dtype=np.float32)
    fl = np.zeros((128, 2), np.float32)
    fl[:, hX] = 1.0
    c["hflag"] = fl
    return c


_PROGS = {}


def build_unfused(kind):
    if kind in _PROGS:
        return _PROGS[kind]
    nc = bass.Bass("TRN2", target_bir_lowering=False)
    din = declare(nc, STAGE_PARAMS[kind], True)

    def ext_in(name, shape, dt):
        return nc.dram_tensor(name, list(shape), dt, kind="ExternalInput").ap()

    def ext_out(name, shape, dt):
        return nc.dram_tensor(name, list(shape), dt, kind="ExternalOutput").ap()

    cx = Cx(nc)
    with contextlib.ExitStack() as top:
        cx.es = top
        consts = load_consts(cx, din)
        banks = Banks(cx)
        if kind == "pro":
            x_in = ext_in("xres_in", [NT, D], F32)
            stage_prologue(cx, consts, banks, x_in, ext_out("xT_out", [D, NT], BF16),
                           ext_out("tail_out", [D, 16], BF16))
        elif kind == "gmlp":
            stage_gmlp(cx, consts, banks, din, 0, 0, ext_in("xres_in", [NT, D], F32),
                       ext_in("xT_in", [D, NT], BF16), ext_out("xres_out", [NT, D], F32),
                       ext_out("xT_out", [D, NT], BF16), ext_out("tail_out", [D, 16], BF16))
        elif kind == "ffn":
            halo_in = ext_in("halo_in", [D, 16], BF16)

            def halo_fn(cx_, halo, b_halo):
                cx_.dma("sp", halo[:], halo_in.rearrange("(k p) t -> p k t", p=128), writes=[b_halo], nonc=True)

            stage_ffn(cx, consts, banks, din, 0, ext_in("xres_in", [NT, D], F32), ext_in("xT_in", [D, NT], BF16),
                      halo_fn, ext_out("xres_out", [NT, D], F32), ext_out("xT_out", [D, NT], BF16),
                      ext_out("tail_out", [D, 16], BF16))
        elif kind == "attproj":
            stage_attproj(cx, consts, banks, din, 0, ext_in("xT_in", [D, NT], BF16),
                          ext_out("qT_out", [D, NT], BF16), ext_out("kT_out", [D, NT], BF16),
                          ext_out("v_out", [NT, D], BF16), ext_out("kbar_out", [D, 8], F32))
        elif kind == "attcore":
            bvd = nc.dram_tensor("bvd", [16, 2048], BF16, kind="Internal")
            stage_attcore(cx, consts, banks, din, 0, 0, ext_in("xres_in", [NT, D], F32),
                          ext_in("qT_in", [D, NT], BF16), ext_in("kT_all", [2, D, NT], BF16),
                          ext_in("v_all", [2, NT, D], BF16), ext_in("kbar_all", [2, D, 8], F32), bvd,
                          ext_out("xres_out", [NT, D], F32),
                          ext_out("xT_out", [D, NT], BF16), ext_out("tail_out", [D, 16], BF16))
        cx.P.finish(cx.outs)
        cx.P.emit_all()
    _PROGS[kind] = nc
    return nc


def run_stage(kind, in_maps, cores):
    nc = build_unfused(kind)
    res = run_bass_kernel_spmd(nc, in_maps, core_ids=list(range(len(cores))))
    return res.results


LAYER_PARAM_IDX = {
    "gmlp": lambda li: {"ln_mix_g": li, "ln_mix_b": li, "a_w_in": li // 2, "a_ln_g": li // 2, "a_ln_b": li // 2,
                        "a_w_s": li // 2, "a_b_s": li // 2, "a_w_out": li // 2},
    "ffn": lambda li: {"ln_ffn_g": li, "ln_ffn_b": li, "f_w_up": li, "f_conv_w": li, "f_conv_b": li,
                       "f_w_down": li},
    "attproj": lambda li: {"b_w_qkv": li // 2},
    "attcore": lambda li: {"ln_mix_g": li, "ln_mix_b": li, "b_w_o": li // 2},
}


def stage_inputs(kind, li, params, consts_c):
    m = {}
    idx = LAYER_PARAM_IDX.get(kind, lambda li: {})(li)
    for n in STAGE_PARAMS[kind]:
        if n in idx:
            m[n] = np.ascontiguousarray(params[n][idx[n]:idx[n] + 1])
        elif n == "rel_bias":
            m[n] = params[n]
        else:
            m[n] = consts_c[n]
    return m


def to_local(x):
    B = x.shape[0]
    xb = x.reshape(B, 16, 256, D)
    return [np.ascontiguousarray(xb[c // 2, (c % 2)::2].reshape(NT, D)) for c in range(2 * B)]


def from_local(outs, B):
    y = np.zeros((B, 16, 256, D), np.float32)
    for c in range(2 * B):
        y[c // 2, (c % 2)::2] = outs[c].reshape(8, 256, D)
    return y.reshape(B, 4096, D)


def make_halo(tails, c):
    half = c % 2
    pt = tails[c ^ 1]
    halo = np.zeros((D, 16), NPBF)
    if half == 0:
        halo[:, 2:16] = pt[:, 0:14]
    else:
        halo[:, :] = pt
    return halo


def forward_unfused(x, params, ncores=8, nlayers=DEPTH, debug=None):
    cores = list(range(ncores))
    cc = [host_consts(c % 2, c % 2) for c in cores]
    xl = to_local(np.asarray(x, np.float32)[: ncores // 2])
    r = run_stage("pro", [dict(xres_in=xl[c], **stage_inputs("pro", 0, params, cc[c])) for c in cores], cores)
    xres = xl
    xT = [r[c]["xT_out"] for c in cores]
    tails = [r[c]["tail_out"] for c in cores]
    for li in range(nlayers):
        if li % 2 == 0:
            r = run_stage("gmlp", [dict(xres_in=xres[c], xT_in=xT[c], **stage_inputs("gmlp", li, params, cc[c]))
                                   for c in cores], cores)
        else:
            r = run_stage("attproj", [dict(xT_in=xT[c], **stage_inputs("attproj", li, params, cc[c]))
                                      for c in cores], cores)
            ims = []
            for c in cores:
                p0, p1 = c, c ^ 1
                ims.append(dict(xres_in=xres[c], qT_in=r[c]["qT_out"],
                                kT_all=np.stack([r[p0]["kT_out"], r[p1]["kT_out"]]),
                                v_all=np.stack([r[p0]["v_out"], r[p1]["v_out"]]),
                                kbar_all=np.stack([r[p0]["kbar_out"], r[p1]["kbar_out"]]),
                                **stage_inputs("attcore", li, params, cc[c])))
            r = run_stage("attcore", ims, cores)
        xres = [r[c]["xres_out"] for c in cores]
        xT = [r[c]["xT_out"] for c in cores]
        tails = [r[c]["tail_out"] for c in cores]
        if debug is not None:
            debug.append(("mix%d" % li, from_local(xres, ncores // 2)))
        r = run_stage("ffn", [dict(xres_in=xres[c], xT_in=xT[c], halo_in=make_halo(tails, c),
                                   **stage_inputs("ffn", li, params, cc[c])) for c in cores], cores)
        xres = [r[c]["xres_out"] for c in cores]
        xT = [r[c]["xT_out"] for c in cores]
        tails = [r[c]["tail_out"] for c in cores]
        if debug is not None:
            debug.append(("ffn%d" % li, from_local(xres, ncores // 2)))
    return from_local(xres, ncores // 2)


PASS_TABLES = ["gmask", "oof", "oh", "hflag"]
CONST_SHAPES["hflag"] = ((128, 2), F32)
COMMON_CONSTS = ["ident", "jm", "tril", "koh", "ones16", "ident32"]


def build_fused(nlayers=DEPTH):
    key = ("fused", nlayers)
    if key in _PROGS:
        return _PROGS[key]
    nc = bass.Bass("TRN2", target_bir_lowering=False)
    din = declare(nc, list(PARAM_SHAPES.keys()) + COMMON_CONSTS, False)
    dinx = {}
    for X in "AB":
        d = dict(din)
        for n in PASS_TABLES:
            shp, dt = CONST_SHAPES[n]
            d[n] = nc.dram_tensor("%s_%s" % (n, X), list(shp), dt, kind="ExternalInput").ap()
        dinx[X] = d

    def internal(name, shape, dt):
        return nc.dram_tensor(name, list(shape), dt, kind="Internal")

    x_in = {X: nc.dram_tensor("x_%s" % X, [NT, D], F32, kind="ExternalInput").ap() for X in "AB"}
    out = nc.dram_tensor("out", [NT, D], F32, kind="ExternalOutput").ap()
    xres = {X: [internal("xres_%s%d" % (X, k), [NT, D], F32).ap() for k in range(2)] for X in "AB"}
    xTb = {X: [internal("xT_%s%d" % (X, k), [D, NT], BF16).ap() for k in range(2)] for X in "AB"}
    tail = {X: internal("tail_%s" % X, [D, 16], BF16).ap() for X in "AB"}
    qT = {X: internal("qT_%s" % X, [D, NT], BF16).ap() for X in "AB"}
    kT_all = internal("kT_all", [2, D, NT], BF16).ap()
    v_all = internal("v_all", [2, NT, D], BF16).ap()
    kbar_all = internal("kbar_all", [2, D, 8], F32).ap()
    bvd = {X: internal("bvd_%s" % X, [16, 2048], BF16) for X in "AB"}
    sig = {"A": 0, "B": 1}
    other = {"A": "B", "B": "A"}
    cx = Cx(nc)
    with contextlib.ExitStack() as top:
        cx.es = top
        consts = load_consts(cx, din)
        banks = Banks(cx)
        cur = {}
        for X in "AB":
            stage_prologue(cx, consts, banks, x_in[X], xTb[X][0], tail[X])
            cur[X] = dict(xres=x_in[X], xT=xTb[X][0], k=0)

        def nxt(X, final=False):
            st = cur[X]
            k = st["k"]
            st["k"] ^= 1
            if final:
                return out, None
            return xres[X][k], xTb[X][k ^ 1]

        def halo_fn_for(X):
            def halo_fn(cx_, halo, b_halo):
                to, b_to = cx_.sb([128, 8, 16], BF16, "to")
                fl, b_fl = cx_.sb([128, 2], F32, "hfl")
                tmp, b_tmp = cx_.sb([128, 8, 16], F32, "htmp")
                cx_.dma("sp", to[:], tail[other[X]].rearrange("(k p) t -> p k t", p=128), writes=[b_to], nonc=True)
                cx_.dma("sp", fl[:], dinx[X]["hflag"], writes=[b_fl])
                cx_.ts("dve", tmp[:], to[:], fl[:, 1:2], None, ALU.mult, None, [b_to, b_fl], [b_tmp])
                cx_.copy("dve", halo[:, :, 0:2], tmp[:, :, 0:2], [b_tmp], [b_halo])
                cx_.stt(halo[:, :, 2:16], to[:, :, 0:14], fl[:, 0:1], tmp[:, :, 2:16], ALU.mult, ALU.add,
                        [b_to, b_fl, b_tmp], [b_halo])
            return halo_fn

        for li in range(nlayers):
            lastl = li == DEPTH - 1
            j = li // 2
            passes = "AB"
            if li % 2 == 0:
                for X in passes:
                    xo, xTo = nxt(X)
                    stage_gmlp(cx, consts, banks, dinx[X], j, li, cur[X]["xres"], cur[X]["xT"], xo, xTo, tail[X])
                    cur[X]["xres"], cur[X]["xT"] = xo, xTo
            else:
                for X in passes:
                    stage_attproj(cx, consts, banks, dinx[X], j, cur[X]["xT"], qT[X], kT_all[sig[X]],
                                  v_all[sig[X]], kbar_all[sig[X]])
                for X in "AB":
                    xo, xTo = nxt(X)
                    stage_attcore(cx, consts, banks, dinx[X], j, li, cur[X]["xres"], qT[X], kT_all, v_all, kbar_all,
                                  bvd[X], xo, xTo, tail[X], qsel=((1,) if (lastl and X == "B") else (0, 1)))
                    cur[X]["xres"], cur[X]["xT"] = xo, xTo
            for X in ("A" if lastl else "AB"):
                cx.outs = []
                xo, xTo = nxt(X, final=lastl)
                stage_ffn(cx, consts, banks, dinx[X], li, cur[X]["xres"], cur[X]["xT"], halo_fn_for(X), xo, xTo,
                          None)
                cur[X]["xres"], cur[X]["xT"] = xo, xTo
        if nlayers < DEPTH:
            cx.es = top
            t, bt = cx.sb([128, D], F32, "dbg")
            cx.outs = []
            for tile in range(16):
                cx.dma("sp", t[:], cur["A"]["xres"][tile * 128:(tile + 1) * 128, :], writes=[bt])
                cx.outs.append(cx.dma("sp", out[tile * 128:(tile + 1) * 128, :], t[:], reads=[bt]))
        cx.P.finish(cx.outs)
        cx.P.emit_all()
    _PROGS[key] = nc
    return nc


def forward_fused(x, params, ncores=8, nlayers=DEPTH):
    nc = build_fused(nlayers)
    xl = to_local(np.asarray(x, np.float32)[: ncores // 2])
    in_maps = []
    for c in range(ncores):
        hA = c % 2
        m = {k: params[k] for k in PARAM_SHAPES}
        ca = host_consts(hA, hA)
        cb = host_consts(hA, 1 - hA)
        for n in COMMON_CONSTS:
            m[n] = ca[n]
        for n in PASS_TABLES:
            m[n + "_A"] = ca[n]
            m[n + "_B"] = cb[n]
        m["x_A"] = xl[c]
        m["x_B"] = xl[c ^ 1]
        in_maps.append(m)
    res = run_bass_kernel_spmd(nc, in_maps, core_ids=list(range(ncores)))
    return from_local([res.results[c]["out"] for c in range(ncores)], ncores // 2)


def kernel(**inputs):
    params = {k: np.ascontiguousarray(np.asarray(v, np.float32)) for k, v in inputs.items() if k != "x"}
    x = np.asarray(inputs["x"], np.float32)
    return forward_fused(x, params).astype(np.float32)
```

```python
import contextlib
import math
import numpy as np
import ml_dtypes
import concourse.bass as bass
import concourse.mybir as mybir
from concourse.bass_utils import run_bass_kernel_spmd

F32 = mybir.dt.float32
BF16 = mybir.dt.bfloat16
AF = mybir.ActivationFunctionType
ALU = mybir.AluOpType
AX = mybir.AxisListType
NPBF = ml_dtypes.bfloat16

D = 1024
DEPTH = 4
NT = 2048
NBLK = 8
H = 16
DH = 64
FF = 2816
ALPHA = (2 * DEPTH) ** 0.25
EPS = 1e-5
NEG = -30000.0

ENGS = ("pe", "act", "dve", "pool", "sp")
SAME_ENGINE_SYNC = True
N_DMA_SLOTS = 20
STORE_Q = "pool"


class Buf:
    __slots__ = ("name", "w", "r")

    def __init__(self, name=""):
        self.name = name
        self.w = None
        self.r = []


class Op:
    __slots__ = ("eng", "emit", "deps", "dma", "slot", "slot_val", "prev_val",
                 "signal", "count", "epoch")

    def __init__(self, eng, emit, dma):
        self.eng = eng
        self.emit = emit
        self.deps = set()
        self.dma = dma
        self.slot = None
        self.slot_val = 0
        self.prev_val = 0
        self.signal = False
        self.count = 0
        self.epoch = 0


class Prog:
    def __init__(self, nc):
        self.nc = nc
        self.ops = {e: [] for e in ENGS}
        self.slot_rr = {e: 0 for e in ENGS}
        self.slot_cum = {}
        self.pending_dma = []
        self.epoch = 0

    def op(self, eng, emit, reads=(), writes=(), dma=False):
        o = Op(eng, emit, dma)
        o.epoch = self.epoch
        for b in reads:
            if b.w is not None and b.w is not o:
                o.deps.add(b.w)
            b.r.append(o)
        for b in writes:
            if b.w is not None and b.w is not o:
                o.deps.add(b.w)
            for r in b.r:
                if r is not o:
                    o.deps.add(r)
            b.w = o
            b.r = []
        if dma:
            k = self.slot_rr[eng]
            self.slot_rr[eng] = (k + 1) % N_DMA_SLOTS
            key = (eng, k)
            prev = self.slot_cum.get(key, 0)
            o.slot = key
            o.prev_val = prev
            o.slot_val = prev + 16
            self.slot_cum[key] = o.slot_val
            self.pending_dma.append(o)
        self.ops[eng].append(o)
        return o

    def barrier(self):
        lasts = []
        for e in ENGS:
            for o in reversed(self.ops[e]):
                if not o.dma and o.emit is not None:
                    lasts.append(o)
                    break
        pend = list(self.pending_dma)
        self.pending_dma = []
        for e in ENGS:
            o = Op(e, None, False)
            o.epoch = self.epoch
            o.deps.update(lasts)
            o.deps.update(pend)
            self.ops[e].append(o)
        self.nbar = getattr(self, "nbar", 0) + 1
        self.epoch = self.nbar // 3

    def finish(self, out_ops):
        o = Op("sp", None, False)
        o.epoch = self.epoch
        o.deps.update(out_ops)
        self.ops["sp"].append(o)

    def emit_all(self):
        nc = self.nc
        for e in ENGS:
            for o in self.ops[e]:
                for d in o.deps:
                    if d.dma:
                        continue
                    if d.eng == o.eng and (d.eng == "pe" or not SAME_ENGINE_SYNC):
                        continue
                    d.signal = True
        for e in ENGS:
            c = {}
            for o in self.ops[e]:
                if o.signal:
                    c[o.epoch] = c.get(o.epoch, 0) + 1
                o.count = c.get(o.epoch, 0)
        with contextlib.ExitStack() as es:
            esem = {}
            for e in ENGS:
                for ep in range(self.epoch + 1):
                    if any(o.signal and o.epoch == ep for o in self.ops[e]):
                        esem[(e, ep)] = es.enter_context(nc.semaphore("s_%s_%d" % (e, ep)))
            dsem = {}
            for key in self.slot_cum:
                dsem[key] = es.enter_context(nc.semaphore("d_%s_%d" % key))
            block = es.enter_context(nc.Block())

            def run(e, eng):
                waited = {}
                for o in self.ops[e]:
                    waits = {}
                    for d in o.deps:
                        if d.dma:
                            s, v = dsem[d.slot], d.slot_val
                            k = ("d",) + d.slot
                        else:
                            if d.eng == e and (e == "pe" or not SAME_ENGINE_SYNC):
                                continue
                            s, v = esem[(d.eng, d.epoch)], d.count
                            k = ("e", d.eng, d.epoch)
                        if waits.get(k, (None, 0))[1] < v:
                            waits[k] = (s, v)
                    if o.dma and o.prev_val > 0:
                        k = ("d",) + o.slot
                        if waits.get(k, (None, 0))[1] < o.prev_val:
                            waits[k] = (dsem[o.slot], o.prev_val)
                    for k, (s, v) in waits.items():
                        if waited.get(k, 0) >= v:
                            continue
                        waited[k] = v
                        eng.wait_ge(s, v)
                    if o.emit is None:
                        continue
                    ins = o.emit(eng)
                    if o.dma:
                        ins.then_inc(dsem[o.slot], 16)
                    elif o.signal:
                        ins.then_inc(esem[(e, o.epoch)], 1)

            @block.tensor
            def _(eng):
                run("pe", eng)

            @block.scalar
            def _(eng):
                run("act", eng)

            @block.vector
            def _(eng):
                run("dve", eng)

            @block.gpsimd
            def _(eng):
                run("pool", eng)

            @block.sync
            def _(eng):
                run("sp", eng)


class Cx:
    def __init__(self, nc):
        self.nc = nc
        self.P = Prog(nc)
        self.es = None
        self.uid = 0
        self.dq = 0
        self.outs = []

    def sb(self, shape, dt, name=None):
        self.uid += 1
        t = self.es.enter_context(self.nc.sbuf_tensor("%s_%d" % (name or "t", self.uid), list(shape), dt))
        return t, Buf(name or "t")

    def ps(self, shape, dt, name=None):
        self.uid += 1
        t = self.es.enter_context(self.nc.psum_tensor("%s_%d" % (name or "p", self.uid), list(shape), dt))
        return t, Buf(name or "p")

    def dram(self, name, shape, dt, kind):
        return self.nc.dram_tensor(name, list(shape), dt, kind=kind)

    def mm(self, out, lhsT, rhs, start, stop, reads, writes):
        self.P.op("pe", lambda e: e.matmul(out, lhsT=lhsT, rhs=rhs, start=start, stop=stop),
                  reads=reads, writes=writes)

    def tr(self, out, in_, ident, reads, writes):
        self.P.op("pe", lambda e: e.transpose(out=out, in_=in_, identity=ident), reads=reads, writes=writes)

    def act(self, out, in_, func, reads, writes, bias=None, scale=None):
        kw = {}
        if bias is not None:
            kw["bias"] = bias
        if scale is not None:
            kw["scale"] = scale
        self.P.op("act", lambda e: e.activation(out=out, in_=in_, func=func, **kw), reads=reads, writes=writes)

    def ts(self, eng, out, in0, s1, s2, op0, op1, reads, writes):
        if op1 is None:
            self.P.op(eng, lambda e: e.tensor_scalar(out=out, in0=in0, scalar1=s1, scalar2=None, op0=op0),
                      reads=reads, writes=writes)
        else:
            self.P.op(eng, lambda e: e.tensor_scalar(out=out, in0=in0, scalar1=s1, scalar2=s2, op0=op0, op1=op1),
                      reads=reads, writes=writes)

    def stt(self, out, in0, scalar, in1, op0, op1, reads, writes):
        self.P.op("dve", lambda e: e.scalar_tensor_tensor(out=out, in0=in0, scalar=scalar, in1=in1,
                                                          op0=op0, op1=op1), reads=reads, writes=writes)

    def tt(self, eng, out, in0, in1, op, reads, writes):
        self.P.op(eng, lambda e: e.tensor_tensor(out=out, in0=in0, in1=in1, op=op), reads=reads, writes=writes)

    def copy(self, eng, out, in_, reads, writes):
        if eng == "act":
            self.P.op("act", lambda e: e.copy(out=out, in_=in_), reads=reads, writes=writes)
        else:
            self.P.op(eng, lambda e: e.tensor_copy(out=out, in_=in_), reads=reads, writes=writes)

    def memset(self, eng, ap, val, writes):
        self.P.op(eng, lambda e: e.memset(ap, val), writes=writes)

    def dma(self, q, out, in_, reads=(), writes=(), nonc=False):
        if nonc:
            def em(e):
                with self.nc.allow_non_contiguous_dma(reason="small strided param load"):
                    return e.dma_start(out=out, in_=in_)
        else:
            def em(e):
                return e.dma_start(out=out, in_=in_)
        return self.P.op(q, em, reads=reads, writes=writes, dma=True)


def bcast_rows(ap2d_row, n=128):
    a = ap2d_row.partition_broadcast(n)
    if len(a.shape) == 3:
        a = a[:, 0, :]
    return a


class Consts:
    pass


def load_consts(cx, din):
    c = Consts()
    c.ident, c.b_ident = cx.sb([128, 128], BF16, "ident")
    cx.dma("sp", c.ident[:], din["ident"], writes=[c.b_ident])
    c.eps, c.b_eps = cx.sb([128, 1], F32, "eps")
    cx.memset("dve", c.eps[:], EPS, [c.b_eps])
    return c


def rstd_op(cx, consts, rstd, b_rstd, var_ap, b_var):
    cx.act(rstd, var_ap, AF.Sqrt, [b_var, consts.b_eps], [b_rstd], bias=consts.eps[:, 0:1])
    cx.P.op("dve", lambda e: e.reciprocal(out=rstd, in_=rstd), reads=[b_rstd], writes=[b_rstd])


class Banks:
    def __init__(self, cx):
        self.pb = []
        self.t = []
        self.b = []
        for i in range(4):
            t, _ = cx.ps([128, 1024], F32, "pb%d" % i)
            self.pb.append(t)
            for hf in range(2):
                self.t.append(t[:, hf * 512:(hf + 1) * 512])
                self.b.append(Buf("bank%d" % (2 * i + hf)))
        self.tT = self.pb[3].bitcast(BF16)[:, 1024:2048]
        self.bT = self.b[7]

    def pair(self, i):
        return self.pb[i][:, :].rearrange("p (b c) -> p b c", b=2), [self.b[2 * i], self.b[2 * i + 1]]


class Epi:
    def __init__(self, cx, consts, banks, lng_row, lnb_row):
        self.cx = cx
        self.consts = consts
        self.banks = banks
        self.lng, self.b_lng = cx.sb([128, D], F32, "lng")
        self.lnb, self.b_lnb = cx.sb([128, D], F32, "lnb")
        cx.dma("sp", self.lng[:], bcast_rows(lng_row), writes=[self.b_lng])
        cx.dma("sp", self.lnb[:], bcast_rows(lnb_row), writes=[self.b_lnb])
        self.xr = [cx.sb([128, D], F32, "xr") for _ in range(2)]
        self.s = [cx.sb([128, D], F32, "s")] * 2
        self.xn = [cx.sb([128, D], F32, "xn")] * 2
        self.xnb = [cx.sb([128, D], BF16, "xnb") for _ in range(2)]
        self.xts = [cx.sb([128, 8, 128], BF16, "xts") for _ in range(2)]
        self.st = [cx.sb([128, 2, 6], F32, "st") for _ in range(2)]
        self.mv = [cx.sb([128, 2], F32, "mv") for _ in range(2)]
        self.rstd = [cx.sb([128, 1], F32, "rstd") for _ in range(2)]
        self.nmr = [cx.sb([128, 1], F32, "nmr") for _ in range(2)]
        self.tl, self.b_tl = cx.sb([128, 8, 16], BF16, "tl")
        self.k = 0
        self.pending = None

    def prefetch(self, tile, xres_in):
        xr, b_xr = self.xr[self.k]
        self.cx.dma("sp", xr[:], xres_in[tile * 128:(tile + 1) * 128, :], writes=[b_xr])
        self.pre = tile

    def run(self, tile, y0, y1, by, xres_in, xres_out, xT_out, tail_out, out_bufs=None, next_tile=None):
        cx = self.cx
        k = self.k
        self.k ^= 1
        self.flush()
        xr, b_xr = self.xr[k]
        s, b_s = self.s[k]
        xn, b_xn = self.xn[k]
        xnb, b_xnb = self.xnb[k]
        xts, b_xts = self.xts[k]
        st, b_st = self.st[k]
        mv, b_mv = self.mv[k]
        rstd, b_rstd = self.rstd[k]
        rows = slice(tile * 128, (tile + 1) * 128)
        if getattr(self, "pre", None) != tile:
            cx.dma("sp", xr[:], xres_in[rows, :], writes=[b_xr])
        self.pre = None
        if next_tile is not None:
            self.prefetch(next_tile, xres_in)
        cx.stt(s[:, 0:512], xr[:, 0:512], ALPHA, y0, ALU.mult, ALU.add, [b_xr, by[0]], [b_s])
        cx.stt(s[:, 512:1024], xr[:, 512:1024], ALPHA, y1, ALU.mult, ALU.add, [b_xr, by[1]], [b_s])
        cx.P.op("dve", lambda e: e.bn_stats(out=st[:, 0, :], in_=s[:, 0:512]), reads=[b_s], writes=[b_st])
        cx.P.op("dve", lambda e: e.bn_stats(out=st[:, 1, :], in_=s[:, 512:1024]), reads=[b_s], writes=[b_st])
        cx.P.op("dve", lambda e: e.bn_aggr(out=mv[:], in_=st[:].rearrange("p a b -> p (a b)")),
                reads=[b_st], writes=[b_mv])
        rstd_op(cx, self.consts, rstd[:], b_rstd, mv[:, 1:2], b_mv)
        cx.stt(s[:], s[:], mv[:, 0:1], self.lng[:], ALU.subtract, ALU.mult, [b_s, b_mv, self.b_lng], [b_s])
        cx.stt(xn[:], s[:], rstd[:, 0:1], self.lnb[:], ALU.mult, ALU.add, [b_s, b_rstd, self.b_lnb], [b_xn])
        o = cx.dma(STORE_Q, xres_out[rows, :], xn[:], reads=[b_xn], writes=out_bufs or ())
        cx.outs.append(o)
        if xT_out is None:
            return
        cx.copy("act", xnb[:], xn[:], [b_xn], [b_xnb])
        self.pending = (tile, xnb, b_xnb, xts, b_xts, xT_out, tail_out, out_bufs)

    def flush(self):
        if self.pending is None:
            return
        cx = self.cx
        tile, xnb, b_xnb, xts, b_xts, xT_out, tail_out, out_bufs = self.pending
        self.pending = None
        bk = self.banks
        for c in range(8):
            cx.tr(bk.tT[:, c * 128:(c + 1) * 128], xnb[:, c * 128:(c + 1) * 128], self.consts.ident[:],
                  [b_xnb, self.consts.b_ident], [bk.bT])
        cx.copy("act", xts[:].rearrange("p c t -> p (c t)"), bk.tT[:, :], [bk.bT], [b_xts])
        o = cx.dma(STORE_Q, xT_out.rearrange("(c p) t -> p c t", p=128)[:, :, tile * 128:(tile + 1) * 128], xts[:],
                   reads=[b_xts], writes=out_bufs or ())
        cx.outs.append(o)
        if tile % 2 == 1:
            blk = tile // 2
            cx.copy("pool", self.tl[:, :, 2 * blk:2 * blk + 2], xts[:, :, 126:128], [b_xts], [self.b_tl])
        if tile == 15 and tail_out is not None:
            o = cx.dma("sp", tail_out.rearrange("(c p) t -> p c t", p=128), self.tl[:], reads=[self.b_tl],
                       writes=out_bufs or (), nonc=True)
            cx.outs.append(o)


class Stager:
    def __init__(self, cx, n=4, size=1024):
        self.bufs = [cx.sb([128, size], F32, "stg32") for _ in range(n)]
        self.k = 0
        self.size = size

    def load(self, cx, dst, src, b_dst, shape2=None, eng="pool"):
        t, b = self.bufs[self.k % len(self.bufs)]
        self.k += 1
        if shape2 is None:
            n = dst.shape[1]
            view = t[:, 0:n]
        else:
            a, bb = shape2
            view = t[:, 0:a * bb].rearrange("p (a b) -> p a b", a=a)
        cx.dma("sp", view, src, writes=[b])
        cx.copy(eng, dst, view, [b], [b_dst])

    def dma(self, cx, src, n=None, shape2=None):
        t, b = self.bufs[self.k % len(self.bufs)]
        self.k += 1
        if shape2 is None:
            view = t[:, 0:n]
        else:
            a, bb = shape2
            view = t[:, 0:a * bb].rearrange("p (a b) -> p a b", a=a)
        cx.dma("sp", view, src, writes=[b])
        return view, b


class WBufs:
    def __init__(self, ncols, piece):
        self.piece = piece
        self.bufs = [Buf("w") for _ in range((ncols + piece - 1) // piece)]

    def get(self, c0, c1):
        return self.bufs[c0 // self.piece:(c1 - 1) // self.piece + 1]


CAST_ROT = ("act", "dve", "act", "dve", "pool")
WUP_CAST = ("act", "pool", "dve", "act", "pool", "dve", "act", "pool")


def load_w_bf16(cx, stager, dst, src, k_chunks, col0, ncols, maxc=1024, order=None):
    wb = WBufs(ncols, maxc)
    pieces = list(range(0, ncols, maxc))
    if order is not None:
        pieces = [pieces[i] for i in order]
    n = 0
    for c0 in pieces:
        c1 = min(ncols, c0 + maxc)
        for k in range(k_chunks):
            stager.load(cx, dst[:, k, c0:c1], src[k * 128:(k + 1) * 128, col0 + c0:col0 + c1], wb.get(c0, c1)[0],
                        eng=CAST_ROT[n % len(CAST_ROT)])
            n += 1
    return wb


def stage_prologue(cx, consts, banks, x_in, xT_out, tail_out, out_bufs=None):
    with contextlib.ExitStack() as es:
        cx.es = es
        xr = [cx.sb([128, D], F32, "pxr") for _ in range(2)]
        xb = [cx.sb([128, D], BF16, "pxb") for _ in range(2)]
        xts = [cx.sb([128, 8, 128], BF16, "pxts") for _ in range(2)]
        tl, b_tl = cx.sb([128, 8, 16], BF16, "ptl")
        for tile in range(16):
            k = tile % 2
            cx.dma("sp", xr[k][0][:], x_in[tile * 128:(tile + 1) * 128, :], writes=[xr[k][1]])
            cx.copy("dve", xb[k][0][:], xr[k][0][:], [xr[k][1]], [xb[k][1]])
            for c in range(8):
                cx.tr(banks.tT[:, c * 128:(c + 1) * 128], xb[k][0][:, c * 128:(c + 1) * 128], consts.ident[:],
                      [xb[k][1], consts.b_ident], [banks.bT])
            cx.copy("act", xts[k][0][:].rearrange("p c t -> p (c t)"), banks.tT[:, :], [banks.bT], [xts[k][1]])
            o = cx.dma("sp", xT_out.rearrange("(c p) t -> p c t", p=128)[:, :, tile * 128:(tile + 1) * 128],
                       xts[k][0][:], reads=[xts[k][1]], writes=out_bufs or ())
            cx.outs.append(o)
            if tile % 2 == 1:
                blk = tile // 2
                cx.copy("pool", tl[:, :, 2 * blk:2 * blk + 2], xts[k][0][:, :, 126:128], [xts[k][1]], [b_tl])
        o = cx.dma("sp", tail_out.rearrange("(c p) t -> p c t", p=128), tl[:], reads=[b_tl],
                   writes=out_bufs or (), nonc=True)
        cx.outs.append(o)
        cx.P.barrier()
    cx.es = None


def stage_gmlp(cx, consts, banks, din, j, li, passes, in_bufs=(), out_bufs=None):
    w_in = din["a_w_in"][j]
    w_out = din["a_w_out"][j]
    with contextlib.ExitStack() as es:
        cx.es = es
        xT, b_xT = cx.sb([128, 8, NT], BF16, "xT")
        for c in range(8):
            cx.dma("sp", xT[:, c, :], passes[0][1][c * 128:(c + 1) * 128, :], reads=in_bufs, writes=[b_xT])
        stager = Stager(cx)
        win, _ = cx.sb([128, 8, 2048], BF16, "win")
        wb_win = load_w_bf16(cx, stager, win, w_in, 8, 0, 2048)
        wout, _ = cx.sb([128, 8, 1024], BF16, "wout")
        wb_wout = load_w_bf16(cx, stager, wout, w_out, 8, 0, 1024)
        def spatial_setup():
            wsn, b_wsn = cx.sb([128, 8, 128], BF16, "wsn")
            cx.dma("pool", wsn[:], din["a_w_s"][j].rearrange("g t s -> t g s"), writes=[b_wsn])
            tril, b_tril = cx.sb([128, 8, 128], BF16, "tril")
            cx.dma("sp", tril[:], din["tril"], writes=[b_tril])
            cx.tt("pool", wsn[:], wsn[:], tril[:], ALU.mult, [b_wsn, b_tril], [b_wsn])
            wmT, b_wmT = cx.sb([128, 8, 128], BF16, "wmT")
            for g in range(8):
                cx.tr(banks.tT[:, g * 128:(g + 1) * 128], wsn[:, g, :], consts.ident[:], [b_wsn, consts.b_ident],
                      [banks.bT])
            cx.copy("act", wmT[:].rearrange("p g t -> p (g t)"), banks.tT[:, :], [banks.bT], [b_wmT])
            lbb, b_lbb = cx.sb([128, 1024], BF16, "lbb")
            cx.dma("pool", lbb[:], bcast_rows(din["a_ln_b"][j:j + 1, :]), writes=[b_lbb])
            bsb, b_bsb = cx.sb([128, 8, 128], F32, "bsb")
            cx.dma("sp", bsb[:].rearrange("p g t -> p (g t)"),
                   bcast_rows(din["a_b_s"][j:j + 1].rearrange("o g t -> o (g t)")), writes=[b_bsb])
            Bt, b_Bt = cx.sb([128, 8, 128], F32, "Bt")
            for c in range(8):
                bank = banks.t[c // 4]
                cx.mm(bank[:, (c % 4) * 128:(c % 4 + 1) * 128], lbb[:, c * 128:(c + 1) * 128], wmT[:, c, :],
                      True, True, [b_lbb, b_wmT], [banks.b[c // 4]])
            for hf in range(2):
                cx.tt("dve", Bt[:, hf * 4:(hf + 1) * 4, :].rearrange("p g t -> p (g t)"), banks.t[hf][:, :],
                      bsb[:, hf * 4:(hf + 1) * 4, :].rearrange("p g t -> p (g t)"), ALU.add,
                      [banks.b[hf], b_bsb], [b_Bt])
            gcol, b_gcol = cx.sb([128, 8], F32, "gcol")
            cx.dma("sp", gcol[:], din["a_ln_g"][j].rearrange("(c p) -> p c", p=128), writes=[b_gcol], nonc=True)
            return wmT, b_wmT, Bt, b_Bt, gcol, b_gcol

        epi = Epi(cx, consts, banks, din["ln_mix_g"][li:li + 1, :], din["ln_mix_b"][li:li + 1, :])
        uT, b_uT = cx.sb([128, 8, 512], F32, "uT")
        vg = [cx.sb([128, D], F32, "vg") for _ in range(2)]
        vn = [cx.sb([128, D], BF16, "vn") for _ in range(2)]
        t1 = [cx.sb([128, 8, 128], F32, "t1") for _ in range(2)]
        zT = [cx.sb([128, 8, 128], BF16, "zT") for _ in range(2)]
        st = [cx.sb([128, 2, 6], F32, "gst") for _ in range(2)]
        mv = [cx.sb([128, 2], F32, "gmv") for _ in range(2)]
        rstd = [cx.sb([128, 1], F32, "grstd") for _ in range(2)]
        nmr = [cx.sb([128, 1], F32, "gnmr") for _ in range(2)]
        sp_state = []

        def run_pass(xres_in, xT_in, xres_out, xT_out, tail_out, first):
            if not first:
                for c in range(8):
                    cx.dma("sp", xT[:, c, :], xT_in[c * 128:(c + 1) * 128, :], reads=in_bufs, writes=[b_xT])
            def u_phase(tg):
                for c in range(8):
                    bank, bb = banks.t[6], banks.b[6]
                    for k in range(8):
                        cx.mm(bank[:, :], win[:, k, c * 128:(c + 1) * 128], xT[:, k, tg * 512:(tg + 1) * 512],
                              k == 0, k == 7, wb_win.get(c * 128, (c + 1) * 128) + [b_xT], [bb])
                    cx.act(uT[:, c, :], bank[:, :], AF.Gelu_apprx_tanh, [bb], [b_uT])

            def part_a(tile):
                k2 = tile % 2
                tcols = slice(tile * 128, (tile + 1) * 128)
                for hf in range(2):
                    for k in range(8):
                        cx.mm(banks.t[2 + hf][:, :], xT[:, k, tcols], win[:, k, 1024 + hf * 512:1024 + (hf + 1) * 512],
                              k == 0, k == 7, [b_xT] + wb_win.get(1024 + hf * 512, 1024 + (hf + 1) * 512), [banks.b[2 + hf]])
                vgt, b_vg = vg[k2]
                vnt, b_vn = vn[k2]
                for hf in range(2):
                    cx.act(vgt[:, hf * 512:(hf + 1) * 512], banks.t[2 + hf][:, :], AF.Gelu_apprx_tanh,
                           [banks.b[2 + hf]], [b_vg])
                stt_, b_st = st[k2]
                mvt, b_mv = mv[k2]
                rs, b_rs = rstd[k2]
                cx.P.op("dve", lambda e, a=stt_, b=vgt: e.bn_stats(out=a[:, 0, :], in_=b[:, 0:512]),
                        reads=[b_vg], writes=[b_st])
                cx.P.op("dve", lambda e, a=stt_, b=vgt: e.bn_stats(out=a[:, 1, :], in_=b[:, 512:1024]),
                        reads=[b_vg], writes=[b_st])
                cx.P.op("dve", lambda e, a=mvt, b=stt_: e.bn_aggr(out=a[:], in_=b[:].rearrange("p a b -> p (a b)")),
                        reads=[b_st], writes=[b_mv])
                rstd_op(cx, consts, rs[:], b_rs, mvt[:, 1:2], b_mv)
                nm, b_nm = nmr[k2]
                cx.stt(nm[:], mvt[:, 0:1], -1.0, rs[:, 0:1], ALU.mult, ALU.mult, [b_mv, b_rs], [b_nm])
                cx.act(vnt[:], vgt[:], AF.Identity, [b_vg, b_rs, b_nm], [b_vn], bias=nm[:, 0:1], scale=rs[:, 0:1])

            def part_b(tile):
                k2 = tile % 2
                tt_ = tile % 4
                vnt, b_vn = vn[k2]
                for c in range(8):
                    cx.mm(banks.t[4 + c // 4][:, (c % 4) * 128:(c % 4 + 1) * 128], vnt[:, c * 128:(c + 1) * 128],
                          wmT[:, c, :], True, True, [b_vn, b_wmT], [banks.b[4 + c // 4]])
                t1t, b_t1 = t1[k2]
                zt, b_z = zT[k2]
                for c in range(8):
                    cx.stt(t1t[:, c, :], banks.t[4 + c // 4][:, (c % 4) * 128:(c % 4 + 1) * 128], gcol[:, c:c + 1],
                           Bt[:, c, :], ALU.mult, ALU.add, [banks.b[4 + c // 4], b_gcol, b_Bt], [b_t1])
                cx.tt("dve", zt[:], t1t[:], uT[:, :, tt_ * 128:(tt_ + 1) * 128], ALU.mult, [b_t1, b_uT], [b_z])

            def part_c(tile):
                k2 = tile % 2
                zt, b_z = zT[k2]
                yb = 0
                for hf in range(2):
                    for c in range(8):
                        cx.mm(banks.t[yb + hf][:, :], zt[:, c, :], wout[:, c, hf * 512:(hf + 1) * 512],
                              c == 0, c == 7, [b_z] + wb_wout.get(hf * 512, (hf + 1) * 512), [banks.b[yb + hf]])
                epi.run(tile, banks.t[yb][:, :], banks.t[yb + 1][:, :], [banks.b[yb], banks.b[yb + 1]],
                        xres_in, xres_out, xT_out, tail_out, out_bufs, next_tile=(tile + 1 if tile + 1 < 16 else None))

            u_phase(0)
            part_a(0)
            if first:
                sp_state.extend(spatial_setup())
            wmT, b_wmT, Bt, b_Bt, gcol, b_gcol = sp_state
            for tile in range(16):
                part_b(tile)
                if tile + 1 < 16:
                    if (tile + 1) % 4 == 0:
                        u_phase((tile + 1) // 4)
                    part_a(tile + 1)
                part_c(tile)
            epi.flush()

        for pi, p_ in enumerate(passes):
            run_pass(*p_, first=(pi == 0))
        epi.flush()
        cx.P.barrier()
    cx.es = None


def stage_ffn(cx, consts, banks, din, li, xres_in, xT_in, halo_fn, xres_out, xT_out, tail_out,
              in_bufs=(), out_bufs=None):
    w_up = din["f_w_up"][li]
    w_down = din["f_w_down"][li]
    last = xT_out is None
    with contextlib.ExitStack() as es:
        cx.es = es
        big, b_wd = cx.sb([128, 22 * 1024], BF16, "wd")
        wd = big[:, :].rearrange("p (f n) -> p f n", f=22)
        xte, b_xte = cx.sb([128, 8, 4, 258], BF16, "xte")
        hbuf, b_h = cx.sb([128, 22, 1024], BF16, "hbuf")
        halo, b_halo = cx.sb([128, 8, 16], BF16, "halo")
        halo_fn(cx, halo, b_halo)
        cpar, b_cpar = cx.sb([44, 4, 128], F32, "cpar")
        for jt in range(3):
            cx.dma("sp", cpar[:, jt, :], din["f_conv_w"][li, jt].rearrange("(c p) -> c p", p=128), writes=[b_cpar])
        cx.dma("sp", cpar[:, 3, :], din["f_conv_b"][li].rearrange("(c p) -> c p", p=128), writes=[b_cpar])
        id32, b_id32 = cx.sb([44, 44], F32, "id32")
        cx.dma("sp", id32[:], din["ident32"][0:44, 0:44], writes=[b_id32])
        cwT, b_cw = cx.sb([128, 4, 44], F32, "cwT")
        b_cb = b_cw
        for jt in range(4):
            cx.tr(banks.t[0][:, jt * 44:(jt + 1) * 44], cpar[:, jt, :], id32[:], [b_cpar, b_id32], [banks.b[0]])
        cx.copy("dve", cwT[:].rearrange("p j c -> p (j c)"), banks.t[0][:, 0:176], [banks.b[0]], [b_cw])
        epi = Epi(cx, consts, banks, din["ln_ffn_g"][li:li + 1, :], din["ln_ffn_b"][li:li + 1, :])
        wup = [cx.sb([128, 8, 2, 256], BF16, "wup") for _ in range(2)]
        stager = Stager(cx)
        tmp = [[cx.sb([128, 2, 256], F32, "ct") for _ in range(2)] for _ in range(3)]
        wd_loaded = False
        nk = 0
        tail_fn = None
        for hf in range(2):
            for k in range(8):
                cx.dma("sp", xte[:, k, :, 2:258],
                       xT_in[k * 128:(k + 1) * 128, hf * 1024:(hf + 1) * 1024].rearrange("p (b t) -> p b t", b=4),
                       reads=in_bufs, writes=[b_xte])
            cx.copy("pool", xte[:, :, :, 0:2],
                    halo[:, :, hf * 8:(hf + 1) * 8].rearrange("p k (b t) -> p k b t", b=4), [b_halo], [b_xte])
            def wup_src(fg_, k):
                return w_up[k * 128:(k + 1) * 128, :].rearrange("p (g f) -> p g f", g=2)[:, :, fg_ * 256:(fg_ + 1) * 256]

            def load_wup(fg_, hf_):
                wt_, b_w_ = wup[(hf_ * 11 + fg_) % 2]
                for k in range(8):
                    stager.load(cx, wt_[:, k, :, :], wup_src(fg_, k), b_w_, shape2=(2, 256), eng=WUP_CAST[k])
                if hf_ == 0:
                    for f in (2 * fg_, 2 * fg_ + 1):
                        stager.load(cx, wd[:, f, :], w_down[f * 128:(f + 1) * 128, :], b_wd, eng="act")

            class Pref:
                def __init__(self, fg_, hf_, with_wd):
                    self.fg_, self.hf_ = fg_, hf_
                    self.wt_, self.b_w_ = wup[(hf_ * 11 + fg_) % 2]
                    self.st = {}
                    self.wd = [2 * fg_, 2 * fg_ + 1] if with_wd else []

                def dma(self, k):
                    self.st[k] = stager.dma(cx, wup_src(self.fg_, k), shape2=(2, 256))

                def cast(self, k):
                    view, b = self.st.pop(k)
                    cx.copy(WUP_CAST[k], self.wt_[:, k, :, :], view, [b], [self.b_w_])

                def wd_dma(self, i):
                    f = self.wd[i]
                    self.st[("wd", i)] = stager.dma(cx, w_down[f * 128:(f + 1) * 128, :], n=1024)

                def wd_cast(self, i):
                    f = self.wd[i]
                    view, b = self.st.pop(("wd", i))
                    cx.copy("act", wd[:, f, :], view, [b], [b_wd])

                def step(self, g):
                    if g == -1:
                        for k in range(4):
                            self.dma(k)
                    elif g == 0:
                        for k in (0, 1, 2):
                            self.cast(k)
                        for k in (4, 5, 6):
                            self.dma(k)
                    elif g == 1:
                        for k in (3, 4, 5):
                            self.cast(k)
                        self.dma(7)
                        if self.wd:
                            self.wd_dma(0)
                    elif g == 2:
                        for k in (6, 7):
                            self.cast(k)
                        if self.wd:
                            self.wd_cast(0)
                            self.wd_dma(1)
                    elif g == 3:
                        if self.wd:
                            self.wd_cast(1)

            if hf == 0:
                load_wup(0, 0)
            for fg in range(11):
                wt, b_w = wup[(hf * 11 + fg) % 2]
                pref = None
                if fg + 1 < 11:
                    pref = Pref(fg + 1, hf, hf == 0)
                elif hf == 0:
                    pref = Pref(0, 1, False)
                if pref is not None:
                    pref.step(-1)
                gi = 0
                for f2 in range(2):
                    fc = fg * 2 + f2
                    for bp in range(2):
                        kk = nk % 2
                        nk += 1
                        pg, bpg = banks.pair(2 * kk)
                        pv, bpv = banks.pair(2 * kk + 1)
                        for bl in range(2):
                            for k in range(8):
                                cx.mm(pg[:, bl, 0:258], wt[:, k, 0, f2 * 128:(f2 + 1) * 128], xte[:, k, 2 * bp + bl, :],
                                      k == 0, k == 7, [b_w, b_xte], [bpg[bl]])
                        for bl in range(2):
                            for k in range(8):
                                cx.mm(pv[:, bl, 0:258], wt[:, k, 1, f2 * 128:(f2 + 1) * 128], xte[:, k, 2 * bp + bl, :],
                                      k == 0, k == 7, [b_w, b_xte], [bpv[bl]])
                        (g0, bg0), (v0, bv0) = tmp[nk % 3]
                        cg = fc
                        cv = 22 + fc
                        cx.act(g0[:], pg[:, :, 0:256], AF.Identity, bpg + [b_cw, b_cb], [bg0],
                               bias=cwT[:, 3, cg:cg + 1], scale=cwT[:, 0, cg:cg + 1])
                        cx.act(v0[:], pv[:, :, 0:256], AF.Identity, bpv + [b_cw, b_cb], [bv0],
                               bias=cwT[:, 3, cv:cv + 1], scale=cwT[:, 0, cv:cv + 1])
                        if tail_fn is not None:
                            tail_fn()
                        cx.stt(g0[:], pg[:, :, 1:257], cwT[:, 1, cg:cg + 1], g0[:], ALU.mult, ALU.add, bpg + [b_cw, bg0], [bg0])
                        cx.stt(v0[:], pv[:, :, 1:257], cwT[:, 1, cv:cv + 1], v0[:], ALU.mult, ALU.add, bpv + [b_cw, bv0], [bv0])
                        cx.stt(g0[:], pg[:, :, 2:258], cwT[:, 2, cg:cg + 1], g0[:], ALU.mult, ALU.add, bpg + [b_cw, bg0], [bg0])
                        cx.stt(v0[:], pv[:, :, 2:258], cwT[:, 2, cv:cv + 1], v0[:], ALU.mult, ALU.add, bpv + [b_cw, bv0], [bv0])

                        def tail_fn(g0=g0, bg0=bg0, v0=v0, bv0=bv0, fc=fc, bp=bp):
                            cx.act(g0[:], g0[:], AF.Gelu_apprx_tanh, [bg0], [bg0])
                            cx.tt("pool", hbuf[:, fc, bp * 512:(bp + 1) * 512].rearrange("p (b t) -> p b t", b=2),
                                  g0[:], v0[:], ALU.mult, [bg0, bv0], [b_h])
                        if pref is not None:
                            pref.step(gi)
                        gi += 1
            tail_fn()
            tail_fn = None
            for tl_ in range(8):
                tile = hf * 8 + tl_
                yb = 2 * (tile % 2)
                for h2 in range(2):
                    for fc in range(22):
                        cx.mm(banks.t[yb + h2][:, :], hbuf[:, fc, tl_ * 128:(tl_ + 1) * 128],
                              wd[:, fc, h2 * 512:(h2 + 1) * 512], fc == 0, fc == 21, [b_h, b_wd], [banks.b[yb + h2]])
                epi.run(tile, banks.t[yb][:, :], banks.t[yb + 1][:, :], [banks.b[yb], banks.b[yb + 1]],
                        xres_in, xres_out, xT_out, tail_out, out_bufs, next_tile=(tile + 1 if tl_ + 1 < 8 else None))
        epi.flush()
        cx.P.barrier()
    cx.es = None


def stage_attproj(cx, consts, banks, din, j, passes, in_bufs=(), out_bufs=None):
    wqkv = din["b_w_qkv"][j]
    with contextlib.ExitStack() as es:
        cx.es = es
        xTs = []
        for pi, p_ in enumerate(passes):
            xT_, b_xT_ = cx.sb([128, 8, NT], BF16, "xT")
            xTs.append((xT_, b_xT_))
            if pi == 0:
                for c in range(8):
                    cx.dma("sp", xT_[:, c, :], p_[0][c * 128:(c + 1) * 128, :], reads=in_bufs, writes=[b_xT_])
        stager = Stager(cx)
        w, _ = cx.sb([128, 8, 3072], BF16, "wqkv")
        wb_w = load_w_bf16(cx, stager, w, wqkv, 8, 0, 3072, maxc=512)
        for pi, p_ in enumerate(passes):
            if pi > 0:
                for c in range(8):
                    cx.dma("sp", xTs[pi][0][:, c, :], p_[0][c * 128:(c + 1) * 128, :], reads=in_bufs,
                           writes=[xTs[pi][1]])
        stg = [cx.sb([128, 512], BF16, "stg") for _ in range(4)]
        kbss = [cx.sb([128, 8], F32, "kbs") for _ in range(2)]
        n = 0
        for pi, (xT_in, qT_out, kT_out, v_out, kbar_out) in enumerate(passes):
          xT, b_xT = xTs[pi]
          for fch in range(16):
              dst = qT_out if fch < 8 else kT_out
              r0 = (fch % 8) * 128
              for tg in range(4):
                  kk = n % 2
                  bank, bb = banks.t[kk], banks.b[kk]
                  for k in range(8):
                      cx.mm(bank[:, :], w[:, k, fch * 128:(fch + 1) * 128], xT[:, k, tg * 512:(tg + 1) * 512],
                            k == 0, k == 7, wb_w.get(fch * 128, (fch + 1) * 128) + [b_xT], [bb])
                  st_, b_st = stg[n % 4]
                  if fch >= 8:
                      kbs, b_kbs = kbss[fch % 2]
                      cx.P.op("dve", lambda e, a=kbs, b=bank, t=tg: e.tensor_reduce(
                          out=a[:, 2 * t:2 * t + 2], in_=b[:, :].rearrange("p (b t) -> p b t", b=2),
                          axis=AX.X, op=ALU.add), reads=[bb], writes=[b_kbs])
                      if tg == 3:
                          cx.ts("dve", kbs[:], kbs[:], 1.0 / 256.0, None, ALU.mult, None, [b_kbs], [b_kbs])
                          o = cx.dma("sp", kbar_out[r0:r0 + 128, :], kbs[:], reads=[b_kbs], writes=out_bufs or ())
                          cx.outs.append(o)
                  if n % 2 == 0 or fch >= 8:
                      cx.act(st_[:], bank[:, :], AF.Copy, [bb] + ([kbss[fch % 2][1]] if fch >= 8 else []), [b_st],
                             scale=(0.125 if fch < 8 else 1.0))
                  else:
                      cx.ts("dve", st_[:], bank[:, :], (0.125 if fch < 8 else 1.0), None, ALU.mult, None, [bb], [b_st])
                  o = cx.dma("sp", dst[r0:r0 + 128, tg * 512:(tg + 1) * 512], st_[:], reads=[b_st],
                             writes=out_bufs or ())
                  cx.outs.append(o)
                  n += 1
          for tile in range(16):
              for hf in range(2):
                  kk = n % 2
                  bank, bb = banks.t[kk], banks.b[kk]
                  for k in range(8):
                      cx.mm(bank[:, :], xT[:, k, tile * 128:(tile + 1) * 128],
                            w[:, k, 2048 + hf * 512:2048 + (hf + 1) * 512], k == 0, k == 7,
                            [b_xT] + wb_w.get(2048 + hf * 512, 2048 + (hf + 1) * 512), [bb])
                  st_, b_st = stg[n % 4]
                  if n % 2 == 0:
                      cx.act(st_[:], bank[:, :], AF.Copy, [bb], [b_st])
                  else:
                      cx.copy("dve", st_[:], bank[:, :], [bb], [b_st])
                  o = cx.dma("sp", v_out[tile * 128:(tile + 1) * 128, hf * 512:(hf + 1) * 512], st_[:],
                             reads=[b_st], writes=out_bufs or ())
                  cx.outs.append(o)
                  n += 1
        cx.P.barrier()
    cx.es = None


def stage_attcore(cx, consts, banks, din, j, li, xres_in, qT_in, kT_all, v_all, kbar_all, bvd, xres_out, xT_out,
                  tail_out, in_bufs=(), out_bufs=None, qsel=(0, 1)):
    QW = 128 * len(qsel)
    q0 = qsel[0] * 128
    wo_d = din["b_w_o"][j]
    with contextlib.ExitStack() as es:
        cx.es = es
        rb, b_rb = cx.sb([33, 16], F32, "rb")
        cx.dma("sp", rb[0:32, :], din["rel_bias"], writes=[b_rb])
        cx.dma("sp", rb[32:33, :], din["ones16"], writes=[b_rb])
        oh, b_oh = cx.sb([33, 2048], F32, "oh")
        cx.dma("sp", oh[:], din["oh"], writes=[b_oh])
        bvs, b_bvs = cx.sb([16, 2048], BF16, "bvs")
        for hf in range(4):
            cx.mm(banks.t[hf][0:16, :], rb[:, :], oh[:, hf * 512:(hf + 1) * 512], True, True, [b_rb, b_oh],
                  [banks.b[hf]])
            cx.copy("dve", bvs[:, hf * 512:(hf + 1) * 512], banks.t[hf][0:16, :], [banks.b[hf]], [b_bvs])
        b_bvd = Buf("bvd")
        cx.dma("sp", bvd.ap(), bvs[:], reads=[b_bvs], writes=[b_bvd])
        chm, b_chm = cx.sb([128, 16], F32, "chm")
        cx.dma("sp", chm[:], bcast_rows(din["rel_bias"][31:32, :]), writes=[b_chm])
        gmask, b_gm = cx.sb([128, 16, 16], F32, "gmask")
        oof, b_oof = cx.sb([128, 16, 16], F32, "oof")
        farm, b_farm = cx.sb([128, 16, 16], F32, "farm")
        for i in range(8):
            for q in range(2):
                cx.dma("sp", gmask[:, 2 * i + q, :], din["gmask"][:, i, :], writes=[b_gm])
                cx.dma("sp", oof[:, 2 * i + q, :], din["oof"][:, i, :], writes=[b_oof])
        cx.memset("pool", farm[:], 0.0, [b_farm])
        for i in range(2, 8):
            cx.memset("pool", farm[:, 2 * i:2 * i + 2, 0:2 * i - 2], 1.0, [b_farm])
        jm, b_jm = cx.sb([128, 128], BF16, "jm")
        cx.dma("sp", jm[:], din["jm"], writes=[b_jm])
        stager = Stager(cx)
        wo, _ = cx.sb([128, 8, 1024], BF16, "wo")
        osb, b_osb = cx.sb([128, 16, D], BF16, "osb")
        kta = [cx.sb([80, 4096], BF16, "kta") for _ in range(2)]
        va = [cx.sb([128, 32, 65], BF16, "va") for _ in range(2)]
        qta = [cx.sb([80, NT], BF16, "qta") for _ in range(2)]
        tp = [cx.sb([128, 8, 256], BF16, "tp") for _ in range(2)]
        for k in range(2):
            cx.dma("sp", kta[k][0][64:80, :], din["koh"], writes=[kta[k][1]])
            cx.memset("pool", va[k][0][:, :, 64:65], 1.0, [va[k][1]])
        kb32s = [cx.sb([64, 16], F32, "kb32") for _ in range(2)]
        kbar = [cx.sb([64, 16], BF16, "kbar") for _ in range(2)]
        mpad, b_mp = cx.sb([128, 16, 80], BF16, "mpad")
        cx.memset("pool", mpad[:], 0.0, [b_mp])
        gm, b_g = cx.sb([128, 16, 16], F32, "gm")
        top8, b_t8 = cx.sb([128, 16, 8], F32, "top8")
        keep, b_kp = cx.sb([128, 16, 16], F32, "keep")
        ebuf = [cx.sb([128, 512], BF16, "ebuf") for _ in range(4)]
        rden = [cx.sb([128, 1], F32, "rden") for _ in range(4)]
        epi = Epi(cx, consts, banks, din["ln_mix_g"][li:li + 1, :], din["ln_mix_b"][li:li + 1, :])
        g7 = banks.t[7]

        def loads(h):
            hk = h % 2
            kt_, b_kt = kta[hk]
            va_, b_va = va[hk]
            qt_, b_qt = qta[hk]
            tp_, b_tp = tp[hk]
            for r in range(2):
                cx.dma("sp", kt_[0:64, :].rearrange("d (i r t) -> d i r t", i=8, r=2)[:, :, r, :],
                       kT_all[r, h * 64:(h + 1) * 64, :].rearrange("d (i t) -> d i t", i=8),
                       reads=in_bufs, writes=[b_kt])
            cx.dma("sp", qt_[0:64, :], qT_in[h * 64:(h + 1) * 64, :], reads=in_bufs, writes=[b_qt])
            kb32_, b_kb32_ = kb32s[hk]
            for r in range(2):
                cx.dma("sp", kb32_[:, :].rearrange("d (i r) -> d i r", r=2)[:, :, r], kbar_all[r, h * 64:(h + 1) * 64, :],
                       reads=in_bufs, writes=[b_kb32_], nonc=True)
            for r in range(2):
                for s_ in range(2):
                    cx.dma("sp", va_[:, :, 0:64].rearrange("p (i r s) d -> p i r s d", i=8, r=2)[:, :, r, s_, :],
                           v_all[r, :, h * 64:(h + 1) * 64].rearrange("(i s p) d -> p i s d", i=8, s=2)[:, :, s_, :],
                           reads=in_bufs, writes=[b_va], nonc=True)
            for rel in (-2, -1, 0, 1):
                for kt in range(2):
                    m0 = (rel + 2) * 512 + 128 * (1 - kt)
                    src = bass.AP(tensor=bvd, offset=h * 2048 + m0, ap=[[1, 128], [1, 256]])
                    cx.dma("sp", tp_[:, (rel + 2) * 2 + kt, :], src, reads=[b_bvd], writes=[b_tp])

        def gate1(h):
            hk = h % 2
            kt_, b_kt = kta[hk]
            qt_, b_qt = qta[hk]
            kbt, b_kb = kbar[hk]
            kb32_, b_kb32_ = kb32s[hk]
            cx.copy("dve", kbt[:], kb32_[:], [b_kb32_], [b_kb])

        def gate1b(h):
            hk = h % 2
            qt_, b_qt = qta[hk]
            kbt, b_kb = kbar[hk]
            for qc in range(16):
                cx.mm(g7[:, qc * 16:(qc + 1) * 16], qt_[0:64, qc * 128:(qc + 1) * 128], kbt[:, :], True, True,
                      [b_qt, b_kb], [banks.b[7]])
            cx.tt("dve", gm[:], g7[:, 0:256].rearrange("p (c n) -> p c n", c=16), gmask[:], ALU.add,
                  [banks.b[7], b_gm], [b_g])
            for qc in range(16):
                cx.P.op("dve", lambda e, c=qc: e.max(out=top8[:, c, :], in_=gm[:, c, :]), reads=[b_g], writes=[b_t8])
            cx.tt("dve", keep[:], gm[:], top8[:, :, 2:3].to_broadcast([128, 16, 16]), ALU.is_ge, [b_g, b_t8], [b_kp])
            cx.tt("dve", keep[:], keep[:], oof[:], ALU.max, [b_kp, b_oof], [b_kp])
            cx.ts("dve", keep[:], keep[:], -NEG, NEG, ALU.mult, ALU.add, [b_kp], [b_kp])
            cx.stt(mpad[:, :, 64:80], farm[:], chm[:, h:h + 1], keep[:], ALU.mult, ALU.add,
                   [b_farm, b_chm, b_kp], [b_mp])

        def gate2(h):
            hk = h % 2
            qt_, b_qt = qta[hk]
            for half in range(2):
                for c in range(8):
                    qc = half * 8 + c
                    cx.tr(banks.tT[0:80, c * 128:(c + 1) * 128], mpad[:, qc, :], consts.ident[:],
                          [b_mp, consts.b_ident], [banks.bT])
                cx.copy("dve", qt_[64:80, half * 1024:(half + 1) * 1024], banks.tT[64:80, :], [banks.bT], [b_qt])

        def main(h, hooks):
            hk = h % 2
            kt_, b_kt = kta[hk]
            va_, b_va = va[hk]
            qt_, b_qt = qta[hk]
            tp_, b_tp = tp[hk]
            slots = [(i, jb) for i in range(NBLK) for jb in range(2 * i + 2)]
            L = 2
            ebs = {}
            for t in range(len(slots) + L):
                if t < len(slots):
                    i, jb = slots[t]
                    if jb == 0 and i in hooks:
                        hooks[i]()
                    rel = jb - 2 * i
                    near = rel >= -2
                    sbk, bsb_ = banks.t[t % 3], banks.b[t % 3]
                    for kt in range(2):
                        gk = 2 * jb + kt
                        cx.mm(sbk[:, kt * QW:(kt + 1) * QW], kt_[0:80, gk * 128:(gk + 1) * 128],
                              qt_[0:80, i * 256 + q0:i * 256 + q0 + QW], True, not near, [b_kt, b_qt], [bsb_])
                        if near:
                            cx.mm(sbk[:, kt * QW:(kt + 1) * QW], jm[:, :],
                                  tp_[:, (rel + 2) * 2 + kt, q0:q0 + QW], False, True, [b_jm, b_tp], [bsb_])
                    eb, b_eb = ebuf[t % 4]
                    cx.act(eb[:, 0:2 * QW], sbk[:, 0:2 * QW], AF.Exp, [bsb_], [b_eb])
                    ebs[t] = (eb, b_eb)
                if t - L >= 0:
                    i, jb = slots[t - L]
                    eb, b_eb = ebs.pop(t - L)
                    nj = 2 * i + 2
                    ob = [(banks.t[3 + 2 * (i % 2) + q], banks.b[3 + 2 * (i % 2) + q]) for q in range(2)]
                    for q in qsel:
                        for kt in range(2):
                            gk = 2 * jb + kt
                            qo = (q - qsel[0]) * 128
                            cx.mm(ob[q][0][:, 0:65], eb[:, kt * QW + qo:kt * QW + qo + 128],
                                  va_[:, gk, :], jb == 0 and kt == 0, jb == nj - 1 and kt == 1,
                                  [b_eb, b_va], [ob[q][1]])
                    if jb == nj - 1:
                        for q in qsel:
                            rd, b_rd = rden[2 * (i % 2) + q]
                            cx.P.op("dve", lambda e, a=rd, b=ob[q][0]: e.reciprocal(out=a[:], in_=b[:, 64:65]),
                                    reads=[ob[q][1]], writes=[b_rd])
                            cx.ts("dve", osb[:, 2 * i + q, h * 64:(h + 1) * 64], ob[q][0][:, 0:64], rd[:, 0:1], None,
                                  ALU.mult, None, [ob[q][1], b_rd], [b_osb])

        loads(0)
        gate1(0)
        gate1b(0)
        gate2(0)
        wb_wo = load_w_bf16(cx, stager, wo, wo_d, 8, 0, 1024)
        for h in range(H):
            hooks = {}
            if h + 1 < H:
                loads(h + 1)
                hooks[3] = (lambda hh=h + 1: (gate1(hh), gate1b(hh)))
                hooks[6] = (lambda hh=h + 1: gate2(hh))
            main(h, hooks)
        ot = [cx.sb([128, 8, 128], BF16, "ot") for _ in range(3)]
        tiles = [t for t in range(16) if t % 2 in qsel]

        def prep(n):
            tile = tiles[n]
            otl, b_ot = ot[n % 3]
            for c in range(8):
                cx.tr(banks.tT[:, c * 128:(c + 1) * 128], osb[:, tile, c * 128:(c + 1) * 128], consts.ident[:],
                      [b_osb, consts.b_ident], [banks.bT])
            cx.copy("act", otl[:].rearrange("p c t -> p (c t)"), banks.tT[:, :], [banks.bT], [b_ot])

        prep(0)
        for n, tile in enumerate(tiles):
            otl, b_ot = ot[n % 3]
            yb = 2 * (n % 2)
            for hf in range(2):
                for c in range(8):
                    cx.mm(banks.t[yb + hf][:, :], otl[:, c, :], wo[:, c, hf * 512:(hf + 1) * 512], c == 0, c == 7,
                          [b_ot] + wb_wo.get(hf * 512, (hf + 1) * 512), [banks.b[yb + hf]])
            if n + 1 < len(tiles):
                prep(n + 1)
            epi.run(tile, banks.t[yb][:, :], banks.t[yb + 1][:, :], [banks.b[yb], banks.b[yb + 1]],
                    xres_in, xres_out, xT_out, tail_out, out_bufs,
                    next_tile=(tiles[n + 1] if n + 1 < len(tiles) else None))
        epi.flush()
        cx.P.barrier()
    cx.es = None


PARAM_SHAPES = {
    "ln_mix_g": (DEPTH, D), "ln_mix_b": (DEPTH, D), "ln_ffn_g": (DEPTH, D), "ln_ffn_b": (DEPTH, D),
    "a_w_in": (2, D, 2048), "a_ln_g": (2, D), "a_ln_b": (2, D), "a_w_s": (2, 8, 128, 128),
    "a_b_s": (2, 8, 128), "a_w_out": (2, D, D), "b_w_qkv": (2, D, 3072), "b_w_o": (2, D, D),
    "rel_bias": (32, 16), "f_w_up": (DEPTH, D, 2 * FF), "f_conv_w": (DEPTH, 3, 2 * FF),
    "f_conv_b": (DEPTH, 2 * FF), "f_w_down": (DEPTH, FF, D),
}
CONST_SHAPES = {
    "ident": ((128, 128), BF16), "jm": ((128, 128), BF16), "tril": ((128, 8, 128), BF16),
    "gmask": ((128, 8, 16), F32), "oof": ((128, 8, 16), F32), "oh": ((33, 2048), F32),
    "koh": ((16, 4096), BF16), "ones16": ((1, 16), F32), "ident32": ((128, 128), F32),
}
STAGE_PARAMS = {
    "pro": ["ident"],
    "gmlp": ["ln_mix_g", "ln_mix_b", "a_w_in", "a_ln_g", "a_ln_b", "a_w_s", "a_b_s", "a_w_out", "ident", "tril"],
    "ffn": ["ln_ffn_g", "ln_ffn_b", "f_w_up", "f_conv_w", "f_conv_b", "f_w_down", "ident", "ident32"],
    "attproj": ["b_w_qkv", "ident"],
    "attcore": ["ln_mix_g", "ln_mix_b", "b_w_o", "rel_bias", "ident", "jm", "gmask", "oof", "oh", "koh", "ones16"],
}


def declare(nc, names, single_layer):
    din = {}
    for n in names:
        if n in PARAM_SHAPES:
            shp = list(PARAM_SHAPES[n])
            if single_layer and n != "rel_bias":
                shp[0] = 1
            din[n] = nc.dram_tensor(n, shp, F32, kind="ExternalInput").ap()
        else:
            shp, dt = CONST_SHAPES[n]
            din[n] = nc.dram_tensor(n, list(shp), dt, kind="ExternalInput").ap()
    return din


def rel_bucket_np(dist):
    n = np.maximum(dist, 0)
    nf = np.maximum(n, 1).astype(np.float32)
    large = 16 + (np.log(nf / np.float32(16)) / np.float32(math.log(8)) * np.float32(16)).astype(np.int32)
    large = np.minimum(large, 31)
    return np.where(n < 16, n, large)


def host_consts(hA, hX):
    c = {}
    c["ident"] = np.eye(128, dtype=np.float32).astype(NPBF)
    c["jm"] = np.eye(128, dtype=np.float32)[::-1].copy().astype(NPBF)
    tr = np.tril(np.ones((128, 128), np.float32))
    c["tril"] = np.ascontiguousarray(np.broadcast_to(tr[:, None, :], (128, 8, 128))).astype(NPBF)
    hs = (hA, 1 - hA)
    gslot = np.array([2 * (s_ // 2) + hs[s_ % 2] for s_ in range(16)])
    gm = np.zeros((8, 16), np.float32)
    oo = np.zeros((8, 16), np.float32)
    for i in range(8):
        G = 2 * i + hX
        gm[i, gslot >= G] = -1e30
        oo[i, gslot >= G] = 1.0
    c["gmask"] = np.ascontiguousarray(np.broadcast_to(gm[None], (128, 8, 16)))
    c["oof"] = np.ascontiguousarray(np.broadcast_to(oo[None], (128, 8, 16)))
    oh = np.zeros((33, 2048), np.float32)
    m = np.arange(512)
    for ri, rel in enumerate((-2, -1, 0, 1)):
        sig = rel % 2
        bd = (2 if rel < 0 else 0) + hX - hs[sig]
        dist = m - 255 + 256 * bd
        bk = rel_bucket_np(dist)
        ok = dist >= 0
        oh[bk[ok], ri * 512 + m[ok]] = 1.0
        oh[32, ri * 512 + m[~ok]] = NEG
    c["oh"] = oh
    koh = np.zeros((16, 4096), np.float32)
    for n in range(16):
        koh[n, n * 256:(n + 1) * 256] = 1.0
    c["koh"] = koh.astype(NPBF)
    c["ones16"] = np.ones((1, 16), np.float32)
    c["ident32"] = np.eye(128, dtype=np.float32)
    fl = np.zeros((128, 2), np.float32)
    fl[:, hX] = 1.0
    c["hflag"] = fl
    return c


_PROGS = {}


def build_unfused(kind):
    if kind in _PROGS:
        return _PROGS[kind]
    nc = bass.Bass("TRN2", target_bir_lowering=False)
    din = declare(nc, STAGE_PARAMS[kind], True)

    def ext_in(name, shape, dt):
        return nc.dram_tensor(name, list(shape), dt, kind="ExternalInput").ap()

    def ext_out(name, shape, dt):
        return nc.dram_tensor(name, list(shape), dt, kind="ExternalOutput").ap()

    cx = Cx(nc)
    with contextlib.ExitStack() as top:
        cx.es = top
        consts = load_consts(cx, din)
        banks = Banks(cx)
        if kind == "pro":
            x_in = ext_in("xres_in", [NT, D], F32)
            stage_prologue(cx, consts, banks, x_in, ext_out("xT_out", [D, NT], BF16),
                           ext_out("tail_out", [D, 16], BF16))
        elif kind == "gmlp":
            stage_gmlp(cx, consts, banks, din, 0, 0, [(ext_in("xres_in", [NT, D], F32),
                       ext_in("xT_in", [D, NT], BF16), ext_out("xres_out", [NT, D], F32),
                       ext_out("xT_out", [D, NT], BF16), ext_out("tail_out", [D, 16], BF16))])
        elif kind == "ffn":
            halo_in = ext_in("halo_in", [D, 16], BF16)

            def halo_fn(cx_, halo, b_halo):
                cx_.dma("sp", halo[:], halo_in.rearrange("(k p) t -> p k t", p=128), writes=[b_halo], nonc=True)

            stage_ffn(cx, consts, banks, din, 0, ext_in("xres_in", [NT, D], F32), ext_in("xT_in", [D, NT], BF16),
                      halo_fn, ext_out("xres_out", [NT, D], F32), ext_out("xT_out", [D, NT], BF16),
                      ext_out("tail_out", [D, 16], BF16))
        elif kind == "attproj":
            stage_attproj(cx, consts, banks, din, 0, [(ext_in("xT_in", [D, NT], BF16),
                          ext_out("qT_out", [D, NT], BF16), ext_out("kT_out", [D, NT], BF16),
                          ext_out("v_out", [NT, D], BF16), ext_out("kbar_out", [D, 8], F32))])
        elif kind == "attcore":
            bvd = nc.dram_tensor("bvd", [16, 2048], BF16, kind="Internal")
            stage_attcore(cx, consts, banks, din, 0, 0, ext_in("xres_in", [NT, D], F32),
                          ext_in("qT_in", [D, NT], BF16), ext_in("kT_all", [2, D, NT], BF16),
                          ext_in("v_all", [2, NT, D], BF16), ext_in("kbar_all", [2, D, 8], F32), bvd,
                          ext_out("xres_out", [NT, D], F32),
                          ext_out("xT_out", [D, NT], BF16), ext_out("tail_out", [D, 16], BF16))
        cx.P.finish(cx.outs)
        cx.P.emit_all()
    _PROGS[kind] = nc
    return nc


def run_stage(kind, in_maps, cores):
    nc = build_unfused(kind)
    res = run_bass_kernel_spmd(nc, in_maps, core_ids=list(range(len(cores))))
    return res.results


LAYER_PARAM_IDX = {
    "gmlp": lambda li: {"ln_mix_g": li, "ln_mix_b": li, "a_w_in": li // 2, "a_ln_g": li // 2, "a_ln_b": li // 2,
                        "a_w_s": li // 2, "a_b_s": li // 2, "a_w_out": li // 2},
    "ffn": lambda li: {"ln_ffn_g": li, "ln_ffn_b": li, "f_w_up": li, "f_conv_w": li, "f_conv_b": li,
                       "f_w_down": li},
    "attproj": lambda li: {"b_w_qkv": li // 2},
    "attcore": lambda li: {"ln_mix_g": li, "ln_mix_b": li, "b_w_o": li // 2},
}


def stage_inputs(kind, li, params, consts_c):
    m = {}
    idx = LAYER_PARAM_IDX.get(kind, lambda li: {})(li)
    for n in STAGE_PARAMS[kind]:
        if n in idx:
            m[n] = np.ascontiguousarray(params[n][idx[n]:idx[n] + 1])
        elif n == "rel_bias":
            m[n] = params[n]
        else:
            m[n] = consts_c[n]
    return m


def to_local(x):
    B = x.shape[0]
    xb = x.reshape(B, 16, 256, D)
    return [np.ascontiguousarray(xb[c // 2, (c % 2)::2].reshape(NT, D)) for c in range(2 * B)]


def from_local(outs, B):
    y = np.zeros((B, 16, 256, D), np.float32)
    for c in range(2 * B):
        y[c // 2, (c % 2)::2] = outs[c].reshape(8, 256, D)
    return y.reshape(B, 4096, D)


def make_halo(tails, c):
    half = c % 2
    pt = tails[c ^ 1]
    halo = np.zeros((D, 16), NPBF)
    if half == 0:
        halo[:, 2:16] = pt[:, 0:14]
    else:
        halo[:, :] = pt
    return halo


def forward_unfused(x, params, ncores=8, nlayers=DEPTH, debug=None):
    cores = list(range(ncores))
    cc = [host_consts(c % 2, c % 2) for c in cores]
    xl = to_local(np.asarray(x, np.float32)[: ncores // 2])
    r = run_stage("pro", [dict(xres_in=xl[c], **stage_inputs("pro", 0, params, cc[c])) for c in cores], cores)
    xres = xl
    xT = [r[c]["xT_out"] for c in cores]
    tails = [r[c]["tail_out"] for c in cores]
    for li in range(nlayers):
        if li % 2 == 0:
            r = run_stage("gmlp", [dict(xres_in=xres[c], xT_in=xT[c], **stage_inputs("gmlp", li, params, cc[c]))
                                   for c in cores], cores)
        else:
            r = run_stage("attproj", [dict(xT_in=xT[c], **stage_inputs("attproj", li, params, cc[c]))
                                      for c in cores], cores)
            ims = []
            for c in cores:
                p0, p1 = c, c ^ 1
                ims.append(dict(xres_in=xres[c], qT_in=r[c]["qT_out"],
                                kT_all=np.stack([r[p0]["kT_out"], r[p1]["kT_out"]]),
                                v_all=np.stack([r[p0]["v_out"], r[p1]["v_out"]]),
                                kbar_all=np.stack([r[p0]["kbar_out"], r[p1]["kbar_out"]]),
                                **stage_inputs("attcore", li, params, cc[c])))
            r = run_stage("attcore", ims, cores)
        xres = [r[c]["xres_out"] for c in cores]
        xT = [r[c]["xT_out"] for c in cores]
        tails = [r[c]["tail_out"] for c in cores]
        if debug is not None:
            debug.append(("mix%d" % li, from_local(xres, ncores // 2)))
        r = run_stage("ffn", [dict(xres_in=xres[c], xT_in=xT[c], halo_in=make_halo(tails, c),
                                   **stage_inputs("ffn", li, params, cc[c])) for c in cores], cores)
        xres = [r[c]["xres_out"] for c in cores]
        xT = [r[c]["xT_out"] for c in cores]
        tails = [r[c]["tail_out"] for c in cores]
        if debug is not None:
            debug.append(("ffn%d" % li, from_local(xres, ncores // 2)))
    return from_local(xres, ncores // 2)


PASS_TABLES = ["gmask", "oof", "oh", "hflag"]
CONST_SHAPES["hflag"] = ((128, 2), F32)
COMMON_CONSTS = ["ident", "jm", "tril", "koh", "ones16", "ident32"]


def build_fused(nlayers=DEPTH):
    key = ("fused", nlayers)
    if key in _PROGS:
        return _PROGS[key]
    nc = bass.Bass("TRN2", target_bir_lowering=False)
    din = declare(nc, list(PARAM_SHAPES.keys()) + COMMON_CONSTS, False)
    dinx = {}
    for X in "AB":
        d = dict(din)
        for n in PASS_TABLES:
            shp, dt = CONST_SHAPES[n]
            d[n] = nc.dram_tensor("%s_%s" % (n, X), list(shp), dt, kind="ExternalInput").ap()
        dinx[X] = d

    def internal(name, shape, dt):
        return nc.dram_tensor(name, list(shape), dt, kind="Internal")

    x_in = {X: nc.dram_tensor("x_%s" % X, [NT, D], F32, kind="ExternalInput").ap() for X in "AB"}
    out = nc.dram_tensor("out", [NT, D], F32, kind="ExternalOutput").ap()
    xres = {X: [internal("xres_%s%d" % (X, k), [NT, D], F32).ap() for k in range(2)] for X in "AB"}
    xTb = {X: [internal("xT_%s%d" % (X, k), [D, NT], BF16).ap() for k in range(2)] for X in "AB"}
    tail = {X: internal("tail_%s" % X, [D, 16], BF16).ap() for X in "AB"}
    qT = {X: internal("qT_%s" % X, [D, NT], BF16).ap() for X in "AB"}
    kT_all = internal("kT_all", [2, D, NT], BF16).ap()
    v_all = internal("v_all", [2, NT, D], BF16).ap()
    kbar_all = internal("kbar_all", [2, D, 8], F32).ap()
    bvd = {X: internal("bvd_%s" % X, [16, 2048], BF16) for X in "AB"}
    sig = {"A": 0, "B": 1}
    other = {"A": "B", "B": "A"}
    cx = Cx(nc)
    with contextlib.ExitStack() as top:
        cx.es = top
        consts = load_consts(cx, din)
        banks = Banks(cx)
        cur = {}
        for X in "AB":
            stage_prologue(cx, consts, banks, x_in[X], xTb[X][0], tail[X])
            cur[X] = dict(xres=x_in[X], xT=xTb[X][0], k=0)

        def nxt(X, final=False):
            st = cur[X]
            k = st["k"]
            st["k"] ^= 1
            if final:
                return out, None
            return xres[X][k], xTb[X][k ^ 1]

        def halo_fn_for(X):
            def halo_fn(cx_, halo, b_halo):
                to, b_to = cx_.sb([128, 8, 16], BF16, "to")
                fl, b_fl = cx_.sb([128, 2], F32, "hfl")
                tmp, b_tmp = cx_.sb([128, 8, 16], F32, "htmp")
                cx_.dma("sp", to[:], tail[other[X]].rearrange("(k p) t -> p k t", p=128), writes=[b_to], nonc=True)
                cx_.dma("sp", fl[:], dinx[X]["hflag"], writes=[b_fl])
                cx_.ts("dve", tmp[:], to[:], fl[:, 1:2], None, ALU.mult, None, [b_to, b_fl], [b_tmp])
                cx_.copy("dve", halo[:, :, 0:2], tmp[:, :, 0:2], [b_tmp], [b_halo])
                cx_.stt(halo[:, :, 2:16], to[:, :, 0:14], fl[:, 0:1], tmp[:, :, 2:16], ALU.mult, ALU.add,
                        [b_to, b_fl, b_tmp], [b_halo])
            return halo_fn

        for li in range(nlayers):
            lastl = li == DEPTH - 1
            j = li // 2
            passes = "AB"
            if li % 2 == 0:
                pl = []
                for X in passes:
                    xo, xTo = nxt(X)
                    pl.append((cur[X]["xres"], cur[X]["xT"], xo, xTo, tail[X]))
                    cur[X]["xres"], cur[X]["xT"] = xo, xTo
                stage_gmlp(cx, consts, banks, din, j, li, pl)
            else:
                stage_attproj(cx, consts, banks, din, j,
                              [(cur[X]["xT"], qT[X], kT_all[sig[X]], v_all[sig[X]], kbar_all[sig[X]]) for X in passes])
                for X in "AB":
                    xo, xTo = nxt(X)
                    stage_attcore(cx, consts, banks, dinx[X], j, li, cur[X]["xres"], qT[X], kT_all, v_all, kbar_all,
                                  bvd[X], xo, xTo, tail[X], qsel=((1,) if (lastl and X == "B") else (0, 1)))
                    cur[X]["xres"], cur[X]["xT"] = xo, xTo
            for X in ("A" if lastl else "AB"):
                cx.outs = []
                xo, xTo = nxt(X, final=lastl)
                stage_ffn(cx, consts, banks, dinx[X], li, cur[X]["xres"], cur[X]["xT"], halo_fn_for(X), xo, xTo,
                          None)
                cur[X]["xres"], cur[X]["xT"] = xo, xTo
        if nlayers < DEPTH:
            cx.es = top
            t, bt = cx.sb([128, D], F32, "dbg")
            cx.outs = []
            for tile in range(16):
                cx.dma("sp", t[:], cur["A"]["xres"][tile * 128:(tile + 1) * 128, :], writes=[bt])
                cx.outs.append(cx.dma("sp", out[tile * 128:(tile + 1) * 128, :], t[:], reads=[bt]))
        cx.P.finish(cx.outs)
        cx.P.emit_all()
    _PROGS[key] = nc
    return nc


def forward_fused(x, params, ncores=8, nlayers=DEPTH):
    nc = build_fused(nlayers)
    xl = to_local(np.asarray(x, np.float32)[: ncores // 2])
    in_maps = []
    for c in range(ncores):
        hA = c % 2
        m = {k: params[k] for k in PARAM_SHAPES}
        ca = host_consts(hA, hA)
        cb = host_consts(hA, 1 - hA)
        for n in COMMON_CONSTS:
            m[n] = ca[n]
        for n in PASS_TABLES:
            m[n + "_A"] = ca[n]
            m[n + "_B"] = cb[n]
        m["x_A"] = xl[c]
        m["x_B"] = xl[c ^ 1]
        in_maps.append(m)
    res = run_bass_kernel_spmd(nc, in_maps, core_ids=list(range(ncores)))
    return from_local([res.results[c]["out"] for c in range(ncores)], ncores // 2)


def kernel(**inputs):
    params = {k: np.ascontiguousarray(np.asarray(v, np.float32)) for k, v in inputs.items() if k != "x"}
    x = np.asarray(inputs["x"], np.float32)
    return forward_fused(x, params).astype(np.float32)
```

```python
import contextlib
import math
import numpy as np
import ml_dtypes
import concourse.bass as bass
import concourse.mybir as mybir
from concourse.bass_utils import run_bass_kernel_spmd

F32 = mybir.dt.float32
BF16 = mybir.dt.bfloat16
AF = mybir.ActivationFunctionType
ALU = mybir.AluOpType
AX = mybir.AxisListType
NPBF = ml_dtypes.bfloat16

D = 1024
DEPTH = 4
NT = 2048
NBLK = 8
H = 16
DH = 64
FF = 2816
ALPHA = (2 * DEPTH) ** 0.25
EPS = 1e-5
NEG = -30000.0

ENGS = ("pe", "act", "dve", "pool", "sp")
SAME_ENGINE_SYNC = True
N_DMA_SLOTS = 20
STORE_Q = "pool"


class Buf:
    __slots__ = ("name", "w", "r")

    def __init__(self, name=""):
        self.name = name
        self.w = None
        self.r = []


class Op:
    __slots__ = ("eng", "emit", "deps", "dma", "slot", "slot_val", "prev_val",
                 "signal", "count", "epoch")

    def __init__(self, eng, emit, dma):
        self.eng = eng
        self.emit = emit
        self.deps = set()
        self.dma = dma
        self.slot = None
        self.slot_val = 0
        self.prev_val = 0
        self.signal = False
        self.count = 0
        self.epoch = 0


class Prog:
    def __init__(self, nc):
        self.nc = nc
        self.ops = {e: [] for e in ENGS}
        self.slot_rr = {e: 0 for e in ENGS}
        self.slot_cum = {}
        self.pending_dma = []
        self.epoch = 0

    def op(self, eng, emit, reads=(), writes=(), dma=False):
        o = Op(eng, emit, dma)
        o.epoch = self.epoch
        for b in reads:
            if b.w is not None and b.w is not o:
                o.deps.add(b.w)
            b.r.append(o)
        for b in writes:
            if b.w is not None and b.w is not o:
                o.deps.add(b.w)
            for r in b.r:
                if r is not o:
                    o.deps.add(r)
            b.w = o
            b.r = []
        if dma:
            k = self.slot_rr[eng]
            self.slot_rr[eng] = (k + 1) % N_DMA_SLOTS
            key = (eng, k)
            prev = self.slot_cum.get(key, 0)
            o.slot = key
            o.prev_val = prev
            o.slot_val = prev + 16
            self.slot_cum[key] = o.slot_val
            self.pending_dma.append(o)
        self.ops[eng].append(o)
        return o

    def barrier(self):
        lasts = []
        for e in ENGS:
            for o in reversed(self.ops[e]):
                if not o.dma and o.emit is not None:
                    lasts.append(o)
                    break
        pend = list(self.pending_dma)
        self.pending_dma = []
        for e in ENGS:
            o = Op(e, None, False)
            o.epoch = self.epoch
            o.deps.update(lasts)
            o.deps.update(pend)
            self.ops[e].append(o)
        self.nbar = getattr(self, "nbar", 0) + 1
        self.epoch = self.nbar // 3

    def finish(self, out_ops):
        o = Op("sp", None, False)
        o.epoch = self.epoch
        o.deps.update(out_ops)
        self.ops["sp"].append(o)

    def emit_all(self):
        nc = self.nc
        for e in ENGS:
            for o in self.ops[e]:
                for d in o.deps:
                    if d.dma:
                        continue
                    if d.eng == o.eng and (d.eng == "pe" or not SAME_ENGINE_SYNC):
                        continue
                    d.signal = True
        for e in ENGS:
            c = {}
            for o in self.ops[e]:
                if o.signal:
                    c[o.epoch] = c.get(o.epoch, 0) + 1
                o.count = c.get(o.epoch, 0)
        with contextlib.ExitStack() as es:
            esem = {}
            for e in ENGS:
                for ep in range(self.epoch + 1):
                    if any(o.signal and o.epoch == ep for o in self.ops[e]):
                        esem[(e, ep)] = es.enter_context(nc.semaphore("s_%s_%d" % (e, ep)))
            dsem = {}
            for key in self.slot_cum:
                dsem[key] = es.enter_context(nc.semaphore("d_%s_%d" % key))
            block = es.enter_context(nc.Block())

            def run(e, eng):
                waited = {}
                for o in self.ops[e]:
                    waits = {}
                    for d in o.deps:
                        if d.dma:
                            s, v = dsem[d.slot], d.slot_val
                            k = ("d",) + d.slot
                        else:
                            if d.eng == e and (e == "pe" or not SAME_ENGINE_SYNC):
                                continue
                            s, v = esem[(d.eng, d.epoch)], d.count
                            k = ("e", d.eng, d.epoch)
                        if waits.get(k, (None, 0))[1] < v:
                            waits[k] = (s, v)
                    if o.dma and o.prev_val > 0:
                        k = ("d",) + o.slot
                        if waits.get(k, (None, 0))[1] < o.prev_val:
                            waits[k] = (dsem[o.slot], o.prev_val)
                    for k, (s, v) in waits.items():
                        if waited.get(k, 0) >= v:
                            continue
                        waited[k] = v
                        eng.wait_ge(s, v)
                    if o.emit is None:
                        continue
                    ins = o.emit(eng)
                    if o.dma:
                        ins.then_inc(dsem[o.slot], 16)
                    elif o.signal:
                        ins.then_inc(esem[(e, o.epoch)], 1)

            @block.tensor
            def _(eng):
                run("pe", eng)

            @block.scalar
            def _(eng):
                run("act", eng)

            @block.vector
            def _(eng):
                run("dve", eng)

            @block.gpsimd
            def _(eng):
                run("pool", eng)

            @block.sync
            def _(eng):
                run("sp", eng)


class Cx:
    def __init__(self, nc):
        self.nc = nc
        self.P = Prog(nc)
        self.es = None
        self.uid = 0
        self.dq = 0
        self.outs = []

    def sb(self, shape, dt, name=None):
        self.uid += 1
        t = self.es.enter_context(self.nc.sbuf_tensor("%s_%d" % (name or "t", self.uid), list(shape), dt))
        return t, Buf(name or "t")

    def ps(self, shape, dt, name=None):
        self.uid += 1
        t = self.es.enter_context(self.nc.psum_tensor("%s_%d" % (name or "p", self.uid), list(shape), dt))
        return t, Buf(name or "p")

    def dram(self, name, shape, dt, kind):
        return self.nc.dram_tensor(name, list(shape), dt, kind=kind)

    def mm(self, out, lhsT, rhs, start, stop, reads, writes):
        self.P.op("pe", lambda e: e.matmul(out, lhsT=lhsT, rhs=rhs, start=start, stop=stop),
                  reads=reads, writes=writes)

    def tr(self, out, in_, ident, reads, writes):
        self.P.op("pe", lambda e: e.transpose(out=out, in_=in_, identity=ident), reads=reads, writes=writes)

    def act(self, out, in_, func, reads, writes, bias=None, scale=None):
        kw = {}
        if bias is not None:
            kw["bias"] = bias
        if scale is not None:
            kw["scale"] = scale
        self.P.op("act", lambda e: e.activation(out=out, in_=in_, func=func, **kw), reads=reads, writes=writes)

    def ts(self, eng, out, in0, s1, s2, op0, op1, reads, writes):
        if op1 is None:
            self.P.op(eng, lambda e: e.tensor_scalar(out=out, in0=in0, scalar1=s1, scalar2=None, op0=op0),
                      reads=reads, writes=writes)
        else:
            self.P.op(eng, lambda e: e.tensor_scalar(out=out, in0=in0, scalar1=s1, scalar2=s2, op0=op0, op1=op1),
                      reads=reads, writes=writes)

    def stt(self, out, in0, scalar, in1, op0, op1, reads, writes):
        self.P.op("dve", lambda e: e.scalar_tensor_tensor(out=out, in0=in0, scalar=scalar, in1=in1,
                                                          op0=op0, op1=op1), reads=reads, writes=writes)

    def tt(self, eng, out, in0, in1, op, reads, writes):
        self.P.op(eng, lambda e: e.tensor_tensor(out=out, in0=in0, in1=in1, op=op), reads=reads, writes=writes)

    def copy(self, eng, out, in_, reads, writes):
        if eng == "act":
            self.P.op("act", lambda e: e.copy(out=out, in_=in_), reads=reads, writes=writes)
        else:
            self.P.op(eng, lambda e: e.tensor_copy(out=out, in_=in_), reads=reads, writes=writes)

    def memset(self, eng, ap, val, writes):
        self.P.op(eng, lambda e: e.memset(ap, val), writes=writes)

    def dma(self, q, out, in_, reads=(), writes=(), nonc=False):
        if nonc:
            def em(e):
                with self.nc.allow_non_contiguous_dma(reason="small strided param load"):
                    return e.dma_start(out=out, in_=in_)
        else:
            def em(e):
                return e.dma_start(out=out, in_=in_)
        return self.P.op(q, em, reads=reads, writes=writes, dma=True)


def bcast_rows(ap2d_row, n=128):
    a = ap2d_row.partition_broadcast(n)
    if len(a.shape) == 3:
        a = a[:, 0, :]
    return a


class Consts:
    pass


def load_consts(cx, din):
    c = Consts()
    c.ident, c.b_ident = cx.sb([128, 128], BF16, "ident")
    cx.dma("sp", c.ident[:], din["ident"], writes=[c.b_ident])
    c.eps, c.b_eps = cx.sb([128, 1], F32, "eps")
    cx.memset("dve", c.eps[:], EPS, [c.b_eps])
    return c


def rstd_op(cx, consts, rstd, b_rstd, var_ap, b_var):
    cx.act(rstd, var_ap, AF.Sqrt, [b_var, consts.b_eps], [b_rstd], bias=consts.eps[:, 0:1])
    cx.P.op("dve", lambda e: e.reciprocal(out=rstd, in_=rstd), reads=[b_rstd], writes=[b_rstd])


class Banks:
    def __init__(self, cx):
        self.pb = []
        self.t = []
        self.b = []
        for i in range(4):
            t, _ = cx.ps([128, 1024], F32, "pb%d" % i)
            self.pb.append(t)
            for hf in range(2):
                self.t.append(t[:, hf * 512:(hf + 1) * 512])
                self.b.append(Buf("bank%d" % (2 * i + hf)))
        self.tT = self.pb[3].bitcast(BF16)[:, 1024:2048]
        self.bT = self.b[7]

    def pair(self, i):
        return self.pb[i][:, :].rearrange("p (b c) -> p b c", b=2), [self.b[2 * i], self.b[2 * i + 1]]


class Epi:
    def __init__(self, cx, consts, banks, lng_row, lnb_row):
        self.cx = cx
        self.consts = consts
        self.banks = banks
        self.lng, self.b_lng = cx.sb([128, D], F32, "lng")
        self.lnb, self.b_lnb = cx.sb([128, D], F32, "lnb")
        cx.dma("sp", self.lng[:], bcast_rows(lng_row), writes=[self.b_lng])
        cx.dma("sp", self.lnb[:], bcast_rows(lnb_row), writes=[self.b_lnb])
        self.xr = [cx.sb([128, D], F32, "xr") for _ in range(2)]
        self.s = [cx.sb([128, D], F32, "s")] * 2
        self.xn = [cx.sb([128, D], F32, "xn")] * 2
        self.xnb = [cx.sb([128, D], BF16, "xnb") for _ in range(2)]
        self.xts = [cx.sb([128, 8, 128], BF16, "xts") for _ in range(2)]
        self.st = [cx.sb([128, 2, 6], F32, "st") for _ in range(2)]
        self.mv = [cx.sb([128, 2], F32, "mv") for _ in range(2)]
        self.rstd = [cx.sb([128, 1], F32, "rstd") for _ in range(2)]
        self.nmr = [cx.sb([128, 1], F32, "nmr") for _ in range(2)]
        self.tl, self.b_tl = cx.sb([128, 8, 16], BF16, "tl")
        self.k = 0
        self.pending = None

    def prefetch(self, tile, xres_in):
        xr, b_xr = self.xr[self.k]
        self.cx.dma("sp", xr[:], xres_in[tile * 128:(tile + 1) * 128, :], writes=[b_xr])
        self.pre = tile

    def run(self, tile, y0, y1, by, xres_in, xres_out, xT_out, tail_out, out_bufs=None, next_tile=None):
        cx = self.cx
        k = self.k
        self.k ^= 1
        self.flush()
        xr, b_xr = self.xr[k]
        s, b_s = self.s[k]
        xn, b_xn = self.xn[k]
        xnb, b_xnb = self.xnb[k]
        xts, b_xts = self.xts[k]
        st, b_st = self.st[k]
        mv, b_mv = self.mv[k]
        rstd, b_rstd = self.rstd[k]
        rows = slice(tile * 128, (tile + 1) * 128)
        if getattr(self, "pre", None) != tile:
            cx.dma("sp", xr[:], xres_in[rows, :], writes=[b_xr])
        self.pre = None
        if next_tile is not None:
            self.prefetch(next_tile, xres_in)
        cx.stt(s[:, 0:512], xr[:, 0:512], ALPHA, y0, ALU.mult, ALU.add, [b_xr, by[0]], [b_s])
        cx.stt(s[:, 512:1024], xr[:, 512:1024], ALPHA, y1, ALU.mult, ALU.add, [b_xr, by[1]], [b_s])
        cx.P.op("dve", lambda e: e.bn_stats(out=st[:, 0, :], in_=s[:, 0:512]), reads=[b_s], writes=[b_st])
        cx.P.op("dve", lambda e: e.bn_stats(out=st[:, 1, :], in_=s[:, 512:1024]), reads=[b_s], writes=[b_st])
        cx.P.op("dve", lambda e: e.bn_aggr(out=mv[:], in_=st[:].rearrange("p a b -> p (a b)")),
                reads=[b_st], writes=[b_mv])
        rstd_op(cx, self.consts, rstd[:], b_rstd, mv[:, 1:2], b_mv)
        cx.stt(s[:], s[:], mv[:, 0:1], self.lng[:], ALU.subtract, ALU.mult, [b_s, b_mv, self.b_lng], [b_s])
        cx.stt(xn[:], s[:], rstd[:, 0:1], self.lnb[:], ALU.mult, ALU.add, [b_s, b_rstd, self.b_lnb], [b_xn])
        o = cx.dma(STORE_Q, xres_out[rows, :], xn[:], reads=[b_xn], writes=out_bufs or ())
        cx.outs.append(o)
        if xT_out is None:
            return
        cx.copy("act", xnb[:], xn[:], [b_xn], [b_xnb])
        self.pending = (tile, xnb, b_xnb, xts, b_xts, xT_out, tail_out, out_bufs)

    def flush(self):
        if self.pending is None:
            return
        cx = self.cx
        tile, xnb, b_xnb, xts, b_xts, xT_out, tail_out, out_bufs = self.pending
        self.pending = None
        bk = self.banks
        for c in range(8):
            cx.tr(bk.tT[:, c * 128:(c + 1) * 128], xnb[:, c * 128:(c + 1) * 128], self.consts.ident[:],
                  [b_xnb, self.consts.b_ident], [bk.bT])
        cx.copy("act", xts[:].rearrange("p c t -> p (c t)"), bk.tT[:, :], [bk.bT], [b_xts])
        o = cx.dma(STORE_Q, xT_out.rearrange("(c p) t -> p c t", p=128)[:, :, tile * 128:(tile + 1) * 128], xts[:],
                   reads=[b_xts], writes=out_bufs or ())
        cx.outs.append(o)
        if tile % 2 == 1:
            blk = tile // 2
            cx.copy("pool", self.tl[:, :, 2 * blk:2 * blk + 2], xts[:, :, 126:128], [b_xts], [self.b_tl])
        if tile == 15 and tail_out is not None:
            o = cx.dma("sp", tail_out.rearrange("(c p) t -> p c t", p=128), self.tl[:], reads=[self.b_tl],
                       writes=out_bufs or (), nonc=True)
            cx.outs.append(o)


class Stager:
    def __init__(self, cx, n=4, size=1024):
        self.bufs = [cx.sb([128, size], F32, "stg32") for _ in range(n)]
        self.k = 0
        self.size = size

    def load(self, cx, dst, src, b_dst, shape2=None, eng="pool"):
        t, b = self.bufs[self.k % len(self.bufs)]
        self.k += 1
        if shape2 is None:
            n = dst.shape[1]
            view = t[:, 0:n]
        else:
            a, bb = shape2
            view = t[:, 0:a * bb].rearrange("p (a b) -> p a b", a=a)
        cx.dma("sp", view, src, writes=[b])
        cx.copy(eng, dst, view, [b], [b_dst])

    def dma(self, cx, src, n=None, shape2=None):
        t, b = self.bufs[self.k % len(self.bufs)]
        self.k += 1
        if shape2 is None:
            view = t[:, 0:n]
        else:
            a, bb = shape2
            view = t[:, 0:a * bb].rearrange("p (a b) -> p a b", a=a)
        cx.dma("sp", view, src, writes=[b])
        return view, b


class WBufs:
    def __init__(self, ncols, piece):
        self.piece = piece
        self.bufs = [Buf("w") for _ in range((ncols + piece - 1) // piece)]

    def get(self, c0, c1):
        return self.bufs[c0 // self.piece:(c1 - 1) // self.piece + 1]


CAST_ROT = ("act", "dve", "act", "dve", "pool")
WUP_CAST = ("act", "pool", "dve", "act", "pool", "dve", "act", "pool")


def load_w_bf16(cx, stager, dst, src, k_chunks, col0, ncols, maxc=1024, order=None):
    wb = WBufs(ncols, maxc)
    pieces = list(range(0, ncols, maxc))
    if order is not None:
        pieces = [pieces[i] for i in order]
    n = 0
    for c0 in pieces:
        c1 = min(ncols, c0 + maxc)
        for k in range(k_chunks):
            stager.load(cx, dst[:, k, c0:c1], src[k * 128:(k + 1) * 128, col0 + c0:col0 + c1], wb.get(c0, c1)[0],
                        eng=CAST_ROT[n % len(CAST_ROT)])
            n += 1
    return wb


def stage_prologue(cx, consts, banks, x_in, xT_out, tail_out, out_bufs=None):
    with contextlib.ExitStack() as es:
        cx.es = es
        xr = [cx.sb([128, D], F32, "pxr") for _ in range(2)]
        xb = [cx.sb([128, D], BF16, "pxb") for _ in range(2)]
        xts = [cx.sb([128, 8, 128], BF16, "pxts") for _ in range(2)]
        tl, b_tl = cx.sb([128, 8, 16], BF16, "ptl")
        for tile in range(16):
            k = tile % 2
            cx.dma("sp", xr[k][0][:], x_in[tile * 128:(tile + 1) * 128, :], writes=[xr[k][1]])
            cx.copy("dve", xb[k][0][:], xr[k][0][:], [xr[k][1]], [xb[k][1]])
            for c in range(8):
                cx.tr(banks.tT[:, c * 128:(c + 1) * 128], xb[k][0][:, c * 128:(c + 1) * 128], consts.ident[:],
                      [xb[k][1], consts.b_ident], [banks.bT])
            cx.copy("act", xts[k][0][:].rearrange("p c t -> p (c t)"), banks.tT[:, :], [banks.bT], [xts[k][1]])
            o = cx.dma("sp", xT_out.rearrange("(c p) t -> p c t", p=128)[:, :, tile * 128:(tile + 1) * 128],
                       xts[k][0][:], reads=[xts[k][1]], writes=out_bufs or ())
            cx.outs.append(o)
            if tile % 2 == 1:
                blk = tile // 2
                cx.copy("pool", tl[:, :, 2 * blk:2 * blk + 2], xts[k][0][:, :, 126:128], [xts[k][1]], [b_tl])
        o = cx.dma("sp", tail_out.rearrange("(c p) t -> p c t", p=128), tl[:], reads=[b_tl],
                   writes=out_bufs or (), nonc=True)
        cx.outs.append(o)
        cx.P.barrier()
    cx.es = None


def stage_gmlp(cx, consts, banks, din, j, li, passes, in_bufs=(), out_bufs=None):
    w_in = din["a_w_in"][j]
    w_out = din["a_w_out"][j]
    with contextlib.ExitStack() as es:
        cx.es = es
        xT, b_xT = cx.sb([128, 8, NT], BF16, "xT")
        for c in range(8):
            cx.dma("sp", xT[:, c, :], passes[0][1][c * 128:(c + 1) * 128, :], reads=in_bufs, writes=[b_xT])
        stager = Stager(cx)
        win, _ = cx.sb([128, 8, 2048], BF16, "win")
        wb_win = load_w_bf16(cx, stager, win, w_in, 8, 0, 2048)
        wout, _ = cx.sb([128, 8, 1024], BF16, "wout")
        wb_wout = load_w_bf16(cx, stager, wout, w_out, 8, 0, 1024)
        def spatial_setup():
            wsn, b_wsn = cx.sb([128, 8, 128], BF16, "wsn")
            cx.dma("pool", wsn[:], din["a_w_s"][j].rearrange("g t s -> t g s"), writes=[b_wsn])
            tril, b_tril = cx.sb([128, 8, 128], BF16, "tril")
            cx.dma("sp", tril[:], din["tril"], writes=[b_tril])
            cx.tt("pool", wsn[:], wsn[:], tril[:], ALU.mult, [b_wsn, b_tril], [b_wsn])
            wmT, b_wmT = cx.sb([128, 8, 128], BF16, "wmT")
            for g in range(8):
                cx.tr(banks.tT[:, g * 128:(g + 1) * 128], wsn[:, g, :], consts.ident[:], [b_wsn, consts.b_ident],
                      [banks.bT])
            cx.copy("act", wmT[:].rearrange("p g t -> p (g t)"), banks.tT[:, :], [banks.bT], [b_wmT])
            lbb, b_lbb = cx.sb([128, 1024], BF16, "lbb")
            cx.dma("pool", lbb[:], bcast_rows(din["a_ln_b"][j:j + 1, :]), writes=[b_lbb])
            bsb, b_bsb = cx.sb([128, 8, 128], F32, "bsb")
            cx.dma("sp", bsb[:].rearrange("p g t -> p (g t)"),
                   bcast_rows(din["a_b_s"][j:j + 1].rearrange("o g t -> o (g t)")), writes=[b_bsb])
            Bt, b_Bt = cx.sb([128, 8, 128], F32, "Bt")
            for c in range(8):
                bank = banks.t[c // 4]
                cx.mm(bank[:, (c % 4) * 128:(c % 4 + 1) * 128], lbb[:, c * 128:(c + 1) * 128], wmT[:, c, :],
                      True, True, [b_lbb, b_wmT], [banks.b[c // 4]])
            for hf in range(2):
                cx.tt("dve", Bt[:, hf * 4:(hf + 1) * 4, :].rearrange("p g t -> p (g t)"), banks.t[hf][:, :],
                      bsb[:, hf * 4:(hf + 1) * 4, :].rearrange("p g t -> p (g t)"), ALU.add,
                      [banks.b[hf], b_bsb], [b_Bt])
            gcol, b_gcol = cx.sb([128, 8], F32, "gcol")
            cx.dma("sp", gcol[:], din["a_ln_g"][j].rearrange("(c p) -> p c", p=128), writes=[b_gcol], nonc=True)
            return wmT, b_wmT, Bt, b_Bt, gcol, b_gcol

        epi = Epi(cx, consts, banks, din["ln_mix_g"][li:li + 1, :], din["ln_mix_b"][li:li + 1, :])
        uT, b_uT = cx.sb([128, 8, 512], F32, "uT")
        vg = [cx.sb([128, D], F32, "vg") for _ in range(2)]
        vn = [cx.sb([128, D], BF16, "vn") for _ in range(2)]
        t1 = [cx.sb([128, 8, 128], F32, "t1") for _ in range(2)]
        zT = [cx.sb([128, 8, 128], BF16, "zT") for _ in range(2)]
        st = [cx.sb([128, 2, 6], F32, "gst") for _ in range(2)]
        mv = [cx.sb([128, 2], F32, "gmv") for _ in range(2)]
        rstd = [cx.sb([128, 1], F32, "grstd") for _ in range(2)]
        nmr = [cx.sb([128, 1], F32, "gnmr") for _ in range(2)]
        sp_state = []

        def run_pass(xres_in, xT_in, xres_out, xT_out, tail_out, first):
            if not first:
                for c in range(8):
                    cx.dma("sp", xT[:, c, :], xT_in[c * 128:(c + 1) * 128, :], reads=in_bufs, writes=[b_xT])
            def u_phase(tg):
                for c in range(8):
                    bank, bb = banks.t[6], banks.b[6]
                    for k in range(8):
                        cx.mm(bank[:, :], win[:, k, c * 128:(c + 1) * 128], xT[:, k, tg * 512:(tg + 1) * 512],
                              k == 0, k == 7, wb_win.get(c * 128, (c + 1) * 128) + [b_xT], [bb])
                    cx.act(uT[:, c, :], bank[:, :], AF.Gelu_apprx_tanh, [bb], [b_uT])

            def part_a(tile):
                k2 = tile % 2
                tcols = slice(tile * 128, (tile + 1) * 128)
                for hf in range(2):
                    for k in range(8):
                        cx.mm(banks.t[2 + hf][:, :], xT[:, k, tcols], win[:, k, 1024 + hf * 512:1024 + (hf + 1) * 512],
                              k == 0, k == 7, [b_xT] + wb_win.get(1024 + hf * 512, 1024 + (hf + 1) * 512), [banks.b[2 + hf]])
                vgt, b_vg = vg[k2]
                vnt, b_vn = vn[k2]
                for hf in range(2):
                    cx.act(vgt[:, hf * 512:(hf + 1) * 512], banks.t[2 + hf][:, :], AF.Gelu_apprx_tanh,
                           [banks.b[2 + hf]], [b_vg])
                stt_, b_st = st[k2]
                mvt, b_mv = mv[k2]
                rs, b_rs = rstd[k2]
                cx.P.op("dve", lambda e, a=stt_, b=vgt: e.bn_stats(out=a[:, 0, :], in_=b[:, 0:512]),
                        reads=[b_vg], writes=[b_st])
                cx.P.op("dve", lambda e, a=stt_, b=vgt: e.bn_stats(out=a[:, 1, :], in_=b[:, 512:1024]),
                        reads=[b_vg], writes=[b_st])
                cx.P.op("dve", lambda e, a=mvt, b=stt_: e.bn_aggr(out=a[:], in_=b[:].rearrange("p a b -> p (a b)")),
                        reads=[b_st], writes=[b_mv])
                rstd_op(cx, consts, rs[:], b_rs, mvt[:, 1:2], b_mv)
                nm, b_nm = nmr[k2]
                cx.stt(nm[:], mvt[:, 0:1], -1.0, rs[:, 0:1], ALU.mult, ALU.mult, [b_mv, b_rs], [b_nm])
                cx.act(vnt[:], vgt[:], AF.Identity, [b_vg, b_rs, b_nm], [b_vn], bias=nm[:, 0:1], scale=rs[:, 0:1])

            def part_b(tile):
                k2 = tile % 2
                tt_ = tile % 4
                vnt, b_vn = vn[k2]
                for c in range(8):
                    cx.mm(banks.t[4 + c // 4][:, (c % 4) * 128:(c % 4 + 1) * 128], vnt[:, c * 128:(c + 1) * 128],
                          wmT[:, c, :], True, True, [b_vn, b_wmT], [banks.b[4 + c // 4]])
                t1t, b_t1 = t1[k2]
                zt, b_z = zT[k2]
                for c in range(8):
                    cx.stt(t1t[:, c, :], banks.t[4 + c // 4][:, (c % 4) * 128:(c % 4 + 1) * 128], gcol[:, c:c + 1],
                           Bt[:, c, :], ALU.mult, ALU.add, [banks.b[4 + c // 4], b_gcol, b_Bt], [b_t1])
                cx.tt("dve", zt[:], t1t[:], uT[:, :, tt_ * 128:(tt_ + 1) * 128], ALU.mult, [b_t1, b_uT], [b_z])

            def part_c(tile):
                k2 = tile % 2
                zt, b_z = zT[k2]
                yb = 0
                for hf in range(2):
                    for c in range(8):
                        cx.mm(banks.t[yb + hf][:, :], zt[:, c, :], wout[:, c, hf * 512:(hf + 1) * 512],
                              c == 0, c == 7, [b_z] + wb_wout.get(hf * 512, (hf + 1) * 512), [banks.b[yb + hf]])
                epi.run(tile, banks.t[yb][:, :], banks.t[yb + 1][:, :], [banks.b[yb], banks.b[yb + 1]],
                        xres_in, xres_out, xT_out, tail_out, out_bufs, next_tile=(tile + 1 if tile + 1 < 16 else None))

            u_phase(0)
            part_a(0)
            if first:
                sp_state.extend(spatial_setup())
            wmT, b_wmT, Bt, b_Bt, gcol, b_gcol = sp_state
            for tile in range(16):
                part_b(tile)
                if tile + 1 < 16:
                    if (tile + 1) % 4 == 0:
                        u_phase((tile + 1) // 4)
                    part_a(tile + 1)
                part_c(tile)
            epi.flush()

        for pi, p_ in enumerate(passes):
            run_pass(*p_, first=(pi == 0))
        epi.flush()
        cx.P.barrier()
    cx.es = None


def stage_ffn(cx, consts, banks, din, li, passes, in_bufs=(), out_bufs=None):
    tail_out = None
    w_up = din["f_w_up"][li]
    w_down = din["f_w_down"][li]
    with contextlib.ExitStack() as es:
        cx.es = es
        big, b_wd = cx.sb([128, 22 * 1024], BF16, "wd")
        wd = big[:, :].rearrange("p (f n) -> p f n", f=22)
        xte, b_xte = cx.sb([128, 8, 4, 258], BF16, "xte")
        hbuf, b_h = cx.sb([128, 22, 1024], BF16, "hbuf")
        halo, b_halo = cx.sb([128, 8, 16], BF16, "halo")
        cpar, b_cpar = cx.sb([44, 4, 128], F32, "cpar")
        for jt in range(3):
            cx.dma("sp", cpar[:, jt, :], din["f_conv_w"][li, jt].rearrange("(c p) -> c p", p=128), writes=[b_cpar])
        cx.dma("sp", cpar[:, 3, :], din["f_conv_b"][li].rearrange("(c p) -> c p", p=128), writes=[b_cpar])
        id32, b_id32 = cx.sb([44, 44], F32, "id32")
        cx.dma("sp", id32[:], din["ident32"][0:44, 0:44], writes=[b_id32])
        cwT, b_cw = cx.sb([128, 4, 44], F32, "cwT")
        b_cb = b_cw
        for jt in range(4):
            cx.tr(banks.t[0][:, jt * 44:(jt + 1) * 44], cpar[:, jt, :], id32[:], [b_cpar, b_id32], [banks.b[0]])
        cx.copy("dve", cwT[:].rearrange("p j c -> p (j c)"), banks.t[0][:, 0:176], [banks.b[0]], [b_cw])
        epi = Epi(cx, consts, banks, din["ln_ffn_g"][li:li + 1, :], din["ln_ffn_b"][li:li + 1, :])
        wup = [cx.sb([128, 8, 2, 256], BF16, "wup") for _ in range(2)]
        stager = Stager(cx)
        tmp = [[cx.sb([128, 2, 256], F32, "ct") for _ in range(2)] for _ in range(3)]
        wd_loaded = False
        nk = 0
        tail_fn = None
        nhalves = 2 * len(passes)
        for ghf in range(nhalves):
            hf = ghf % 2
            xres_in, xT_in, halo_fn, xres_out, xT_out = passes[ghf // 2]
            if hf == 0:
                halo_fn(cx, halo, b_halo)
            for k in range(8):
                cx.dma("sp", xte[:, k, :, 2:258],
                       xT_in[k * 128:(k + 1) * 128, hf * 1024:(hf + 1) * 1024].rearrange("p (b t) -> p b t", b=4),
                       reads=in_bufs, writes=[b_xte])
            cx.copy("pool", xte[:, :, :, 0:2],
                    halo[:, :, hf * 8:(hf + 1) * 8].rearrange("p k (b t) -> p k b t", b=4), [b_halo], [b_xte])
            def wup_src(fg_, k):
                return w_up[k * 128:(k + 1) * 128, :].rearrange("p (g f) -> p g f", g=2)[:, :, fg_ * 256:(fg_ + 1) * 256]

            def load_wup(fg_, hf_):
                wt_, b_w_ = wup[(hf_ * 11 + fg_) % 2]
                for k in range(8):
                    stager.load(cx, wt_[:, k, :, :], wup_src(fg_, k), b_w_, shape2=(2, 256), eng=WUP_CAST[k])
                if hf_ == 0:
                    for f in (2 * fg_, 2 * fg_ + 1):
                        stager.load(cx, wd[:, f, :], w_down[f * 128:(f + 1) * 128, :], b_wd, eng="act")

            class Pref:
                def __init__(self, fg_, hf_, with_wd):
                    self.fg_, self.hf_ = fg_, hf_
                    self.wt_, self.b_w_ = wup[(hf_ * 11 + fg_) % 2]
                    self.st = {}
                    self.wd = [2 * fg_, 2 * fg_ + 1] if with_wd else []

                def dma(self, k):
                    self.st[k] = stager.dma(cx, wup_src(self.fg_, k), shape2=(2, 256))

                def cast(self, k):
                    view, b = self.st.pop(k)
                    cx.copy(WUP_CAST[k], self.wt_[:, k, :, :], view, [b], [self.b_w_])

                def wd_dma(self, i):
                    f = self.wd[i]
                    self.st[("wd", i)] = stager.dma(cx, w_down[f * 128:(f + 1) * 128, :], n=1024)

                def wd_cast(self, i):
                    f = self.wd[i]
                    view, b = self.st.pop(("wd", i))
                    cx.copy("act", wd[:, f, :], view, [b], [b_wd])

                def step(self, g):
                    if g == -1:
                        for k in range(4):
                            self.dma(k)
                    elif g == 0:
                        for k in (0, 1, 2):
                            self.cast(k)
                        for k in (4, 5, 6):
                            self.dma(k)
                    elif g == 1:
                        for k in (3, 4, 5):
                            self.cast(k)
                        self.dma(7)
                        if self.wd:
                            self.wd_dma(0)
                    elif g == 2:
                        for k in (6, 7):
                            self.cast(k)
                        if self.wd:
                            self.wd_cast(0)
                            self.wd_dma(1)
                    elif g == 3:
                        if self.wd:
                            self.wd_cast(1)

            if ghf == 0:
                load_wup(0, 0)
            for fg in range(11):
                wt, b_w = wup[(ghf * 11 + fg) % 2]
                pref = None
                if fg + 1 < 11:
                    pref = Pref(fg + 1, ghf, ghf == 0)
                elif ghf + 1 < nhalves:
                    pref = Pref(0, ghf + 1, False)
                if pref is not None:
                    pref.step(-1)
                gi = 0
                for f2 in range(2):
                    fc = fg * 2 + f2
                    for bp in range(2):
                        kk = nk % 2
                        nk += 1
                        pg, bpg = banks.pair(2 * kk)
                        pv, bpv = banks.pair(2 * kk + 1)
                        for bl in range(2):
                            for k in range(8):
                                cx.mm(pg[:, bl, 0:258], wt[:, k, 0, f2 * 128:(f2 + 1) * 128], xte[:, k, 2 * bp + bl, :],
                                      k == 0, k == 7, [b_w, b_xte], [bpg[bl]])
                        for bl in range(2):
                            for k in range(8):
                                cx.mm(pv[:, bl, 0:258], wt[:, k, 1, f2 * 128:(f2 + 1) * 128], xte[:, k, 2 * bp + bl, :],
                                      k == 0, k == 7, [b_w, b_xte], [bpv[bl]])
                        (g0, bg0), (v0, bv0) = tmp[nk % 3]
                        cg = fc
                        cv = 22 + fc
                        cx.act(g0[:], pg[:, :, 0:256], AF.Identity, bpg + [b_cw, b_cb], [bg0],
                               bias=cwT[:, 3, cg:cg + 1], scale=cwT[:, 0, cg:cg + 1])
                        cx.act(v0[:], pv[:, :, 0:256], AF.Identity, bpv + [b_cw, b_cb], [bv0],
                               bias=cwT[:, 3, cv:cv + 1], scale=cwT[:, 0, cv:cv + 1])
                        if tail_fn is not None:
                            tail_fn()
                        cx.stt(g0[:], pg[:, :, 1:257], cwT[:, 1, cg:cg + 1], g0[:], ALU.mult, ALU.add, bpg + [b_cw, bg0], [bg0])
                        cx.stt(v0[:], pv[:, :, 1:257], cwT[:, 1, cv:cv + 1], v0[:], ALU.mult, ALU.add, bpv + [b_cw, bv0], [bv0])
                        cx.stt(g0[:], pg[:, :, 2:258], cwT[:, 2, cg:cg + 1], g0[:], ALU.mult, ALU.add, bpg + [b_cw, bg0], [bg0])
                        cx.stt(v0[:], pv[:, :, 2:258], cwT[:, 2, cv:cv + 1], v0[:], ALU.mult, ALU.add, bpv + [b_cw, bv0], [bv0])

                        def tail_fn(g0=g0, bg0=bg0, v0=v0, bv0=bv0, fc=fc, bp=bp):
                            cx.act(g0[:], g0[:], AF.Gelu_apprx_tanh, [bg0], [bg0])
                            cx.tt("pool", hbuf[:, fc, bp * 512:(bp + 1) * 512].rearrange("p (b t) -> p b t", b=2),
                                  g0[:], v0[:], ALU.mult, [bg0, bv0], [b_h])
                        if pref is not None:
                            pref.step(gi)
                        gi += 1
            tail_fn()
            tail_fn = None
            for tl_ in range(8):
                tile = hf * 8 + tl_
                yb = 2 * (tile % 2)
                for h2 in range(2):
                    for fc in range(22):
                        cx.mm(banks.t[yb + h2][:, :], hbuf[:, fc, tl_ * 128:(tl_ + 1) * 128],
                              wd[:, fc, h2 * 512:(h2 + 1) * 512], fc == 0, fc == 21, [b_h, b_wd], [banks.b[yb + h2]])
                epi.run(tile, banks.t[yb][:, :], banks.t[yb + 1][:, :], [banks.b[yb], banks.b[yb + 1]],
                        xres_in, xres_out, xT_out, tail_out, out_bufs, next_tile=(tile + 1 if tl_ + 1 < 8 else None))
        epi.flush()
        cx.P.barrier()
    cx.es = None


def stage_attproj(cx, consts, banks, din, j, passes, in_bufs=(), out_bufs=None):
    wqkv = din["b_w_qkv"][j]
    with contextlib.ExitStack() as es:
        cx.es = es
        xTs = []
        for pi, p_ in enumerate(passes):
            xT_, b_xT_ = cx.sb([128, 8, NT], BF16, "xT")
            xTs.append((xT_, b_xT_))
            if pi == 0:
                for c in range(8):
                    cx.dma("sp", xT_[:, c, :], p_[0][c * 128:(c + 1) * 128, :], reads=in_bufs, writes=[b_xT_])
        stager = Stager(cx)
        w, _ = cx.sb([128, 8, 3072], BF16, "wqkv")
        wb_w = load_w_bf16(cx, stager, w, wqkv, 8, 0, 3072, maxc=512)
        for pi, p_ in enumerate(passes):
            if pi > 0:
                for c in range(8):
                    cx.dma("sp", xTs[pi][0][:, c, :], p_[0][c * 128:(c + 1) * 128, :], reads=in_bufs,
                           writes=[xTs[pi][1]])
        stg = [cx.sb([128, 512], BF16, "stg") for _ in range(4)]
        kbss = [cx.sb([128, 8], F32, "kbs") for _ in range(2)]
        n = 0
        for pi, (xT_in, qT_out, kT_out, v_out, kbar_out) in enumerate(passes):
          xT, b_xT = xTs[pi]
          for fch in range(16):
              dst = qT_out if fch < 8 else kT_out
              r0 = (fch % 8) * 128
              for tg in range(4):
                  kk = n % 2
                  bank, bb = banks.t[kk], banks.b[kk]
                  for k in range(8):
                      cx.mm(bank[:, :], w[:, k, fch * 128:(fch + 1) * 128], xT[:, k, tg * 512:(tg + 1) * 512],
                            k == 0, k == 7, wb_w.get(fch * 128, (fch + 1) * 128) + [b_xT], [bb])
                  st_, b_st = stg[n % 4]
                  if fch >= 8:
                      kbs, b_kbs = kbss[fch % 2]
                      cx.P.op("dve", lambda e, a=kbs, b=bank, t=tg: e.tensor_reduce(
                          out=a[:, 2 * t:2 * t + 2], in_=b[:, :].rearrange("p (b t) -> p b t", b=2),
                          axis=AX.X, op=ALU.add), reads=[bb], writes=[b_kbs])
                      if tg == 3:
                          cx.ts("dve", kbs[:], kbs[:], 1.0 / 256.0, None, ALU.mult, None, [b_kbs], [b_kbs])
                          o = cx.dma("sp", kbar_out[r0:r0 + 128, :], kbs[:], reads=[b_kbs], writes=out_bufs or ())
                          cx.outs.append(o)
                  if n % 2 == 0 or fch >= 8:
                      cx.act(st_[:], bank[:, :], AF.Copy, [bb] + ([kbss[fch % 2][1]] if fch >= 8 else []), [b_st],
                             scale=(0.125 if fch < 8 else 1.0))
                  else:
                      cx.ts("dve", st_[:], bank[:, :], (0.125 if fch < 8 else 1.0), None, ALU.mult, None, [bb], [b_st])
                  o = cx.dma("sp", dst[r0:r0 + 128, tg * 512:(tg + 1) * 512], st_[:], reads=[b_st],
                             writes=out_bufs or ())
                  cx.outs.append(o)
                  n += 1
          for tile in range(16):
              for hf in range(2):
                  kk = n % 2
                  bank, bb = banks.t[kk], banks.b[kk]
                  for k in range(8):
                      cx.mm(bank[:, :], xT[:, k, tile * 128:(tile + 1) * 128],
                            w[:, k, 2048 + hf * 512:2048 + (hf + 1) * 512], k == 0, k == 7,
                            [b_xT] + wb_w.get(2048 + hf * 512, 2048 + (hf + 1) * 512), [bb])
                  st_, b_st = stg[n % 4]
                  if n % 2 == 0:
                      cx.act(st_[:], bank[:, :], AF.Copy, [bb], [b_st])
                  else:
                      cx.copy("dve", st_[:], bank[:, :], [bb], [b_st])
                  o = cx.dma("sp", v_out[tile * 128:(tile + 1) * 128, hf * 512:(hf + 1) * 512], st_[:],
                             reads=[b_st], writes=out_bufs or ())
                  cx.outs.append(o)
                  n += 1
        cx.P.barrier()
    cx.es = None


def stage_attcore(cx, consts, banks, din, j, li, xres_in, qT_in, kT_all, v_all, kbar_all, bvd, xres_out, xT_out,
                  tail_out, in_bufs=(), out_bufs=None, qsel=(0, 1)):
    QW = 128 * len(qsel)
    q0 = qsel[0] * 128
    wo_d = din["b_w_o"][j]
    with contextlib.ExitStack() as es:
        cx.es = es
        rb, b_rb = cx.sb([33, 16], F32, "rb")
        cx.dma("sp", rb[0:32, :], din["rel_bias"], writes=[b_rb])
        cx.dma("sp", rb[32:33, :], din["ones16"], writes=[b_rb])
        oh, b_oh = cx.sb([33, 2048], F32, "oh")
        cx.dma("sp", oh[:], din["oh"], writes=[b_oh])
        bvs, b_bvs = cx.sb([16, 2048], BF16, "bvs")
        for hf in range(4):
            cx.mm(banks.t[hf][0:16, :], rb[:, :], oh[:, hf * 512:(hf + 1) * 512], True, True, [b_rb, b_oh],
                  [banks.b[hf]])
            cx.copy("dve", bvs[:, hf * 512:(hf + 1) * 512], banks.t[hf][0:16, :], [banks.b[hf]], [b_bvs])
        b_bvd = Buf("bvd")
        cx.dma("sp", bvd.ap(), bvs[:], reads=[b_bvs], writes=[b_bvd])
        chm, b_chm = cx.sb([128, 16], F32, "chm")
        cx.dma("sp", chm[:], bcast_rows(din["rel_bias"][31:32, :]), writes=[b_chm])
        gmask, b_gm = cx.sb([128, 16, 16], F32, "gmask")
        oof, b_oof = cx.sb([128, 16, 16], F32, "oof")
        farm, b_farm = cx.sb([128, 16, 16], F32, "farm")
        for i in range(8):
            for q in range(2):
                cx.dma("sp", gmask[:, 2 * i + q, :], din["gmask"][:, i, :], writes=[b_gm])
                cx.dma("sp", oof[:, 2 * i + q, :], din["oof"][:, i, :], writes=[b_oof])
        cx.memset("pool", farm[:], 0.0, [b_farm])
        for i in range(2, 8):
            cx.memset("pool", farm[:, 2 * i:2 * i + 2, 0:2 * i - 2], 1.0, [b_farm])
        jm, b_jm = cx.sb([128, 128], BF16, "jm")
        cx.dma("sp", jm[:], din["jm"], writes=[b_jm])
        stager = Stager(cx)
        wo, _ = cx.sb([128, 8, 1024], BF16, "wo")
        osb, b_osb = cx.sb([128, 16, D], BF16, "osb")
        kta = [cx.sb([80, 4096], BF16, "kta") for _ in range(2)]
        va = [cx.sb([128, 32, 65], BF16, "va") for _ in range(2)]
        qta = [cx.sb([80, NT], BF16, "qta") for _ in range(2)]
        tp = [cx.sb([128, 8, 256], BF16, "tp") for _ in range(2)]
        for k in range(2):
            cx.dma("sp", kta[k][0][64:80, :], din["koh"], writes=[kta[k][1]])
            cx.memset("pool", va[k][0][:, :, 64:65], 1.0, [va[k][1]])
        kb32s = [cx.sb([64, 16], F32, "kb32") for _ in range(2)]
        kbar = [cx.sb([64, 16], BF16, "kbar") for _ in range(2)]
        mpad, b_mp = cx.sb([128, 16, 80], BF16, "mpad")
        cx.memset("pool", mpad[:], 0.0, [b_mp])
        gm, b_g = cx.sb([128, 16, 16], F32, "gm")
        top8, b_t8 = cx.sb([128, 16, 8], F32, "top8")
        keep, b_kp = cx.sb([128, 16, 16], F32, "keep")
        ebuf = [cx.sb([128, 512], BF16, "ebuf") for _ in range(4)]
        rden = [cx.sb([128, 1], F32, "rden") for _ in range(4)]
        epi = Epi(cx, consts, banks, din["ln_mix_g"][li:li + 1, :], din["ln_mix_b"][li:li + 1, :])
        g7 = banks.t[7]

        def loads(h):
            hk = h % 2
            kt_, b_kt = kta[hk]
            va_, b_va = va[hk]
            qt_, b_qt = qta[hk]
            tp_, b_tp = tp[hk]
            for r in range(2):
                cx.dma("sp", kt_[0:64, :].rearrange("d (i r t) -> d i r t", i=8, r=2)[:, :, r, :],
                       kT_all[r, h * 64:(h + 1) * 64, :].rearrange("d (i t) -> d i t", i=8),
                       reads=in_bufs, writes=[b_kt])
            cx.dma("sp", qt_[0:64, :], qT_in[h * 64:(h + 1) * 64, :], reads=in_bufs, writes=[b_qt])
            kb32_, b_kb32_ = kb32s[hk]
            for r in range(2):
                cx.dma("sp", kb32_[:, :].rearrange("d (i r) -> d i r", r=2)[:, :, r], kbar_all[r, h * 64:(h + 1) * 64, :],
                       reads=in_bufs, writes=[b_kb32_], nonc=True)
            for r in range(2):
                for s_ in range(2):
                    cx.dma("sp", va_[:, :, 0:64].rearrange("p (i r s) d -> p i r s d", i=8, r=2)[:, :, r, s_, :],
                           v_all[r, :, h * 64:(h + 1) * 64].rearrange("(i s p) d -> p i s d", i=8, s=2)[:, :, s_, :],
                           reads=in_bufs, writes=[b_va], nonc=True)
            for rel in (-2, -1, 0, 1):
                for kt in range(2):
                    m0 = (rel + 2) * 512 + 128 * (1 - kt)
                    src = bass.AP(tensor=bvd, offset=h * 2048 + m0, ap=[[1, 128], [1, 256]])
                    cx.dma("sp", tp_[:, (rel + 2) * 2 + kt, :], src, reads=[b_bvd], writes=[b_tp])

        def gate1(h):
            hk = h % 2
            kt_, b_kt = kta[hk]
            qt_, b_qt = qta[hk]
            kbt, b_kb = kbar[hk]
            kb32_, b_kb32_ = kb32s[hk]
            cx.copy("dve", kbt[:], kb32_[:], [b_kb32_], [b_kb])

        def gate1b(h):
            hk = h % 2
            qt_, b_qt = qta[hk]
            kbt, b_kb = kbar[hk]
            for qc in range(16):
                cx.mm(g7[:, qc * 16:(qc + 1) * 16], qt_[0:64, qc * 128:(qc + 1) * 128], kbt[:, :], True, True,
                      [b_qt, b_kb], [banks.b[7]])
            cx.tt("dve", gm[:], g7[:, 0:256].rearrange("p (c n) -> p c n", c=16), gmask[:], ALU.add,
                  [banks.b[7], b_gm], [b_g])
            for qc in range(16):
                cx.P.op("dve", lambda e, c=qc: e.max(out=top8[:, c, :], in_=gm[:, c, :]), reads=[b_g], writes=[b_t8])
            cx.tt("dve", keep[:], gm[:], top8[:, :, 2:3].to_broadcast([128, 16, 16]), ALU.is_ge, [b_g, b_t8], [b_kp])
            cx.tt("dve", keep[:], keep[:], oof[:], ALU.max, [b_kp, b_oof], [b_kp])
            cx.ts("dve", keep[:], keep[:], -NEG, NEG, ALU.mult, ALU.add, [b_kp], [b_kp])
            cx.stt(mpad[:, :, 64:80], farm[:], chm[:, h:h + 1], keep[:], ALU.mult, ALU.add,
                   [b_farm, b_chm, b_kp], [b_mp])

        def gate2(h):
            hk = h % 2
            qt_, b_qt = qta[hk]
            for half in range(2):
                for c in range(8):
                    qc = half * 8 + c
                    cx.tr(banks.tT[0:80, c * 128:(c + 1) * 128], mpad[:, qc, :], consts.ident[:],
                          [b_mp, consts.b_ident], [banks.bT])
                cx.copy("dve", qt_[64:80, half * 1024:(half + 1) * 1024], banks.tT[64:80, :], [banks.bT], [b_qt])

        def main(h, hooks):
            hk = h % 2
            kt_, b_kt = kta[hk]
            va_, b_va = va[hk]
            qt_, b_qt = qta[hk]
            tp_, b_tp = tp[hk]
            slots = [(i, jb) for i in range(NBLK) for jb in range(2 * i + 2)]
            L = 2
            ebs = {}
            for t in range(len(slots) + L):
                if t < len(slots):
                    i, jb = slots[t]
                    if jb == 0 and i in hooks:
                        hooks[i]()
                    rel = jb - 2 * i
                    near = rel >= -2
                    sbk, bsb_ = banks.t[t % 3], banks.b[t % 3]
                    for kt in range(2):
                        gk = 2 * jb + kt
                        cx.mm(sbk[:, kt * QW:(kt + 1) * QW], kt_[0:80, gk * 128:(gk + 1) * 128],
                              qt_[0:80, i * 256 + q0:i * 256 + q0 + QW], True, not near, [b_kt, b_qt], [bsb_])
                        if near:
                            cx.mm(sbk[:, kt * QW:(kt + 1) * QW], jm[:, :],
                                  tp_[:, (rel + 2) * 2 + kt, q0:q0 + QW], False, True, [b_jm, b_tp], [bsb_])
                    eb, b_eb = ebuf[t % 4]
                    cx.act(eb[:, 0:2 * QW], sbk[:, 0:2 * QW], AF.Exp, [bsb_], [b_eb])
                    ebs[t] = (eb, b_eb)
                if t - L >= 0:
                    i, jb = slots[t - L]
                    eb, b_eb = ebs.pop(t - L)
                    nj = 2 * i + 2
                    ob = [(banks.t[3 + 2 * (i % 2) + q], banks.b[3 + 2 * (i % 2) + q]) for q in range(2)]
                    for q in qsel:
                        for kt in range(2):
                            gk = 2 * jb + kt
                            qo = (q - qsel[0]) * 128
                            cx.mm(ob[q][0][:, 0:65], eb[:, kt * QW + qo:kt * QW + qo + 128],
                                  va_[:, gk, :], jb == 0 and kt == 0, jb == nj - 1 and kt == 1,
                                  [b_eb, b_va], [ob[q][1]])
                    if jb == nj - 1:
                        for q in qsel:
                            rd, b_rd = rden[2 * (i % 2) + q]
                            cx.P.op("dve", lambda e, a=rd, b=ob[q][0]: e.reciprocal(out=a[:], in_=b[:, 64:65]),
                                    reads=[ob[q][1]], writes=[b_rd])
                            cx.ts("dve", osb[:, 2 * i + q, h * 64:(h + 1) * 64], ob[q][0][:, 0:64], rd[:, 0:1], None,
                                  ALU.mult, None, [ob[q][1], b_rd], [b_osb])

        loads(0)
        gate1(0)
        gate1b(0)
        gate2(0)
        wb_wo = load_w_bf16(cx, stager, wo, wo_d, 8, 0, 1024)
        for h in range(H):
            hooks = {}
            if h + 1 < H:
                loads(h + 1)
                hooks[3] = (lambda hh=h + 1: (gate1(hh), gate1b(hh)))
                hooks[6] = (lambda hh=h + 1: gate2(hh))
            main(h, hooks)
        ot = [cx.sb([128, 8, 128], BF16, "ot") for _ in range(3)]
        tiles = [t for t in range(16) if t % 2 in qsel]

        def prep(n):
            tile = tiles[n]
            otl, b_ot = ot[n % 3]
            for c in range(8):
                cx.tr(banks.tT[:, c * 128:(c + 1) * 128], osb[:, tile, c * 128:(c + 1) * 128], consts.ident[:],
                      [b_osb, consts.b_ident], [banks.bT])
            cx.copy("act", otl[:].rearrange("p c t -> p (c t)"), banks.tT[:, :], [banks.bT], [b_ot])

        prep(0)
        for n, tile in enumerate(tiles):
            otl, b_ot = ot[n % 3]
            yb = 2 * (n % 2)
            for hf in range(2):
                for c in range(8):
                    cx.mm(banks.t[yb + hf][:, :], otl[:, c, :], wo[:, c, hf * 512:(hf + 1) * 512], c == 0, c == 7,
                          [b_ot] + wb_wo.get(hf * 512, (hf + 1) * 512), [banks.b[yb + hf]])
            if n + 1 < len(tiles):
                prep(n + 1)
            epi.run(tile, banks.t[yb][:, :], banks.t[yb + 1][:, :], [banks.b[yb], banks.b[yb + 1]],
                    xres_in, xres_out, xT_out, tail_out, out_bufs,
                    next_tile=(tiles[n + 1] if n + 1 < len(tiles) else None))
        epi.flush()
        cx.P.barrier()
    cx.es = None


PARAM_SHAPES = {
    "ln_mix_g": (DEPTH, D), "ln_mix_b": (DEPTH, D), "ln_ffn_g": (DEPTH, D), "ln_ffn_b": (DEPTH, D),
    "a_w_in": (2, D, 2048), "a_ln_g": (2, D), "a_ln_b": (2, D), "a_w_s": (2, 8, 128, 128),
    "a_b_s": (2, 8, 128), "a_w_out": (2, D, D), "b_w_qkv": (2, D, 3072), "b_w_o": (2, D, D),
    "rel_bias": (32, 16), "f_w_up": (DEPTH, D, 2 * FF), "f_conv_w": (DEPTH, 3, 2 * FF),
    "f_conv_b": (DEPTH, 2 * FF), "f_w_down": (DEPTH, FF, D),
}
CONST_SHAPES = {
    "ident": ((128, 128), BF16), "jm": ((128, 128), BF16), "tril": ((128, 8, 128), BF16),
    "gmask": ((128, 8, 16), F32), "oof": ((128, 8, 16), F32), "oh": ((33, 2048), F32),
    "koh": ((16, 4096), BF16), "ones16": ((1, 16), F32), "ident32": ((128, 128), F32),
}
STAGE_PARAMS = {
    "pro": ["ident"],
    "gmlp": ["ln_mix_g", "ln_mix_b", "a_w_in", "a_ln_g", "a_ln_b", "a_w_s", "a_b_s", "a_w_out", "ident", "tril"],
    "ffn": ["ln_ffn_g", "ln_ffn_b", "f_w_up", "f_conv_w", "f_conv_b", "f_w_down", "ident", "ident32"],
    "attproj": ["b_w_qkv", "ident"],
    "attcore": ["ln_mix_g", "ln_mix_b", "b_w_o", "rel_bias", "ident", "jm", "gmask", "oof", "oh", "koh", "ones16"],
}


def declare(nc, names, single_layer):
    din = {}
    for n in names:
        if n in PARAM_SHAPES:
            shp = list(PARAM_SHAPES[n])
            if single_layer and n != "rel_bias":
                shp[0] = 1
            din[n] = nc.dram_tensor(n, shp, F32, kind="ExternalInput").ap()
        else:
            shp, dt = CONST_SHAPES[n]
            din[n] = nc.dram_tensor(n, list(shp), dt, kind="ExternalInput").ap()
    return din


def rel_bucket_np(dist):
    n = np.maximum(dist, 0)
    nf = np.maximum(n, 1).astype(np.float32)
    large = 16 + (np.log(nf / np.float32(16)) / np.float32(math.log(8)) * np.float32(16)).astype(np.int32)
    large = np.minimum(large, 31)
    return np.where(n < 16, n, large)


def host_consts(hA, hX):
    c = {}
    c["ident"] = np.eye(128, dtype=np.float32).astype(NPBF)
    c["jm"] = np.eye(128, dtype=np.float32)[::-1].copy().astype(NPBF)
    tr = np.tril(np.ones((128, 128), np.float32))
    c["tril"] = np.ascontiguousarray(np.broadcast_to(tr[:, None, :], (128, 8, 128))).astype(NPBF)
    hs = (hA, 1 - hA)
    gslot = np.array([2 * (s_ // 2) + hs[s_ % 2] for s_ in range(16)])
    gm = np.zeros((8, 16), np.float32)
    oo = np.zeros((8, 16), np.float32)
    for i in range(8):
        G = 2 * i + hX
        gm[i, gslot >= G] = -1e30
        oo[i, gslot >= G] = 1.0
    c["gmask"] = np.ascontiguousarray(np.broadcast_to(gm[None], (128, 8, 16)))
    c["oof"] = np.ascontiguousarray(np.broadcast_to(oo[None], (128, 8, 16)))
    oh = np.zeros((33, 2048), np.float32)
    m = np.arange(512)
    for ri, rel in enumerate((-2, -1, 0, 1)):
        sig = rel % 2
        bd = (2 if rel < 0 else 0) + hX - hs[sig]
        dist = m - 255 + 256 * bd
        bk = rel_bucket_np(dist)
        ok = dist >= 0
        oh[bk[ok], ri * 512 + m[ok]] = 1.0
        oh[32, ri * 512 + m[~ok]] = NEG
    c["oh"] = oh
    koh = np.zeros((16, 4096), np.float32)
    for n in range(16):
        koh[n, n * 256:(n + 1) * 256] = 1.0
    c["koh"] = koh.astype(NPBF)
    c["ones16"] = np.ones((1, 16), np.float32)
    c["ident32"] = np.eye(128, dtype=np.float32)
    fl = np.zeros((128, 2), np.float32)
    fl[:, hX] = 1.0
    c["hflag"] = fl
    return c


_PROGS = {}


def build_unfused(kind):
    if kind in _PROGS:
        return _PROGS[kind]
    nc = bass.Bass("TRN2", target_bir_lowering=False)
    din = declare(nc, STAGE_PARAMS[kind], True)

    def ext_in(name, shape, dt):
        return nc.dram_tensor(name, list(shape), dt, kind="ExternalInput").ap()

    def ext_out(name, shape, dt):
        return nc.dram_tensor(name, list(shape), dt, kind="ExternalOutput").ap()

    cx = Cx(nc)
    with contextlib.ExitStack() as top:
        cx.es = top
        consts = load_consts(cx, din)
        banks = Banks(cx)
        if kind == "pro":
            x_in = ext_in("xres_in", [NT, D], F32)
            stage_prologue(cx, consts, banks, x_in, ext_out("xT_out", [D, NT], BF16),
                           ext_out("tail_out", [D, 16], BF16))
        elif kind == "gmlp":
            stage_gmlp(cx, consts, banks, din, 0, 0, [(ext_in("xres_in", [NT, D], F32),
                       ext_in("xT_in", [D, NT], BF16), ext_out("xres_out", [NT, D], F32),
                       ext_out("xT_out", [D, NT], BF16), ext_out("tail_out", [D, 16], BF16))])
        elif kind == "ffn":
            halo_in = ext_in("halo_in", [D, 16], BF16)

            def halo_fn(cx_, halo, b_halo):
                cx_.dma("sp", halo[:], halo_in.rearrange("(k p) t -> p k t", p=128), writes=[b_halo], nonc=True)

            ext_out("tail_out", [D, 16], BF16)
            stage_ffn(cx, consts, banks, din, 0, [(ext_in("xres_in", [NT, D], F32), ext_in("xT_in", [D, NT], BF16),
                      halo_fn, ext_out("xres_out", [NT, D], F32), ext_out("xT_out", [D, NT], BF16))])
        elif kind == "attproj":
            stage_attproj(cx, consts, banks, din, 0, [(ext_in("xT_in", [D, NT], BF16),
                          ext_out("qT_out", [D, NT], BF16), ext_out("kT_out", [D, NT], BF16),
                          ext_out("v_out", [NT, D], BF16), ext_out("kbar_out", [D, 8], F32))])
        elif kind == "attcore":
            bvd = nc.dram_tensor("bvd", [16, 2048], BF16, kind="Internal")
            stage_attcore(cx, consts, banks, din, 0, 0, ext_in("xres_in", [NT, D], F32),
                          ext_in("qT_in", [D, NT], BF16), ext_in("kT_all", [2, D, NT], BF16),
                          ext_in("v_all", [2, NT, D], BF16), ext_in("kbar_all", [2, D, 8], F32), bvd,
                          ext_out("xres_out", [NT, D], F32),
                          ext_out("xT_out", [D, NT], BF16), ext_out("tail_out", [D, 16], BF16))
        cx.P.finish(cx.outs)
        cx.P.emit_all()
    _PROGS[kind] = nc
    return nc


def run_stage(kind, in_maps, cores):
    nc = build_unfused(kind)
    res = run_bass_kernel_spmd(nc, in_maps, core_ids=list(range(len(cores))))
    return res.results


LAYER_PARAM_IDX = {
    "gmlp": lambda li: {"ln_mix_g": li, "ln_mix_b": li, "a_w_in": li // 2, "a_ln_g": li // 2, "a_ln_b": li // 2,
                        "a_w_s": li // 2, "a_b_s": li // 2, "a_w_out": li // 2},
    "ffn": lambda li: {"ln_ffn_g": li, "ln_ffn_b": li, "f_w_up": li, "f_conv_w": li, "f_conv_b": li,
                       "f_w_down": li},
    "attproj": lambda li: {"b_w_qkv": li // 2},
    "attcore": lambda li: {"ln_mix_g": li, "ln_mix_b": li, "b_w_o": li // 2},
}


def stage_inputs(kind, li, params, consts_c):
    m = {}
    idx = LAYER_PARAM_IDX.get(kind, lambda li: {})(li)
    for n in STAGE_PARAMS[kind]:
        if n in idx:
            m[n] = np.ascontiguousarray(params[n][idx[n]:idx[n] + 1])
        elif n == "rel_bias":
            m[n] = params[n]
        else:
            m[n] = consts_c[n]
    return m


def to_local(x):
    B = x.shape[0]
    xb = x.reshape(B, 16, 256, D)
    return [np.ascontiguousarray(xb[c // 2, (c % 2)::2].reshape(NT, D)) for c in range(2 * B)]


def from_local(outs, B):
    y = np.zeros((B, 16, 256, D), np.float32)
    for c in range(2 * B):
        y[c // 2, (c % 2)::2] = outs[c].reshape(8, 256, D)
    return y.reshape(B, 4096, D)


def make_halo(tails, c):
    half = c % 2
    pt = tails[c ^ 1]
    halo = np.zeros((D, 16), NPBF)
    if half == 0:
        halo[:, 2:16] = pt[:, 0:14]
    else:
        halo[:, :] = pt
    return halo


def forward_unfused(x, params, ncores=8, nlayers=DEPTH, debug=None):
    cores = list(range(ncores))
    cc = [host_consts(c % 2, c % 2) for c in cores]
    xl = to_local(np.asarray(x, np.float32)[: ncores // 2])
    r = run_stage("pro", [dict(xres_in=xl[c], **stage_inputs("pro", 0, params, cc[c])) for c in cores], cores)
    xres = xl
    xT = [r[c]["xT_out"] for c in cores]
    tails = [r[c]["tail_out"] for c in cores]
    for li in range(nlayers):
        if li % 2 == 0:
            r = run_stage("gmlp", [dict(xres_in=xres[c], xT_in=xT[c], **stage_inputs("gmlp", li, params, cc[c]))
                                   for c in cores], cores)
        else:
            r = run_stage("attproj", [dict(xT_in=xT[c], **stage_inputs("attproj", li, params, cc[c]))
                                      for c in cores], cores)
            ims = []
            for c in cores:
                p0, p1 = c, c ^ 1
                ims.append(dict(xres_in=xres[c], qT_in=r[c]["qT_out"],
                                kT_all=np.stack([r[p0]["kT_out"], r[p1]["kT_out"]]),
                                v_all=np.stack([r[p0]["v_out"], r[p1]["v_out"]]),
                                kbar_all=np.stack([r[p0]["kbar_out"], r[p1]["kbar_out"]]),
                                **stage_inputs("attcore", li, params, cc[c])))
            r = run_stage("attcore", ims, cores)
        xres = [r[c]["xres_out"] for c in cores]
        xT = [r[c]["xT_out"] for c in cores]
        tails = [r[c]["tail_out"] for c in cores]
        if debug is not None:
            debug.append(("mix%d" % li, from_local(xres, ncores // 2)))
        r = run_stage("ffn", [dict(xres_in=xres[c], xT_in=xT[c], halo_in=make_halo(tails, c),
                                   **stage_inputs("ffn", li, params, cc[c])) for c in cores], cores)
        xres = [r[c]["xres_out"] for c in cores]
        xT = [r[c]["xT_out"] for c in cores]
        tails = [r[c]["tail_out"] for c in cores]
        if debug is not None:
            debug.append(("ffn%d" % li, from_local(xres, ncores // 2)))
    return from_local(xres, ncores // 2)


PASS_TABLES = ["gmask", "oof", "oh", "hflag"]
CONST_SHAPES["hflag"] = ((128, 2), F32)
COMMON_CONSTS = ["ident", "jm", "tril", "koh", "ones16", "ident32"]


def build_fused(nlayers=DEPTH):
    key = ("fused", nlayers)
    if key in _PROGS:
        return _PROGS[key]
    nc = bass.Bass("TRN2", target_bir_lowering=False)
    din = declare(nc, list(PARAM_SHAPES.keys()) + COMMON_CONSTS, False)
    dinx = {}
    for X in "AB":
        d = dict(din)
        for n in PASS_TABLES:
            shp, dt = CONST_SHAPES[n]
            d[n] = nc.dram_tensor("%s_%s" % (n, X), list(shp), dt, kind="ExternalInput").ap()
        dinx[X] = d

    def internal(name, shape, dt):
        return nc.dram_tensor(name, list(shape), dt, kind="Internal")

    x_in = {X: nc.dram_tensor("x_%s" % X, [NT, D], F32, kind="ExternalInput").ap() for X in "AB"}
    out = nc.dram_tensor("out", [NT, D], F32, kind="ExternalOutput").ap()
    xres = {X: [internal("xres_%s%d" % (X, k), [NT, D], F32).ap() for k in range(2)] for X in "AB"}
    xTb = {X: [internal("xT_%s%d" % (X, k), [D, NT], BF16).ap() for k in range(2)] for X in "AB"}
    tail = {X: internal("tail_%s" % X, [D, 16], BF16).ap() for X in "AB"}
    qT = {X: internal("qT_%s" % X, [D, NT], BF16).ap() for X in "AB"}
    kT_all = internal("kT_all", [2, D, NT], BF16).ap()
    v_all = internal("v_all", [2, NT, D], BF16).ap()
    kbar_all = internal("kbar_all", [2, D, 8], F32).ap()
    bvd = {X: internal("bvd_%s" % X, [16, 2048], BF16) for X in "AB"}
    sig = {"A": 0, "B": 1}
    other = {"A": "B", "B": "A"}
    cx = Cx(nc)
    with contextlib.ExitStack() as top:
        cx.es = top
        consts = load_consts(cx, din)
        banks = Banks(cx)
        cur = {}
        for X in "AB":
            stage_prologue(cx, consts, banks, x_in[X], xTb[X][0], tail[X])
            cur[X] = dict(xres=x_in[X], xT=xTb[X][0], k=0)

        def nxt(X, final=False):
            st = cur[X]
            k = st["k"]
            st["k"] ^= 1
            if final:
                return out, None
            return xres[X][k], xTb[X][k ^ 1]

        def halo_fn_for(X):
            def halo_fn(cx_, halo, b_halo):
                to, b_to = cx_.sb([128, 8, 16], BF16, "to")
                fl, b_fl = cx_.sb([128, 2], F32, "hfl")
                tmp, b_tmp = cx_.sb([128, 8, 16], F32, "htmp")
                cx_.dma("sp", to[:], tail[other[X]].rearrange("(k p) t -> p k t", p=128), writes=[b_to], nonc=True)
                cx_.dma("sp", fl[:], dinx[X]["hflag"], writes=[b_fl])
                cx_.ts("dve", tmp[:], to[:], fl[:, 1:2], None, ALU.mult, None, [b_to, b_fl], [b_tmp])
                cx_.copy("dve", halo[:, :, 0:2], tmp[:, :, 0:2], [b_tmp], [b_halo])
                cx_.stt(halo[:, :, 2:16], to[:, :, 0:14], fl[:, 0:1], tmp[:, :, 2:16], ALU.mult, ALU.add,
                        [b_to, b_fl, b_tmp], [b_halo])
            return halo_fn

        for li in range(nlayers):
            lastl = li == DEPTH - 1
            j = li // 2
            passes = "AB"
            if li % 2 == 0:
                pl = []
                for X in passes:
                    xo, xTo = nxt(X)
                    pl.append((cur[X]["xres"], cur[X]["xT"], xo, xTo, tail[X]))
                    cur[X]["xres"], cur[X]["xT"] = xo, xTo
                stage_gmlp(cx, consts, banks, din, j, li, pl)
            else:
                stage_attproj(cx, consts, banks, din, j,
                              [(cur[X]["xT"], qT[X], kT_all[sig[X]], v_all[sig[X]], kbar_all[sig[X]]) for X in passes])
                for X in "AB":
                    xo, xTo = nxt(X)
                    stage_attcore(cx, consts, banks, dinx[X], j, li, cur[X]["xres"], qT[X], kT_all, v_all, kbar_all,
                                  bvd[X], xo, xTo, tail[X], qsel=((1,) if (lastl and X == "B") else (0, 1)))
                    cur[X]["xres"], cur[X]["xT"] = xo, xTo
            pl = []
            cx.outs = []
            for X in ("A" if lastl else "AB"):
                xo, xTo = nxt(X, final=lastl)
                pl.append((cur[X]["xres"], cur[X]["xT"], halo_fn_for(X), xo, xTo))
                cur[X]["xres"], cur[X]["xT"] = xo, xTo
            stage_ffn(cx, consts, banks, din, li, pl)
        if nlayers < DEPTH:
            cx.es = top
            t, bt = cx.sb([128, D], F32, "dbg")
            cx.outs = []
            for tile in range(16):
                cx.dma("sp", t[:], cur["A"]["xres"][tile * 128:(tile + 1) * 128, :], writes=[bt])
                cx.outs.append(cx.dma("sp", out[tile * 128:(tile + 1) * 128, :], t[:], reads=[bt]))
        cx.P.finish(cx.outs)
        cx.P.emit_all()
    _PROGS[key] = nc
    return nc


def forward_fused(x, params, ncores=8, nlayers=DEPTH):
    nc = build_fused(nlayers)
    xl = to_local(np.asarray(x, np.float32)[: ncores // 2])
    in_maps = []
    for c in range(ncores):
        hA = c % 2
        m = {k: params[k] for k in PARAM_SHAPES}
        ca = host_consts(hA, hA)
        cb = host_consts(hA, 1 - hA)
        for n in COMMON_CONSTS:
            m[n] = ca[n]
        for n in PASS_TABLES:
            m[n + "_A"] = ca[n]
            m[n + "_B"] = cb[n]
        m["x_A"] = xl[c]
        m["x_B"] = xl[c ^ 1]
        in_maps.append(m)
    res = run_bass_kernel_spmd(nc, in_maps, core_ids=list(range(ncores)))
    return from_local([res.results[c]["out"] for c in range(ncores)], ncores // 2)


def kernel(**inputs):
    params = {k: np.ascontiguousarray(np.asarray(v, np.float32)) for k, v in inputs.items() if k != "x"}
    x = np.asarray(inputs["x"], np.float32)
    return forward_fused(x, params).astype(np.float32)
```

```python
import contextlib
import math
import numpy as np
import ml_dtypes
import concourse.bass as bass
import concourse.mybir as mybir
from concourse.bass_utils import run_bass_kernel_spmd

F32 = mybir.dt.float32
BF16 = mybir.dt.bfloat16
AF = mybir.ActivationFunctionType
ALU = mybir.AluOpType
AX = mybir.AxisListType
NPBF = ml_dtypes.bfloat16

D = 1024
DEPTH = 4
NT = 2048
NBLK = 8
H = 16
DH = 64
FF = 2816
ALPHA = (2 * DEPTH) ** 0.25
EPS = 1e-5
NEG = -30000.0

ENGS = ("pe", "act", "dve", "pool", "sp")
SAME_ENGINE_SYNC = True
N_DMA_SLOTS = 28
STORE_Q = "pool"


class Buf:
    __slots__ = ("name", "w", "r")

    def __init__(self, name=""):
        self.name = name
        self.w = None
        self.r = []


class Op:
    __slots__ = ("eng", "emit", "deps", "dma", "slot", "slot_val", "prev_val",
                 "signal", "count", "epoch")

    def __init__(self, eng, emit, dma):
        self.eng = eng
        self.emit = emit
        self.deps = set()
        self.dma = dma
        self.slot = None
        self.slot_val = 0
        self.prev_val = 0
        self.signal = False
        self.count = 0
        self.epoch = 0


class Prog:
    def __init__(self, nc):
        self.nc = nc
        self.ops = {e: [] for e in ENGS}
        self.slot_rr = {e: 0 for e in ENGS}
        self.slot_cum = {}
        self.pending_dma = []
        self.epoch = 0

    def op(self, eng, emit, reads=(), writes=(), dma=False):
        o = Op(eng, emit, dma)
        o.epoch = self.epoch
        for b in reads:
            if b.w is not None and b.w is not o:
                o.deps.add(b.w)
            b.r.append(o)
        for b in writes:
            if b.w is not None and b.w is not o:
                o.deps.add(b.w)
            for r in b.r:
                if r is not o:
                    o.deps.add(r)
            b.w = o
            b.r = []
        if dma:
            k = self.slot_rr[eng]
            self.slot_rr[eng] = (k + 1) % N_DMA_SLOTS
            key = (eng, k)
            prev = self.slot_cum.get(key, 0)
            o.slot = key
            o.prev_val = prev
            o.slot_val = prev + 16
            self.slot_cum[key] = o.slot_val
            self.pending_dma.append(o)
        self.ops[eng].append(o)
        return o

    def barrier(self):
        lasts = []
        for e in ENGS:
            for o in reversed(self.ops[e]):
                if not o.dma and o.emit is not None:
                    lasts.append(o)
                    break
        pend = list(self.pending_dma)
        self.pending_dma = []
        for e in ENGS:
            o = Op(e, None, False)
            o.epoch = self.epoch
            o.deps.update(lasts)
            o.deps.update(pend)
            self.ops[e].append(o)
        self.nbar = getattr(self, "nbar", 0) + 1
        self.epoch = self.nbar // 3

    def finish(self, out_ops):
        o = Op("sp", None, False)
        o.epoch = self.epoch
        o.deps.update(out_ops)
        self.ops["sp"].append(o)

    def emit_all(self):
        nc = self.nc
        for e in ENGS:
            for o in self.ops[e]:
                for d in o.deps:
                    if d.dma:
                        continue
                    if d.eng == o.eng and (d.eng == "pe" or not SAME_ENGINE_SYNC):
                        continue
                    d.signal = True
        for e in ENGS:
            c = {}
            for o in self.ops[e]:
                if o.signal:
                    c[o.epoch] = c.get(o.epoch, 0) + 1
                o.count = c.get(o.epoch, 0)
        with contextlib.ExitStack() as es:
            esem = {}
            for e in ENGS:
                for ep in range(self.epoch + 1):
                    if any(o.signal and o.epoch == ep for o in self.ops[e]):
                        esem[(e, ep)] = es.enter_context(nc.semaphore("s_%s_%d" % (e, ep)))
            dsem = {}
            for key in self.slot_cum:
                dsem[key] = es.enter_context(nc.semaphore("d_%s_%d" % key))
            block = es.enter_context(nc.Block())

            def run(e, eng):
                waited = {}
                for o in self.ops[e]:
                    waits = {}
                    for d in o.deps:
                        if d.dma:
                            s, v = dsem[d.slot], d.slot_val
                            k = ("d",) + d.slot
                        else:
                            if d.eng == e and (e == "pe" or not SAME_ENGINE_SYNC):
                                continue
                            s, v = esem[(d.eng, d.epoch)], d.count
                            k = ("e", d.eng, d.epoch)
                        if waits.get(k, (None, 0))[1] < v:
                            waits[k] = (s, v)
                    if o.dma and o.prev_val > 0:
                        k = ("d",) + o.slot
                        if waits.get(k, (None, 0))[1] < o.prev_val:
                            waits[k] = (dsem[o.slot], o.prev_val)
                    for k, (s, v) in waits.items():
                        if waited.get(k, 0) >= v:
                            continue
                        waited[k] = v
                        eng.wait_ge(s, v)
                    if o.emit is None:
                        continue
                    ins = o.emit(eng)
                    if o.dma:
                        ins.then_inc(dsem[o.slot], 16)
                    elif o.signal:
                        ins.then_inc(esem[(e, o.epoch)], 1)

            @block.tensor
            def _(eng):
                run("pe", eng)

            @block.scalar
            def _(eng):
                run("act", eng)

            @block.vector
            def _(eng):
                run("dve", eng)

            @block.gpsimd
            def _(eng):
                run("pool", eng)

            @block.sync
            def _(eng):
                run("sp", eng)


class Cx:
    def __init__(self, nc):
        self.nc = nc
        self.P = Prog(nc)
        self.es = None
        self.uid = 0
        self.dq = 0
        self.outs = []

    def sb(self, shape, dt, name=None):
        self.uid += 1
        t = self.es.enter_context(self.nc.sbuf_tensor("%s_%d" % (name or "t", self.uid), list(shape), dt))
        return t, Buf(name or "t")

    def ps(self, shape, dt, name=None):
        self.uid += 1
        t = self.es.enter_context(self.nc.psum_tensor("%s_%d" % (name or "p", self.uid), list(shape), dt))
        return t, Buf(name or "p")

    def dram(self, name, shape, dt, kind):
        return self.nc.dram_tensor(name, list(shape), dt, kind=kind)

    def mm(self, out, lhsT, rhs, start, stop, reads, writes):
        self.P.op("pe", lambda e: e.matmul(out, lhsT=lhsT, rhs=rhs, start=start, stop=stop),
                  reads=reads, writes=writes)

    def tr(self, out, in_, ident, reads, writes):
        self.P.op("pe", lambda e: e.transpose(out=out, in_=in_, identity=ident), reads=reads, writes=writes)

    def act(self, out, in_, func, reads, writes, bias=None, scale=None):
        kw = {}
        if bias is not None:
            kw["bias"] = bias
        if scale is not None:
            kw["scale"] = scale
        self.P.op("act", lambda e: e.activation(out=out, in_=in_, func=func, **kw), reads=reads, writes=writes)

    def ts(self, eng, out, in0, s1, s2, op0, op1, reads, writes):
        if op1 is None:
            self.P.op(eng, lambda e: e.tensor_scalar(out=out, in0=in0, scalar1=s1, scalar2=None, op0=op0),
                      reads=reads, writes=writes)
        else:
            self.P.op(eng, lambda e: e.tensor_scalar(out=out, in0=in0, scalar1=s1, scalar2=s2, op0=op0, op1=op1),
                      reads=reads, writes=writes)

    def stt(self, out, in0, scalar, in1, op0, op1, reads, writes):
        self.P.op("dve", lambda e: e.scalar_tensor_tensor(out=out, in0=in0, scalar=scalar, in1=in1,
                                                          op0=op0, op1=op1), reads=reads, writes=writes)

    def tt(self, eng, out, in0, in1, op, reads, writes):
        self.P.op(eng, lambda e: e.tensor_tensor(out=out, in0=in0, in1=in1, op=op), reads=reads, writes=writes)

    def copy(self, eng, out, in_, reads, writes):
        if eng == "act":
            self.P.op("act", lambda e: e.copy(out=out, in_=in_), reads=reads, writes=writes)
        else:
            self.P.op(eng, lambda e: e.tensor_copy(out=out, in_=in_), reads=reads, writes=writes)

    def memset(self, eng, ap, val, writes):
        self.P.op(eng, lambda e: e.memset(ap, val), writes=writes)

    def dma(self, q, out, in_, reads=(), writes=(), nonc=False):
        if nonc:
            def em(e):
                with self.nc.allow_non_contiguous_dma(reason="small strided param load"):
                    return e.dma_start(out=out, in_=in_)
        else:
            def em(e):
                return e.dma_start(out=out, in_=in_)
        return self.P.op(q, em, reads=reads, writes=writes, dma=True)


def bcast_rows(ap2d_row, n=128):
    a = ap2d_row.partition_broadcast(n)
    if len(a.shape) == 3:
        a = a[:, 0, :]
    return a


class Consts:
    pass


def load_consts(cx, din):
    c = Consts()
    c.ident, c.b_ident = cx.sb([128, 128], BF16, "ident")
    cx.dma("sp", c.ident[:], din["ident"], writes=[c.b_ident])
    c.eps, c.b_eps = cx.sb([128, 1], F32, "eps")
    cx.memset("dve", c.eps[:], EPS, [c.b_eps])
    return c


def rstd_op(cx, consts, rstd, b_rstd, var_ap, b_var):
    cx.act(rstd, var_ap, AF.Sqrt, [b_var, consts.b_eps], [b_rstd], bias=consts.eps[:, 0:1])
    cx.P.op("dve", lambda e: e.reciprocal(out=rstd, in_=rstd), reads=[b_rstd], writes=[b_rstd])


class Banks:
    def __init__(self, cx):
        self.pb = []
        self.t = []
        self.b = []
        for i in range(4):
            t, _ = cx.ps([128, 1024], F32, "pb%d" % i)
            self.pb.append(t)
            for hf in range(2):
                self.t.append(t[:, hf * 512:(hf + 1) * 512])
                self.b.append(Buf("bank%d" % (2 * i + hf)))
        self.tT = self.pb[3].bitcast(BF16)[:, 1024:2048]
        self.bT = self.b[7]

    def pair(self, i):
        return self.pb[i][:, :].rearrange("p (b c) -> p b c", b=2), [self.b[2 * i], self.b[2 * i + 1]]


class Epi:
    def __init__(self, cx, consts, banks, lng_row, lnb_row):
        self.cx = cx
        self.consts = consts
        self.banks = banks
        self.lng, self.b_lng = cx.sb([128, D], F32, "lng")
        self.lnb, self.b_lnb = cx.sb([128, D], F32, "lnb")
        cx.dma("sp", self.lng[:], bcast_rows(lng_row), writes=[self.b_lng])
        cx.dma("sp", self.lnb[:], bcast_rows(lnb_row), writes=[self.b_lnb])
        self.xr = [cx.sb([128, D], F32, "xr") for _ in range(2)]
        self.s = [cx.sb([128, D], F32, "s")] * 2
        self.xn = [cx.sb([128, D], F32, "xn")] * 2
        self.xnb = [cx.sb([128, D], BF16, "xnb") for _ in range(2)]
        self.xts = [cx.sb([128, 8, 128], BF16, "xts") for _ in range(2)]
        self.st = [cx.sb([128, 2, 6], F32, "st") for _ in range(2)]
        self.mv = [cx.sb([128, 2], F32, "mv") for _ in range(2)]
        self.rstd = [cx.sb([128, 1], F32, "rstd") for _ in range(2)]
        self.nmr = [cx.sb([128, 1], F32, "nmr") for _ in range(2)]
        self.tl, self.b_tl = cx.sb([128, 8, 16], BF16, "tl")
        self.k = 0
        self.pending = None

    def prefetch(self, tile, xres_in):
        xr, b_xr = self.xr[self.k]
        self.cx.dma("sp", xr[:], xres_in[tile * 128:(tile + 1) * 128, :], writes=[b_xr])
        self.pre = tile

    def run(self, tile, y0, y1, by, xres_in, xres_out, xT_out, tail_out, out_bufs=None, next_tile=None):
        cx = self.cx
        k = self.k
        self.k ^= 1
        self.flush()
        xr, b_xr = self.xr[k]
        s, b_s = self.s[k]
        xn, b_xn = self.xn[k]
        xnb, b_xnb = self.xnb[k]
        xts, b_xts = self.xts[k]
        st, b_st = self.st[k]
        mv, b_mv = self.mv[k]
        rstd, b_rstd = self.rstd[k]
        rows = slice(tile * 128, (tile + 1) * 128)
        if getattr(self, "pre", None) != tile:
            cx.dma("sp", xr[:], xres_in[rows, :], writes=[b_xr])
        self.pre = None
        if next_tile is not None:
            self.prefetch(next_tile, xres_in)
        cx.stt(s[:, 0:512], xr[:, 0:512], ALPHA, y0, ALU.mult, ALU.add, [b_xr, by[0]], [b_s])
        cx.stt(s[:, 512:1024], xr[:, 512:1024], ALPHA, y1, ALU.mult, ALU.add, [b_xr, by[1]], [b_s])
        cx.P.op("dve", lambda e: e.bn_stats(out=st[:, 0, :], in_=s[:, 0:512]), reads=[b_s], writes=[b_st])
        cx.P.op("dve", lambda e: e.bn_stats(out=st[:, 1, :], in_=s[:, 512:1024]), reads=[b_s], writes=[b_st])
        cx.P.op("dve", lambda e: e.bn_aggr(out=mv[:], in_=st[:].rearrange("p a b -> p (a b)")),
                reads=[b_st], writes=[b_mv])
        rstd_op(cx, self.consts, rstd[:], b_rstd, mv[:, 1:2], b_mv)
        cx.stt(s[:], s[:], mv[:, 0:1], self.lng[:], ALU.subtract, ALU.mult, [b_s, b_mv, self.b_lng], [b_s])
        cx.stt(xn[:], s[:], rstd[:, 0:1], self.lnb[:], ALU.mult, ALU.add, [b_s, b_rstd, self.b_lnb], [b_xn])
        o = cx.dma(STORE_Q, xres_out[rows, :], xn[:], reads=[b_xn], writes=out_bufs or ())
        cx.outs.append(o)
        if xT_out is None:
            return
        cx.copy("act", xnb[:], xn[:], [b_xn], [b_xnb])
        self.pending = (tile, xnb, b_xnb, xts, b_xts, xT_out, tail_out, out_bufs)

    def flush(self):
        if self.pending is None:
            return
        cx = self.cx
        tile, xnb, b_xnb, xts, b_xts, xT_out, tail_out, out_bufs = self.pending
        self.pending = None
        bk = self.banks
        for c in range(8):
            cx.tr(bk.tT[:, c * 128:(c + 1) * 128], xnb[:, c * 128:(c + 1) * 128], self.consts.ident[:],
                  [b_xnb, self.consts.b_ident], [bk.bT])
        cx.copy("act", xts[:].rearrange("p c t -> p (c t)"), bk.tT[:, :], [bk.bT], [b_xts])
        o = cx.dma(STORE_Q, xT_out.rearrange("(c p) t -> p c t", p=128)[:, :, tile * 128:(tile + 1) * 128], xts[:],
                   reads=[b_xts], writes=out_bufs or ())
        cx.outs.append(o)
        if tile % 2 == 1:
            blk = tile // 2
            cx.copy("pool", self.tl[:, :, 2 * blk:2 * blk + 2], xts[:, :, 126:128], [b_xts], [self.b_tl])
        if tile == 15 and tail_out is not None:
            o = cx.dma("sp", tail_out.rearrange("(c p) t -> p c t", p=128), self.tl[:], reads=[self.b_tl],
                       writes=out_bufs or (), nonc=True)
            cx.outs.append(o)


class Stager:
    def __init__(self, cx, n=4, size=1024):
        self.bufs = [cx.sb([128, size], F32, "stg32") for _ in range(n)]
        self.k = 0
        self.size = size

    def load(self, cx, dst, src, b_dst, shape2=None, eng="pool"):
        t, b = self.bufs[self.k % len(self.bufs)]
        self.k += 1
        if shape2 is None:
            n = dst.shape[1]
            view = t[:, 0:n]
        else:
            a, bb = shape2
            view = t[:, 0:a * bb].rearrange("p (a b) -> p a b", a=a)
        cx.dma("sp", view, src, writes=[b])
        cx.copy(eng, dst, view, [b], [b_dst])

    def dma(self, cx, src, n=None, shape2=None):
        t, b = self.bufs[self.k % len(self.bufs)]
        self.k += 1
        if shape2 is None:
            view = t[:, 0:n]
        else:
            a, bb = shape2
            view = t[:, 0:a * bb].rearrange("p (a b) -> p a b", a=a)
        cx.dma("sp", view, src, writes=[b])
        return view, b


class WBufs:
    def __init__(self, ncols, piece):
        self.piece = piece
        self.bufs = [Buf("w") for _ in range((ncols + piece - 1) // piece)]

    def get(self, c0, c1):
        return self.bufs[c0 // self.piece:(c1 - 1) // self.piece + 1]


CAST_ROT = ("act", "dve", "act", "dve", "pool")
WUP_CAST = ("act", "pool", "dve", "act", "pool", "dve", "act", "pool")


def load_w_bf16(cx, stager, dst, src, k_chunks, col0, ncols, maxc=1024, order=None):
    wb = WBufs(ncols, maxc)
    pieces = list(range(0, ncols, maxc))
    if order is not None:
        pieces = [pieces[i] for i in order]
    n = 0
    for c0 in pieces:
        c1 = min(ncols, c0 + maxc)
        for k in range(k_chunks):
            stager.load(cx, dst[:, k, c0:c1], src[k * 128:(k + 1) * 128, col0 + c0:col0 + c1], wb.get(c0, c1)[0],
                        eng=CAST_ROT[n % len(CAST_ROT)])
            n += 1
    return wb


def stage_prologue(cx, consts, banks, x_in, xT_out, tail_out, out_bufs=None):
    with contextlib.ExitStack() as es:
        cx.es = es
        xr = [cx.sb([128, D], F32, "pxr") for _ in range(2)]
        xb = [cx.sb([128, D], BF16, "pxb") for _ in range(2)]
        xts = [cx.sb([128, 8, 128], BF16, "pxts") for _ in range(2)]
        tl, b_tl = cx.sb([128, 8, 16], BF16, "ptl")
        for tile in range(16):
            k = tile % 2
            cx.dma("sp", xr[k][0][:], x_in[tile * 128:(tile + 1) * 128, :], writes=[xr[k][1]])
            cx.copy("dve", xb[k][0][:], xr[k][0][:], [xr[k][1]], [xb[k][1]])
            for c in range(8):
                cx.tr(banks.tT[:, c * 128:(c + 1) * 128], xb[k][0][:, c * 128:(c + 1) * 128], consts.ident[:],
                      [xb[k][1], consts.b_ident], [banks.bT])
            cx.copy("act", xts[k][0][:].rearrange("p c t -> p (c t)"), banks.tT[:, :], [banks.bT], [xts[k][1]])
            o = cx.dma("sp", xT_out.rearrange("(c p) t -> p c t", p=128)[:, :, tile * 128:(tile + 1) * 128],
                       xts[k][0][:], reads=[xts[k][1]], writes=out_bufs or ())
            cx.outs.append(o)
            if tile % 2 == 1:
                blk = tile // 2
                cx.copy("pool", tl[:, :, 2 * blk:2 * blk + 2], xts[k][0][:, :, 126:128], [xts[k][1]], [b_tl])
        o = cx.dma("sp", tail_out.rearrange("(c p) t -> p c t", p=128), tl[:], reads=[b_tl],
                   writes=out_bufs or (), nonc=True)
        cx.outs.append(o)
        cx.P.barrier()
    cx.es = None


def stage_gmlp(cx, consts, banks, din, j, li, passes, in_bufs=(), out_bufs=None):
    w_in = din["a_w_in"][j]
    w_out = din["a_w_out"][j]
    with contextlib.ExitStack() as es:
        cx.es = es
        xT, b_xT = cx.sb([128, 8, NT], BF16, "xT")
        for c in range(8):
            cx.dma("sp", xT[:, c, :], passes[0][1][c * 128:(c + 1) * 128, :], reads=in_bufs, writes=[b_xT])
        stager = Stager(cx)
        win, _ = cx.sb([128, 8, 2048], BF16, "win")
        wb_win = load_w_bf16(cx, stager, win, w_in, 8, 0, 2048)
        wout, _ = cx.sb([128, 8, 1024], BF16, "wout")
        wb_wout = load_w_bf16(cx, stager, wout, w_out, 8, 0, 1024)
        def spatial_setup():
            wsn, b_wsn = cx.sb([128, 8, 128], BF16, "wsn")
            cx.dma("pool", wsn[:], din["a_w_s"][j].rearrange("g t s -> t g s"), writes=[b_wsn])
            tril, b_tril = cx.sb([128, 8, 128], BF16, "tril")
            cx.dma("sp", tril[:], din["tril"], writes=[b_tril])
            cx.tt("pool", wsn[:], wsn[:], tril[:], ALU.mult, [b_wsn, b_tril], [b_wsn])
            wmT, b_wmT = cx.sb([128, 8, 128], BF16, "wmT")
            for g in range(8):
                cx.tr(banks.tT[:, g * 128:(g + 1) * 128], wsn[:, g, :], consts.ident[:], [b_wsn, consts.b_ident],
                      [banks.bT])
            cx.copy("act", wmT[:].rearrange("p g t -> p (g t)"), banks.tT[:, :], [banks.bT], [b_wmT])
            lbb, b_lbb = cx.sb([128, 1024], BF16, "lbb")
            cx.dma("pool", lbb[:], bcast_rows(din["a_ln_b"][j:j + 1, :]), writes=[b_lbb])
            bsb, b_bsb = cx.sb([128, 8, 128], F32, "bsb")
            cx.dma("sp", bsb[:].rearrange("p g t -> p (g t)"),
                   bcast_rows(din["a_b_s"][j:j + 1].rearrange("o g t -> o (g t)")), writes=[b_bsb])
            Bt, b_Bt = cx.sb([128, 8, 128], F32, "Bt")
            for c in range(8):
                bank = banks.t[c // 4]
                cx.mm(bank[:, (c % 4) * 128:(c % 4 + 1) * 128], lbb[:, c * 128:(c + 1) * 128], wmT[:, c, :],
                      True, True, [b_lbb, b_wmT], [banks.b[c // 4]])
            for hf in range(2):
                cx.tt("dve", Bt[:, hf * 4:(hf + 1) * 4, :].rearrange("p g t -> p (g t)"), banks.t[hf][:, :],
                      bsb[:, hf * 4:(hf + 1) * 4, :].rearrange("p g t -> p (g t)"), ALU.add,
                      [banks.b[hf], b_bsb], [b_Bt])
            gcol, b_gcol = cx.sb([128, 8], F32, "gcol")
            cx.dma("sp", gcol[:], din["a_ln_g"][j].rearrange("(c p) -> p c", p=128), writes=[b_gcol], nonc=True)
            return wmT, b_wmT, Bt, b_Bt, gcol, b_gcol

        epi = Epi(cx, consts, banks, din["ln_mix_g"][li:li + 1, :], din["ln_mix_b"][li:li + 1, :])
        uT, b_uT = cx.sb([128, 8, 512], F32, "uT")
        vg = [cx.sb([128, D], F32, "vg") for _ in range(2)]
        vn = [cx.sb([128, D], BF16, "vn") for _ in range(2)]
        t1 = [cx.sb([128, 8, 128], F32, "t1") for _ in range(2)]
        zT = [cx.sb([128, 8, 128], BF16, "zT") for _ in range(2)]
        st = [cx.sb([128, 2, 6], F32, "gst") for _ in range(2)]
        mv = [cx.sb([128, 2], F32, "gmv") for _ in range(2)]
        rstd = [cx.sb([128, 1], F32, "grstd") for _ in range(2)]
        nmr = [cx.sb([128, 1], F32, "gnmr") for _ in range(2)]
        sp_state = []

        def run_pass(xres_in, xT_in, xres_out, xT_out, tail_out, first):
            if not first:
                for c in range(8):
                    cx.dma("sp", xT[:, c, :], xT_in[c * 128:(c + 1) * 128, :], reads=in_bufs, writes=[b_xT])
            def u_phase(tg):
                for c in range(8):
                    bank, bb = banks.t[6], banks.b[6]
                    for k in range(8):
                        cx.mm(bank[:, :], win[:, k, c * 128:(c + 1) * 128], xT[:, k, tg * 512:(tg + 1) * 512],
                              k == 0, k == 7, wb_win.get(c * 128, (c + 1) * 128) + [b_xT], [bb])
                    cx.act(uT[:, c, :], bank[:, :], AF.Gelu_apprx_tanh, [bb], [b_uT])

            def part_a(tile):
                k2 = tile % 2
                tcols = slice(tile * 128, (tile + 1) * 128)
                for hf in range(2):
                    for k in range(8):
                        cx.mm(banks.t[2 + hf][:, :], xT[:, k, tcols], win[:, k, 1024 + hf * 512:1024 + (hf + 1) * 512],
                              k == 0, k == 7, [b_xT] + wb_win.get(1024 + hf * 512, 1024 + (hf + 1) * 512), [banks.b[2 + hf]])
                vgt, b_vg = vg[k2]
                vnt, b_vn = vn[k2]
                for hf in range(2):
                    cx.act(vgt[:, hf * 512:(hf + 1) * 512], banks.t[2 + hf][:, :], AF.Gelu_apprx_tanh,
                           [banks.b[2 + hf]], [b_vg])
                stt_, b_st = st[k2]
                mvt, b_mv = mv[k2]
                rs, b_rs = rstd[k2]
                cx.P.op("dve", lambda e, a=stt_, b=vgt: e.bn_stats(out=a[:, 0, :], in_=b[:, 0:512]),
                        reads=[b_vg], writes=[b_st])
                cx.P.op("dve", lambda e, a=stt_, b=vgt: e.bn_stats(out=a[:, 1, :], in_=b[:, 512:1024]),
                        reads=[b_vg], writes=[b_st])
                cx.P.op("dve", lambda e, a=mvt, b=stt_: e.bn_aggr(out=a[:], in_=b[:].rearrange("p a b -> p (a b)")),
                        reads=[b_st], writes=[b_mv])
                rstd_op(cx, consts, rs[:], b_rs, mvt[:, 1:2], b_mv)
                nm, b_nm = nmr[k2]
                cx.stt(nm[:], mvt[:, 0:1], -1.0, rs[:, 0:1], ALU.mult, ALU.mult, [b_mv, b_rs], [b_nm])
                cx.act(vnt[:], vgt[:], AF.Identity, [b_vg, b_rs, b_nm], [b_vn], bias=nm[:, 0:1], scale=rs[:, 0:1])

            def part_b(tile):
                k2 = tile % 2
                tt_ = tile % 4
                vnt, b_vn = vn[k2]
                for c in range(8):
                    cx.mm(banks.t[4 + c // 4][:, (c % 4) * 128:(c % 4 + 1) * 128], vnt[:, c * 128:(c + 1) * 128],
                          wmT[:, c, :], True, True, [b_vn, b_wmT], [banks.b[4 + c // 4]])
                t1t, b_t1 = t1[k2]
                zt, b_z = zT[k2]
                for c in range(8):
                    cx.stt(t1t[:, c, :], banks.t[4 + c // 4][:, (c % 4) * 128:(c % 4 + 1) * 128], gcol[:, c:c + 1],
                           Bt[:, c, :], ALU.mult, ALU.add, [banks.b[4 + c // 4], b_gcol, b_Bt], [b_t1])
                cx.tt("dve", zt[:], t1t[:], uT[:, :, tt_ * 128:(tt_ + 1) * 128], ALU.mult, [b_t1, b_uT], [b_z])

            def part_c(tile):
                k2 = tile % 2
                zt, b_z = zT[k2]
                yb = 0
                for hf in range(2):
                    for c in range(8):
                        cx.mm(banks.t[yb + hf][:, :], zt[:, c, :], wout[:, c, hf * 512:(hf + 1) * 512],
                              c == 0, c == 7, [b_z] + wb_wout.get(hf * 512, (hf + 1) * 512), [banks.b[yb + hf]])
                epi.run(tile, banks.t[yb][:, :], banks.t[yb + 1][:, :], [banks.b[yb], banks.b[yb + 1]],
                        xres_in, xres_out, xT_out, tail_out, out_bufs, next_tile=(tile + 1 if tile + 1 < 16 else None))

            u_phase(0)
            part_a(0)
            if first:
                sp_state.extend(spatial_setup())
            wmT, b_wmT, Bt, b_Bt, gcol, b_gcol = sp_state
            for tile in range(16):
                part_b(tile)
                if tile + 1 < 16:
                    if (tile + 1) % 4 == 0:
                        u_phase((tile + 1) // 4)
                    part_a(tile + 1)
                part_c(tile)
            epi.flush()

        for pi, p_ in enumerate(passes):
            run_pass(*p_, first=(pi == 0))
        epi.flush()
        cx.P.barrier()
    cx.es = None


def stage_ffn(cx, consts, banks, din, li, passes, in_bufs=(), out_bufs=None):
    tail_out = None
    w_up = din["f_w_up"][li]
    w_down = din["f_w_down"][li]
    with contextlib.ExitStack() as es:
        cx.es = es
        big, b_wd = cx.sb([128, 22 * 1024], BF16, "wd")
        wd = big[:, :].rearrange("p (f n) -> p f n", f=22)
        xte, b_xte = cx.sb([128, 8, 4, 258], BF16, "xte")
        hbuf, b_h = cx.sb([128, 22, 1024], BF16, "hbuf")
        halo, b_halo = cx.sb([128, 8, 16], BF16, "halo")
        cpar, b_cpar = cx.sb([44, 4, 128], F32, "cpar")
        for jt in range(3):
            cx.dma("sp", cpar[:, jt, :], din["f_conv_w"][li, jt].rearrange("(c p) -> c p", p=128), writes=[b_cpar])
        cx.dma("sp", cpar[:, 3, :], din["f_conv_b"][li].rearrange("(c p) -> c p", p=128), writes=[b_cpar])
        id32, b_id32 = cx.sb([44, 44], F32, "id32")
        cx.dma("sp", id32[:], din["ident32"][0:44, 0:44], writes=[b_id32])
        cwT, b_cw = cx.sb([128, 4, 44], F32, "cwT")
        b_cb = b_cw
        for jt in range(4):
            cx.tr(banks.t[0][:, jt * 44:(jt + 1) * 44], cpar[:, jt, :], id32[:], [b_cpar, b_id32], [banks.b[0]])
        cx.copy("dve", cwT[:].rearrange("p j c -> p (j c)"), banks.t[0][:, 0:176], [banks.b[0]], [b_cw])
        epi = Epi(cx, consts, banks, din["ln_ffn_g"][li:li + 1, :], din["ln_ffn_b"][li:li + 1, :])
        wup = [cx.sb([128, 8, 2, 256], BF16, "wup") for _ in range(2)]
        stager = Stager(cx)
        tmp = [[cx.sb([128, 2, 256], F32, "ct") for _ in range(2)] for _ in range(3)]
        wd_loaded = False
        nk = 0
        tail_fn = None
        nhalves = 2 * len(passes)
        for ghf in range(nhalves):
            hf = ghf % 2
            xres_in, xT_in, halo_fn, xres_out, xT_out = passes[ghf // 2]
            if hf == 0:
                halo_fn(cx, halo, b_halo)
            for k in range(8):
                cx.dma("sp", xte[:, k, :, 2:258],
                       xT_in[k * 128:(k + 1) * 128, hf * 1024:(hf + 1) * 1024].rearrange("p (b t) -> p b t", b=4),
                       reads=in_bufs, writes=[b_xte])
            cx.copy("pool", xte[:, :, :, 0:2],
                    halo[:, :, hf * 8:(hf + 1) * 8].rearrange("p k (b t) -> p k b t", b=4), [b_halo], [b_xte])
            def wup_src(fg_, k):
                return w_up[k * 128:(k + 1) * 128, :].rearrange("p (g f) -> p g f", g=2)[:, :, fg_ * 256:(fg_ + 1) * 256]

            def load_wup(fg_, hf_):
                wt_, b_w_ = wup[(hf_ * 11 + fg_) % 2]
                for k in range(8):
                    stager.load(cx, wt_[:, k, :, :], wup_src(fg_, k), b_w_, shape2=(2, 256), eng=WUP_CAST[k])
                if hf_ == 0:
                    for f in (2 * fg_, 2 * fg_ + 1):
                        stager.load(cx, wd[:, f, :], w_down[f * 128:(f + 1) * 128, :], b_wd, eng="act")

            class Pref:
                def __init__(self, fg_, hf_, with_wd):
                    self.fg_, self.hf_ = fg_, hf_
                    self.wt_, self.b_w_ = wup[(hf_ * 11 + fg_) % 2]
                    self.st = {}
                    self.wd = [2 * fg_, 2 * fg_ + 1] if with_wd else []

                def dma(self, k):
                    self.st[k] = stager.dma(cx, wup_src(self.fg_, k), shape2=(2, 256))

                def cast(self, k):
                    view, b = self.st.pop(k)
                    cx.copy(WUP_CAST[k], self.wt_[:, k, :, :], view, [b], [self.b_w_])

                def wd_dma(self, i):
                    f = self.wd[i]
                    self.st[("wd", i)] = stager.dma(cx, w_down[f * 128:(f + 1) * 128, :], n=1024)

                def wd_cast(self, i):
                    f = self.wd[i]
                    view, b = self.st.pop(("wd", i))
                    cx.copy("act", wd[:, f, :], view, [b], [b_wd])

                def step(self, g):
                    if g == -1:
                        for k in range(4):
                            self.dma(k)
                    elif g == 0:
                        for k in (0, 1, 2):
                            self.cast(k)
                        for k in (4, 5, 6):
                            self.dma(k)
                    elif g == 1:
                        for k in (3, 4, 5):
                            self.cast(k)
                        self.dma(7)
                        if self.wd:
                            self.wd_dma(0)
                    elif g == 2:
                        for k in (6, 7):
                            self.cast(k)
                        if self.wd:
                            self.wd_cast(0)
                            self.wd_dma(1)
                    elif g == 3:
                        if self.wd:
                            self.wd_cast(1)

            if ghf == 0:
                load_wup(0, 0)
            for fg in range(11):
                wt, b_w = wup[(ghf * 11 + fg) % 2]
                pref = None
                if fg + 1 < 11:
                    pref = Pref(fg + 1, ghf, ghf == 0)
                elif ghf + 1 < nhalves:
                    pref = Pref(0, ghf + 1, False)
                if pref is not None:
                    pref.step(-1)
                gi = 0
                for f2 in range(2):
                    fc = fg * 2 + f2
                    for bp in range(2):
                        kk = nk % 2
                        nk += 1
                        pg, bpg = banks.pair(2 * kk)
                        pv, bpv = banks.pair(2 * kk + 1)
                        for bl in range(2):
                            for k in range(8):
                                cx.mm(pg[:, bl, 0:258], wt[:, k, 0, f2 * 128:(f2 + 1) * 128], xte[:, k, 2 * bp + bl, :],
                                      k == 0, k == 7, [b_w, b_xte], [bpg[bl]])
                        for bl in range(2):
                            for k in range(8):
                                cx.mm(pv[:, bl, 0:258], wt[:, k, 1, f2 * 128:(f2 + 1) * 128], xte[:, k, 2 * bp + bl, :],
                                      k == 0, k == 7, [b_w, b_xte], [bpv[bl]])
                        (g0, bg0), (v0, bv0) = tmp[nk % 3]
                        cg = fc
                        cv = 22 + fc
                        cx.act(g0[:], pg[:, :, 0:256], AF.Identity, bpg + [b_cw, b_cb], [bg0],
                               bias=cwT[:, 3, cg:cg + 1], scale=cwT[:, 0, cg:cg + 1])
                        cx.act(v0[:], pv[:, :, 0:256], AF.Identity, bpv + [b_cw, b_cb], [bv0],
                               bias=cwT[:, 3, cv:cv + 1], scale=cwT[:, 0, cv:cv + 1])
                        if tail_fn is not None:
                            tail_fn()
                        cx.stt(g0[:], pg[:, :, 1:257], cwT[:, 1, cg:cg + 1], g0[:], ALU.mult, ALU.add, bpg + [b_cw, bg0], [bg0])
                        cx.stt(v0[:], pv[:, :, 1:257], cwT[:, 1, cv:cv + 1], v0[:], ALU.mult, ALU.add, bpv + [b_cw, bv0], [bv0])
                        cx.stt(g0[:], pg[:, :, 2:258], cwT[:, 2, cg:cg + 1], g0[:], ALU.mult, ALU.add, bpg + [b_cw, bg0], [bg0])
                        cx.stt(v0[:], pv[:, :, 2:258], cwT[:, 2, cv:cv + 1], v0[:], ALU.mult, ALU.add, bpv + [b_cw, bv0], [bv0])

                        def tail_fn(g0=g0, bg0=bg0, v0=v0, bv0=bv0, fc=fc, bp=bp):
                            cx.act(g0[:], g0[:], AF.Gelu_apprx_tanh, [bg0], [bg0])
                            cx.tt("pool", hbuf[:, fc, bp * 512:(bp + 1) * 512].rearrange("p (b t) -> p b t", b=2),
                                  g0[:], v0[:], ALU.mult, [bg0, bv0], [b_h])
                        if pref is not None:
                            pref.step(gi)
                        gi += 1
            tail_fn()
            tail_fn = None
            for tl_ in range(8):
                tile = hf * 8 + tl_
                yb = 2 * (tile % 2)
                for h2 in range(2):
                    for fc in range(22):
                        cx.mm(banks.t[yb + h2][:, :], hbuf[:, fc, tl_ * 128:(tl_ + 1) * 128],
                              wd[:, fc, h2 * 512:(h2 + 1) * 512], fc == 0, fc == 21, [b_h, b_wd], [banks.b[yb + h2]])
                epi.run(tile, banks.t[yb][:, :], banks.t[yb + 1][:, :], [banks.b[yb], banks.b[yb + 1]],
                        xres_in, xres_out, xT_out, tail_out, out_bufs, next_tile=(tile + 1 if tl_ + 1 < 8 else None))
        epi.flush()
        cx.P.barrier()
    cx.es = None


def stage_attproj(cx, consts, banks, din, j, passes, in_bufs=(), out_bufs=None):
    wqkv = din["b_w_qkv"][j]
    with contextlib.ExitStack() as es:
        cx.es = es
        xTs = []
        for pi, p_ in enumerate(passes):
            xT_, b_xT_ = cx.sb([128, 8, NT], BF16, "xT")
            xTs.append((xT_, b_xT_))
            if pi == 0:
                for c in range(8):
                    cx.dma("sp", xT_[:, c, :], p_[0][c * 128:(c + 1) * 128, :], reads=in_bufs, writes=[b_xT_])
        stager = Stager(cx)
        w, _ = cx.sb([128, 8, 3072], BF16, "wqkv")
        wb_w = load_w_bf16(cx, stager, w, wqkv, 8, 0, 3072, maxc=512)
        for pi, p_ in enumerate(passes):
            if pi > 0:
                for c in range(8):
                    cx.dma("sp", xTs[pi][0][:, c, :], p_[0][c * 128:(c + 1) * 128, :], reads=in_bufs,
                           writes=[xTs[pi][1]])
        stg = [cx.sb([128, 512], BF16, "stg") for _ in range(4)]
        kbss = [cx.sb([128, 8], F32, "kbs") for _ in range(2)]
        n = 0
        for pi, (xT_in, qT_out, kT_out, v_out, kbar_out) in enumerate(passes):
          xT, b_xT = xTs[pi]
          for fch in range(16):
              dst = qT_out if fch < 8 else kT_out
              r0 = (fch % 8) * 128
              for tg in range(4):
                  kk = n % 2
                  bank, bb = banks.t[kk], banks.b[kk]
                  for k in range(8):
                      cx.mm(bank[:, :], w[:, k, fch * 128:(fch + 1) * 128], xT[:, k, tg * 512:(tg + 1) * 512],
                            k == 0, k == 7, wb_w.get(fch * 128, (fch + 1) * 128) + [b_xT], [bb])
                  st_, b_st = stg[n % 4]
                  if fch >= 8:
                      kbs, b_kbs = kbss[fch % 2]
                      cx.P.op("dve", lambda e, a=kbs, b=bank, t=tg: e.tensor_reduce(
                          out=a[:, 2 * t:2 * t + 2], in_=b[:, :].rearrange("p (b t) -> p b t", b=2),
                          axis=AX.X, op=ALU.add), reads=[bb], writes=[b_kbs])
                      if tg == 3:
                          cx.ts("dve", kbs[:], kbs[:], 1.0 / 256.0, None, ALU.mult, None, [b_kbs], [b_kbs])
                          o = cx.dma("sp", kbar_out[r0:r0 + 128, :], kbs[:], reads=[b_kbs], writes=out_bufs or ())
                          cx.outs.append(o)
                  if n % 2 == 0 or fch >= 8:
                      cx.act(st_[:], bank[:, :], AF.Copy, [bb] + ([kbss[fch % 2][1]] if fch >= 8 else []), [b_st],
                             scale=(0.125 if fch < 8 else 1.0))
                  else:
                      cx.ts("dve", st_[:], bank[:, :], (0.125 if fch < 8 else 1.0), None, ALU.mult, None, [bb], [b_st])
                  o = cx.dma("sp", dst[r0:r0 + 128, tg * 512:(tg + 1) * 512], st_[:], reads=[b_st],
                             writes=out_bufs or ())
                  cx.outs.append(o)
                  n += 1
          for tile in range(16):
              for hf in range(2):
                  kk = n % 2
                  bank, bb = banks.t[kk], banks.b[kk]
                  for k in range(8):
                      cx.mm(bank[:, :], xT[:, k, tile * 128:(tile + 1) * 128],
                            w[:, k, 2048 + hf * 512:2048 + (hf + 1) * 512], k == 0, k == 7,
                            [b_xT] + wb_w.get(2048 + hf * 512, 2048 + (hf + 1) * 512), [bb])
                  st_, b_st = stg[n % 4]
                  if n % 2 == 0:
                      cx.act(st_[:], bank[:, :], AF.Copy, [bb], [b_st])
                  else:
                      cx.copy("dve", st_[:], bank[:, :], [bb], [b_st])
                  o = cx.dma("sp", v_out[tile * 128:(tile + 1) * 128, hf * 512:(hf + 1) * 512], st_[:],
                             reads=[b_st], writes=out_bufs or ())
                  cx.outs.append(o)
                  n += 1
        cx.P.barrier()
    cx.es = None


def stage_attcore(cx, consts, banks, din, j, li, xres_in, qT_in, kT_all, v_all, kbar_all, bvd, xres_out, xT_out,
                  tail_out, in_bufs=(), out_bufs=None, qsel=(0, 1)):
    QW = 128 * len(qsel)
    q0 = qsel[0] * 128
    wo_d = din["b_w_o"][j]
    with contextlib.ExitStack() as es:
        cx.es = es
        rb, b_rb = cx.sb([33, 16], F32, "rb")
        cx.dma("sp", rb[0:32, :], din["rel_bias"], writes=[b_rb])
        cx.dma("sp", rb[32:33, :], din["ones16"], writes=[b_rb])
        oh, b_oh = cx.sb([33, 2048], F32, "oh")
        cx.dma("sp", oh[:], din["oh"], writes=[b_oh])
        bvs, b_bvs = cx.sb([16, 2048], BF16, "bvs")
        for hf in range(4):
            cx.mm(banks.t[hf][0:16, :], rb[:, :], oh[:, hf * 512:(hf + 1) * 512], True, True, [b_rb, b_oh],
                  [banks.b[hf]])
            cx.copy("dve", bvs[:, hf * 512:(hf + 1) * 512], banks.t[hf][0:16, :], [banks.b[hf]], [b_bvs])
        b_bvd = Buf("bvd")
        cx.dma("sp", bvd.ap(), bvs[:], reads=[b_bvs], writes=[b_bvd])
        chm, b_chm = cx.sb([128, 16], F32, "chm")
        cx.dma("sp", chm[:], bcast_rows(din["rel_bias"][31:32, :]), writes=[b_chm])
        gmask, b_gm = cx.sb([128, 16, 16], F32, "gmask")
        oof, b_oof = cx.sb([128, 16, 16], F32, "oof")
        farm, b_farm = cx.sb([128, 16, 16], F32, "farm")
        for i in range(8):
            for q in range(2):
                cx.dma("sp", gmask[:, 2 * i + q, :], din["gmask"][:, i, :], writes=[b_gm])
                cx.dma("sp", oof[:, 2 * i + q, :], din["oof"][:, i, :], writes=[b_oof])
        cx.memset("pool", farm[:], 0.0, [b_farm])
        for i in range(2, 8):
            cx.memset("pool", farm[:, 2 * i:2 * i + 2, 0:2 * i - 2], 1.0, [b_farm])
        jm, b_jm = cx.sb([128, 128], BF16, "jm")
        cx.dma("sp", jm[:], din["jm"], writes=[b_jm])
        stager = Stager(cx)
        wo, _ = cx.sb([128, 8, 1024], BF16, "wo")
        osb, b_osb = cx.sb([128, 16, D], BF16, "osb")
        kta = [cx.sb([80, 4096], BF16, "kta") for _ in range(2)]
        va = [cx.sb([128, 32, 65], BF16, "va") for _ in range(2)]
        qta = [cx.sb([80, NT], BF16, "qta") for _ in range(2)]
        tp = [cx.sb([128, 8, 256], BF16, "tp") for _ in range(2)]
        for k in range(2):
            cx.dma("sp", kta[k][0][64:80, :], din["koh"], writes=[kta[k][1]])
            cx.memset("pool", va[k][0][:, :, 64:65], 1.0, [va[k][1]])
        kb32s = [cx.sb([64, 16], F32, "kb32") for _ in range(2)]
        kbar = [cx.sb([64, 16], BF16, "kbar") for _ in range(2)]
        mpad, b_mp = cx.sb([128, 16, 80], BF16, "mpad")
        cx.memset("pool", mpad[:], 0.0, [b_mp])
        gm, b_g = cx.sb([128, 16, 16], F32, "gm")
        top8, b_t8 = cx.sb([128, 16, 8], F32, "top8")
        keep, b_kp = cx.sb([128, 16, 16], F32, "keep")
        ebuf = [cx.sb([128, 512], BF16, "ebuf") for _ in range(4)]
        rden = [cx.sb([128, 1], F32, "rden") for _ in range(4)]
        epi = Epi(cx, consts, banks, din["ln_mix_g"][li:li + 1, :], din["ln_mix_b"][li:li + 1, :])
        g7 = banks.t[7]

        def loads(h):
            hk = h % 2
            kt_, b_kt = kta[hk]
            va_, b_va = va[hk]
            qt_, b_qt = qta[hk]
            tp_, b_tp = tp[hk]
            for r in range(2):
                cx.dma("sp", kt_[0:64, :].rearrange("d (i r t) -> d i r t", i=8, r=2)[:, :, r, :],
                       kT_all[r, h * 64:(h + 1) * 64, :].rearrange("d (i t) -> d i t", i=8),
                       reads=in_bufs, writes=[b_kt])
            cx.dma("sp", qt_[0:64, :], qT_in[h * 64:(h + 1) * 64, :], reads=in_bufs, writes=[b_qt])
            kb32_, b_kb32_ = kb32s[hk]
            for r in range(2):
                cx.dma("sp", kb32_[:, :].rearrange("d (i r) -> d i r", r=2)[:, :, r], kbar_all[r, h * 64:(h + 1) * 64, :],
                       reads=in_bufs, writes=[b_kb32_], nonc=True)
            for r in range(2):
                for s_ in range(2):
                    cx.dma("sp", va_[:, :, 0:64].rearrange("p (i r s) d -> p i r s d", i=8, r=2)[:, :, r, s_, :],
                           v_all[r, :, h * 64:(h + 1) * 64].rearrange("(i s p) d -> p i s d", i=8, s=2)[:, :, s_, :],
                           reads=in_bufs, writes=[b_va], nonc=True)
            for rel in (-2, -1, 0, 1):
                for kt in range(2):
                    m0 = (rel + 2) * 512 + 128 * (1 - kt)
                    src = bass.AP(tensor=bvd, offset=h * 2048 + m0, ap=[[1, 128], [1, 256]])
                    cx.dma("sp", tp_[:, (rel + 2) * 2 + kt, :], src, reads=[b_bvd], writes=[b_tp])

        def gate1(h):
            hk = h % 2
            kt_, b_kt = kta[hk]
            qt_, b_qt = qta[hk]
            kbt, b_kb = kbar[hk]
            kb32_, b_kb32_ = kb32s[hk]
            cx.copy("dve", kbt[:], kb32_[:], [b_kb32_], [b_kb])

        def gate1b(h):
            hk = h % 2
            qt_, b_qt = qta[hk]
            kbt, b_kb = kbar[hk]
            for qc in range(16):
                cx.mm(g7[:, qc * 16:(qc + 1) * 16], qt_[0:64, qc * 128:(qc + 1) * 128], kbt[:, :], True, True,
                      [b_qt, b_kb], [banks.b[7]])
            cx.tt("dve", gm[:], g7[:, 0:256].rearrange("p (c n) -> p c n", c=16), gmask[:], ALU.add,
                  [banks.b[7], b_gm], [b_g])
            for qc in range(16):
                cx.P.op("dve", lambda e, c=qc: e.max(out=top8[:, c, :], in_=gm[:, c, :]), reads=[b_g], writes=[b_t8])
            cx.tt("dve", keep[:], gm[:], top8[:, :, 2:3].to_broadcast([128, 16, 16]), ALU.is_ge, [b_g, b_t8], [b_kp])
            cx.tt("dve", keep[:], keep[:], oof[:], ALU.max, [b_kp, b_oof], [b_kp])
            cx.ts("dve", keep[:], keep[:], -NEG, NEG, ALU.mult, ALU.add, [b_kp], [b_kp])
            cx.stt(mpad[:, :, 64:80], farm[:], chm[:, h:h + 1], keep[:], ALU.mult, ALU.add,
                   [b_farm, b_chm, b_kp], [b_mp])

        def gate2(h):
            hk = h % 2
            qt_, b_qt = qta[hk]
            for half in range(2):
                for c in range(8):
                    qc = half * 8 + c
                    cx.tr(banks.tT[0:80, c * 128:(c + 1) * 128], mpad[:, qc, :], consts.ident[:],
                          [b_mp, consts.b_ident], [banks.bT])
                cx.copy("dve", qt_[64:80, half * 1024:(half + 1) * 1024], banks.tT[64:80, :], [banks.bT], [b_qt])

        def main(h, hooks):
            hk = h % 2
            kt_, b_kt = kta[hk]
            va_, b_va = va[hk]
            qt_, b_qt = qta[hk]
            tp_, b_tp = tp[hk]
            slots = [(i, jb) for i in range(NBLK) for jb in range(2 * i + 2)]
            L = 2
            ebs = {}
            for t in range(len(slots) + L):
                if t < len(slots):
                    i, jb = slots[t]
                    if jb == 0 and i in hooks:
                        hooks[i]()
                    rel = jb - 2 * i
                    near = rel >= -2
                    sbk, bsb_ = banks.t[t % 3], banks.b[t % 3]
                    for kt in range(2):
                        gk = 2 * jb + kt
                        cx.mm(sbk[:, kt * QW:(kt + 1) * QW], kt_[0:80, gk * 128:(gk + 1) * 128],
                              qt_[0:80, i * 256 + q0:i * 256 + q0 + QW], True, not near, [b_kt, b_qt], [bsb_])
                        if near:
                            cx.mm(sbk[:, kt * QW:(kt + 1) * QW], jm[:, :],
                                  tp_[:, (rel + 2) * 2 + kt, q0:q0 + QW], False, True, [b_jm, b_tp], [bsb_])
                    eb, b_eb = ebuf[t % 4]
                    cx.act(eb[:, 0:2 * QW], sbk[:, 0:2 * QW], AF.Exp, [bsb_], [b_eb])
                    ebs[t] = (eb, b_eb)
                if t - L >= 0:
                    i, jb = slots[t - L]
                    eb, b_eb = ebs.pop(t - L)
                    nj = 2 * i + 2
                    ob = [(banks.t[3 + 2 * (i % 2) + q], banks.b[3 + 2 * (i % 2) + q]) for q in range(2)]
                    for q in qsel:
                        for kt in range(2):
                            gk = 2 * jb + kt
                            qo = (q - qsel[0]) * 128
                            cx.mm(ob[q][0][:, 0:65], eb[:, kt * QW + qo:kt * QW + qo + 128],
                                  va_[:, gk, :], jb == 0 and kt == 0, jb == nj - 1 and kt == 1,
                                  [b_eb, b_va], [ob[q][1]])
                    if jb == nj - 1:
                        for q in qsel:
                            rd, b_rd = rden[2 * (i % 2) + q]
                            cx.P.op("dve", lambda e, a=rd, b=ob[q][0]: e.reciprocal(out=a[:], in_=b[:, 64:65]),
                                    reads=[ob[q][1]], writes=[b_rd])
                            cx.ts("dve", osb[:, 2 * i + q, h * 64:(h + 1) * 64], ob[q][0][:, 0:64], rd[:, 0:1], None,
                                  ALU.mult, None, [ob[q][1], b_rd], [b_osb])

        loads(0)
        gate1(0)
        gate1b(0)
        gate2(0)
        wb_wo = load_w_bf16(cx, stager, wo, wo_d, 8, 0, 1024)
        for h in range(H):
            hooks = {}
            if h + 1 < H:
                loads(h + 1)
                hooks[3] = (lambda hh=h + 1: (gate1(hh), gate1b(hh)))
                hooks[6] = (lambda hh=h + 1: gate2(hh))
            main(h, hooks)
        ot = [cx.sb([128, 8, 128], BF16, "ot") for _ in range(3)]
        tiles = [t for t in range(16) if t % 2 in qsel]

        def prep(n):
            tile = tiles[n]
            otl, b_ot = ot[n % 3]
            for c in range(8):
                cx.tr(banks.tT[:, c * 128:(c + 1) * 128], osb[:, tile, c * 128:(c + 1) * 128], consts.ident[:],
                      [b_osb, consts.b_ident], [banks.bT])
            cx.copy("act", otl[:].rearrange("p c t -> p (c t)"), banks.tT[:, :], [banks.bT], [b_ot])

        prep(0)
        for n, tile in enumerate(tiles):
            otl, b_ot = ot[n % 3]
            yb = 2 * (n % 2)
            for hf in range(2):
                for c in range(8):
                    cx.mm(banks.t[yb + hf][:, :], otl[:, c, :], wo[:, c, hf * 512:(hf + 1) * 512], c == 0, c == 7,
                          [b_ot] + wb_wo.get(hf * 512, (hf + 1) * 512), [banks.b[yb + hf]])
            if n + 1 < len(tiles):
                prep(n + 1)
            epi.run(tile, banks.t[yb][:, :], banks.t[yb + 1][:, :], [banks.b[yb], banks.b[yb + 1]],
                    xres_in, xres_out, xT_out, tail_out, out_bufs,
                    next_tile=(tiles[n + 1] if n + 1 < len(tiles) else None))
        epi.flush()
        cx.P.barrier()
    cx.es = None


PARAM_SHAPES = {
    "ln_mix_g": (DEPTH, D), "ln_mix_b": (DEPTH, D), "ln_ffn_g": (DEPTH, D), "ln_ffn_b": (DEPTH, D),
    "a_w_in": (2, D, 2048), "a_ln_g": (2, D), "a_ln_b": (2, D), "a_w_s": (2, 8, 128, 128),
    "a_b_s": (2, 8, 128), "a_w_out": (2, D, D), "b_w_qkv": (2, D, 3072), "b_w_o": (2, D, D),
    "rel_bias": (32, 16), "f_w_up": (DEPTH, D, 2 * FF), "f_conv_w": (DEPTH, 3, 2 * FF),
    "f_conv_b": (DEPTH, 2 * FF), "f_w_down": (DEPTH, FF, D),
}
CONST_SHAPES = {
    "ident": ((128, 128), BF16), "jm": ((128, 128), BF16), "tril": ((128, 8, 128), BF16),
    "gmask": ((128, 8, 16), F32), "oof": ((128, 8, 16), F32), "oh": ((33, 2048), F32),
    "koh": ((16, 4096), BF16), "ones16": ((1, 16), F32), "ident32": ((128, 128), F32),
}
STAGE_PARAMS = {
    "pro": ["ident"],
    "gmlp": ["ln_mix_g", "ln_mix_b", "a_w_in", "a_ln_g", "a_ln_b", "a_w_s", "a_b_s", "a_w_out", "ident", "tril"],
    "ffn": ["ln_ffn_g", "ln_ffn_b", "f_w_up", "f_conv_w", "f_conv_b", "f_w_down", "ident", "ident32"],
    "attproj": ["b_w_qkv", "ident"],
    "attcore": ["ln_mix_g", "ln_mix_b", "b_w_o", "rel_bias", "ident", "jm", "gmask", "oof", "oh", "koh", "ones16"],
}


def declare(nc, names, single_layer):
    din = {}
    for n in names:
        if n in PARAM_SHAPES:
            shp = list(PARAM_SHAPES[n])
            if single_layer and n != "rel_bias":
                shp[0] = 1
            din[n] = nc.dram_tensor(n, shp, F32, kind="ExternalInput").ap()
        else:
            shp, dt = CONST_SHAPES[n]
            din[n] = nc.dram_tensor(n, list(shp), dt, kind="ExternalInput").ap()
    return din


def rel_bucket_np(dist):
    n = np.maximum(dist, 0)
    nf = np.maximum(n, 1).astype(np.float32)
    large = 16 + (np.log(nf / np.float32(16)) / np.float32(math.log(8)) * np.float32(16)).astype(np.int32)
    large = np.minimum(large, 31)
    return np.where(n < 16, n, large)


def host_consts(hA, hX):
    c = {}
    c["ident"] = np.eye(128, dtype=np.float32).astype(NPBF)
    c["jm"] = np.eye(128, dtype=np.float32)[::-1].copy().astype(NPBF)
    tr = np.tril(np.ones((128, 128), np.float32))
    c["tril"] = np.ascontiguousarray(np.broadcast_to(tr[:, None, :], (128, 8, 128))).astype(NPBF)
    hs = (hA, 1 - hA)
    gslot = np.array([2 * (s_ // 2) + hs[s_ % 2] for s_ in range(16)])
    gm = np.zeros((8, 16), np.float32)
    oo = np.zeros((8, 16), np.float32)
    for i in range(8):
        G = 2 * i + hX
        gm[i, gslot >= G] = -1e30
        oo[i, gslot >= G] = 1.0
    c["gmask"] = np.ascontiguousarray(np.broadcast_to(gm[None], (128, 8, 16)))
    c["oof"] = np.ascontiguousarray(np.broadcast_to(oo[None], (128, 8, 16)))
    oh = np.zeros((33, 2048), np.float32)
    m = np.arange(512)
    for ri, rel in enumerate((-2, -1, 0, 1)):
        sig = rel % 2
        bd = (2 if rel < 0 else 0) + hX - hs[sig]
        dist = m - 255 + 256 * bd
        bk = rel_bucket_np(dist)
        ok = dist >= 0
        oh[bk[ok], ri * 512 + m[ok]] = 1.0
        oh[32, ri * 512 + m[~ok]] = NEG
    c["oh"] = oh
    koh = np.zeros((16, 4096), np.float32)
    for n in range(16):
        koh[n, n * 256:(n + 1) * 256] = 1.0
    c["koh"] = koh.astype(NPBF)
    c["ones16"] = np.ones((1, 16), np.float32)
    c["ident32"] = np.eye(128, dtype=np.float32)
    fl = np.zeros((128, 2), np.float32)
    fl[:, hX] = 1.0
    c["hflag"] = fl
    return c


_PROGS = {}


def build_unfused(kind):
    if kind in _PROGS:
        return _PROGS[kind]
    nc = bass.Bass("TRN2", target_bir_lowering=False)
    din = declare(nc, STAGE_PARAMS[kind], True)

    def ext_in(name, shape, dt):
        return nc.dram_tensor(name, list(shape), dt, kind="ExternalInput").ap()

    def ext_out(name, shape, dt):
        return nc.dram_tensor(name, list(shape), dt, kind="ExternalOutput").ap()

    cx = Cx(nc)
    with contextlib.ExitStack() as top:
        cx.es = top
        consts = load_consts(cx, din)
        banks = Banks(cx)
        if kind == "pro":
            x_in = ext_in("xres_in", [NT, D], F32)
            stage_prologue(cx, consts, banks, x_in, ext_out("xT_out", [D, NT], BF16),
                           ext_out("tail_out", [D, 16], BF16))
        elif kind == "gmlp":
            stage_gmlp(cx, consts, banks, din, 0, 0, [(ext_in("xres_in", [NT, D], F32),
                       ext_in("xT_in", [D, NT], BF16), ext_out("xres_out", [NT, D], F32),
                       ext_out("xT_out", [D, NT], BF16), ext_out("tail_out", [D, 16], BF16))])
        elif kind == "ffn":
            halo_in = ext_in("halo_in", [D, 16], BF16)

            def halo_fn(cx_, halo, b_halo):
                cx_.dma("sp", halo[:], halo_in.rearrange("(k p) t -> p k t", p=128), writes=[b_halo], nonc=True)

            ext_out("tail_out", [D, 16], BF16)
            stage_ffn(cx, consts, banks, din, 0, [(ext_in("xres_in", [NT, D], F32), ext_in("xT_in", [D, NT], BF16),
                      halo_fn, ext_out("xres_out", [NT, D], F32), ext_out("xT_out", [D, NT], BF16))])
        elif kind == "attproj":
            stage_attproj(cx, consts, banks, din, 0, [(ext_in("xT_in", [D, NT], BF16),
                          ext_out("qT_out", [D, NT], BF16), ext_out("kT_out", [D, NT], BF16),
                          ext_out("v_out", [NT, D], BF16), ext_out("kbar_out", [D, 8], F32))])
        elif kind == "attcore":
            bvd = nc.dram_tensor("bvd", [16, 2048], BF16, kind="Internal")
            stage_attcore(cx, consts, banks, din, 0, 0, ext_in("xres_in", [NT, D], F32),
                          ext_in("qT_in", [D, NT], BF16), ext_in("kT_all", [2, D, NT], BF16),
                          ext_in("v_all", [2, NT, D], BF16), ext_in("kbar_all", [2, D, 8], F32), bvd,
                          ext_out("xres_out", [NT, D], F32),
                          ext_out("xT_out", [D, NT], BF16), ext_out("tail_out", [D, 16], BF16))
        cx.P.finish(cx.outs)
        cx.P.emit_all()
    _PROGS[kind] = nc
    return nc


def run_stage(kind, in_maps, cores):
    nc = build_unfused(kind)
    res = run_bass_kernel_spmd(nc, in_maps, core_ids=list(range(len(cores))))
    return res.results


LAYER_PARAM_IDX = {
    "gmlp": lambda li: {"ln_mix_g": li, "ln_mix_b": li, "a_w_in": li // 2, "a_ln_g": li // 2, "a_ln_b": li // 2,
                        "a_w_s": li // 2, "a_b_s": li // 2, "a_w_out": li // 2},
    "ffn": lambda li: {"ln_ffn_g": li, "ln_ffn_b": li, "f_w_up": li, "f_conv_w": li, "f_conv_b": li,
                       "f_w_down": li},
    "attproj": lambda li: {"b_w_qkv": li // 2},
    "attcore": lambda li: {"ln_mix_g": li, "ln_mix_b": li, "b_w_o": li // 2},
}


def stage_inputs(kind, li, params, consts_c):
    m = {}
    idx = LAYER_PARAM_IDX.get(kind, lambda li: {})(li)
    for n in STAGE_PARAMS[kind]:
        if n in idx:
            m[n] = np.ascontiguousarray(params[n][idx[n]:idx[n] + 1])
        elif n == "rel_bias":
            m[n] = params[n]
        else:
            m[n] = consts_c[n]
    return m


def to_local(x):
    B = x.shape[0]
    xb = x.reshape(B, 16, 256, D)
    return [np.ascontiguousarray(xb[c // 2, (c % 2)::2].reshape(NT, D)) for c in range(2 * B)]


def from_local(outs, B):
    y = np.zeros((B, 16, 256, D), np.float32)
    for c in range(2 * B):
        y[c // 2, (c % 2)::2] = outs[c].reshape(8, 256, D)
    return y.reshape(B, 4096, D)


def make_halo(tails, c):
    half = c % 2
    pt = tails[c ^ 1]
    halo = np.zeros((D, 16), NPBF)
    if half == 0:
        halo[:, 2:16] = pt[:, 0:14]
    else:
        halo[:, :] = pt
    return halo


def forward_unfused(x, params, ncores=8, nlayers=DEPTH, debug=None):
    cores = list(range(ncores))
    cc = [host_consts(c % 2, c % 2) for c in cores]
    xl = to_local(np.asarray(x, np.float32)[: ncores // 2])
    r = run_stage("pro", [dict(xres_in=xl[c], **stage_inputs("pro", 0, params, cc[c])) for c in cores], cores)
    xres = xl
    xT = [r[c]["xT_out"] for c in cores]
    tails = [r[c]["tail_out"] for c in cores]
    for li in range(nlayers):
        if li % 2 == 0:
            r = run_stage("gmlp", [dict(xres_in=xres[c], xT_in=xT[c], **stage_inputs("gmlp", li, params, cc[c]))
                                   for c in cores], cores)
        else:
            r = run_stage("attproj", [dict(xT_in=xT[c], **stage_inputs("attproj", li, params, cc[c]))
                                      for c in cores], cores)
            ims = []
            for c in cores:
                p0, p1 = c, c ^ 1
                ims.append(dict(xres_in=xres[c], qT_in=r[c]["qT_out"],
                                kT_all=np.stack([r[p0]["kT_out"], r[p1]["kT_out"]]),
                                v_all=np.stack([r[p0]["v_out"], r[p1]["v_out"]]),
                                kbar_all=np.stack([r[p0]["kbar_out"], r[p1]["kbar_out"]]),
                                **stage_inputs("attcore", li, params, cc[c])))
            r = run_stage("attcore", ims, cores)
        xres = [r[c]["xres_out"] for c in cores]
        xT = [r[c]["xT_out"] for c in cores]
        tails = [r[c]["tail_out"] for c in cores]
        if debug is not None:
            debug.append(("mix%d" % li, from_local(xres, ncores // 2)))
        r = run_stage("ffn", [dict(xres_in=xres[c], xT_in=xT[c], halo_in=make_halo(tails, c),
                                   **stage_inputs("ffn", li, params, cc[c])) for c in cores], cores)
        xres = [r[c]["xres_out"] for c in cores]
        xT = [r[c]["xT_out"] for c in cores]
        tails = [r[c]["tail_out"] for c in cores]
        if debug is not None:
            debug.append(("ffn%d" % li, from_local(xres, ncores // 2)))
    return from_local(xres, ncores // 2)


PASS_TABLES = ["gmask", "oof", "oh", "hflag"]
CONST_SHAPES["hflag"] = ((128, 2), F32)
COMMON_CONSTS = ["ident", "jm", "tril", "koh", "ones16", "ident32"]


def build_fused(nlayers=DEPTH):
    key = ("fused", nlayers)
    if key in _PROGS:
        return _PROGS[key]
    nc = bass.Bass("TRN2", target_bir_lowering=False)
    din = declare(nc, list(PARAM_SHAPES.keys()) + COMMON_CONSTS, False)
    dinx = {}
    for X in "AB":
        d = dict(din)
        for n in PASS_TABLES:
            shp, dt = CONST_SHAPES[n]
            d[n] = nc.dram_tensor("%s_%s" % (n, X), list(shp), dt, kind="ExternalInput").ap()
        dinx[X] = d

    def internal(name, shape, dt):
        return nc.dram_tensor(name, list(shape), dt, kind="Internal")

    x_in = {X: nc.dram_tensor("x_%s" % X, [NT, D], F32, kind="ExternalInput").ap() for X in "AB"}
    out = nc.dram_tensor("out", [NT, D], F32, kind="ExternalOutput").ap()
    xres = {X: [internal("xres_%s%d" % (X, k), [NT, D], F32).ap() for k in range(2)] for X in "AB"}
    xTb = {X: [internal("xT_%s%d" % (X, k), [D, NT], BF16).ap() for k in range(2)] for X in "AB"}
    tail = {X: internal("tail_%s" % X, [D, 16], BF16).ap() for X in "AB"}
    qT = {X: internal("qT_%s" % X, [D, NT], BF16).ap() for X in "AB"}
    kT_all = internal("kT_all", [2, D, NT], BF16).ap()
    v_all = internal("v_all", [2, NT, D], BF16).ap()
    kbar_all = internal("kbar_all", [2, D, 8], F32).ap()
    bvd = {X: internal("bvd_%s" % X, [16, 2048], BF16) for X in "AB"}
    sig = {"A": 0, "B": 1}
    other = {"A": "B", "B": "A"}
    cx = Cx(nc)
    with contextlib.ExitStack() as top:
        cx.es = top
        consts = load_consts(cx, din)
        banks = Banks(cx)
        cur = {}
        for X in "AB":
            stage_prologue(cx, consts, banks, x_in[X], xTb[X][0], tail[X])
            cur[X] = dict(xres=x_in[X], xT=xTb[X][0], k=0)

        def nxt(X, final=False):
            st = cur[X]
            k = st["k"]
            st["k"] ^= 1
            if final:
                return out, None
            return xres[X][k], xTb[X][k ^ 1]

        def halo_fn_for(X):
            def halo_fn(cx_, halo, b_halo):
                to, b_to = cx_.sb([128, 8, 16], BF16, "to")
                fl, b_fl = cx_.sb([128, 2], F32, "hfl")
                tmp, b_tmp = cx_.sb([128, 8, 16], F32, "htmp")
                cx_.dma("sp", to[:], tail[other[X]].rearrange("(k p) t -> p k t", p=128), writes=[b_to], nonc=True)
                cx_.dma("sp", fl[:], dinx[X]["hflag"], writes=[b_fl])
                cx_.ts("dve", tmp[:], to[:], fl[:, 1:2], None, ALU.mult, None, [b_to, b_fl], [b_tmp])
                cx_.copy("dve", halo[:, :, 0:2], tmp[:, :, 0:2], [b_tmp], [b_halo])
                cx_.stt(halo[:, :, 2:16], to[:, :, 0:14], fl[:, 0:1], tmp[:, :, 2:16], ALU.mult, ALU.add,
                        [b_to, b_fl, b_tmp], [b_halo])
            return halo_fn

        for li in range(nlayers):
            lastl = li == DEPTH - 1
            j = li // 2
            passes = "AB"
            if li % 2 == 0:
                pl = []
                for X in passes:
                    xo, xTo = nxt(X)
                    pl.append((cur[X]["xres"], cur[X]["xT"], xo, xTo, tail[X]))
                    cur[X]["xres"], cur[X]["xT"] = xo, xTo
                stage_gmlp(cx, consts, banks, din, j, li, pl)
            else:
                stage_attproj(cx, consts, banks, din, j,
                              [(cur[X]["xT"], qT[X], kT_all[sig[X]], v_all[sig[X]], kbar_all[sig[X]]) for X in passes])
                for X in "AB":
                    xo, xTo = nxt(X)
                    stage_attcore(cx, consts, banks, dinx[X], j, li, cur[X]["xres"], qT[X], kT_all, v_all, kbar_all,
                                  bvd[X], xo, xTo, tail[X], qsel=((1,) if (lastl and X == "B") else (0, 1)))
                    cur[X]["xres"], cur[X]["xT"] = xo, xTo
            pl = []
            cx.outs = []
            for X in ("A" if lastl else "AB"):
                xo, xTo = nxt(X, final=lastl)
                pl.append((cur[X]["xres"], cur[X]["xT"], halo_fn_for(X), xo, xTo))
                cur[X]["xres"], cur[X]["xT"] = xo, xTo
            stage_ffn(cx, consts, banks, din, li, pl)
        if nlayers < DEPTH:
            cx.es = top
            t, bt = cx.sb([128, D], F32, "dbg")
            cx.outs = []
            for tile in range(16):
                cx.dma("sp", t[:], cur["A"]["xres"][tile * 128:(tile + 1) * 128, :], writes=[bt])
                cx.outs.append(cx.dma("sp", out[tile * 128:(tile + 1) * 128, :], t[:], reads=[bt]))
        cx.P.finish(cx.outs)
        cx.P.emit_all()
    _PROGS[key] = nc
    return nc


def forward_fused(x, params, ncores=8, nlayers=DEPTH):
    nc = build_fused(nlayers)
    xl = to_local(np.asarray(x, np.float32)[: ncores // 2])
    in_maps = []
    for c in range(ncores):
        hA = c % 2
        m = {k: params[k] for k in PARAM_SHAPES}
        ca = host_consts(hA, hA)
        cb = host_consts(hA, 1 - hA)
        for n in COMMON_CONSTS:
            m[n] = ca[n]
        for n in PASS_TABLES:
            m[n + "_A"] = ca[n]
            m[n + "_B"] = cb[n]
        m["x_A"] = xl[c]
        m["x_B"] = xl[c ^ 1]
        in_maps.append(m)
    res = run_bass_kernel_spmd(nc, in_maps, core_ids=list(range(ncores)))
    return from_local([res.results[c]["out"] for c in range(ncores)], ncores // 2)


def kernel(**inputs):
    params = {k: np.ascontiguousarray(np.asarray(v, np.float32)) for k, v in inputs.items() if k != "x"}
    x = np.asarray(inputs["x"], np.float32)
    return forward_fused(x, params).astype(np.float32)
```
